# Optimizing a Trainium2 kernel written in Bass

```python
import math
import jax, jax.numpy as jnp
from jax import lax
import numpy as np

D_MODEL = 1024
BATCH = 8
SEQ = 2048
DEPTH = 1

GRID_W = 64
CTX_LEN = 256
EPS = 1e-6

NA_HEADS = 8
HEAD_DIM = 64
NA_WIDTH = NA_HEADS * HEAD_DIM
NA_ROWS = 8
NA_COLS = 16
ROPE_THETA = 10000.0

SSD_HEADS = 8
SSD_HEAD_DIM = 64
SSD_WIDTH = SSD_HEADS * SSD_HEAD_DIM
SSD_GROUPS = 2
SSD_STATE = 128
SSD_CONV = 3
SSD_CHUNK = 128
GN = SSD_GROUPS * SSD_STATE
XBC_WIDTH = SSD_WIDTH + 2 * GN

MIX_WIDTH = NA_WIDTH + SSD_WIDTH
IN_WIDTH = 3 * NA_WIDTH + SSD_WIDTH + XBC_WIDTH + 2 * SSD_HEADS

D_FF = 2816
FFN_CONV = 3

kernel_name = "hymba_na_ssd_convglu_prefix_dit"


def rms_norm(x, w):
    xf = x.astype(jnp.float32)
    y = xf * lax.rsqrt(jnp.mean(xf * xf, axis=-1, keepdims=True) + EPS)
    return (y * w.astype(jnp.float32)).astype(x.dtype)


def modulate(h, shift, scale):
    return h * (1.0 + scale) + shift


def flip(t):
    return jnp.flip(t, axis=1)


def depthwise_conv(x, w, b):
    k, ch = w.shape
    y = lax.conv_general_dilated(
        x, w[:, None, :].astype(x.dtype), window_strides=(1,),
        padding=[(k // 2, k // 2)], dimension_numbers=("NWC", "WIO", "NWC"),
        feature_group_count=ch)
    return y + b


def axial_rope(x, row_pos, col_pos):
    n_freq = x.shape[-1] // 4
    freqs = ROPE_THETA ** (-jnp.arange(n_freq, dtype=jnp.float32) / n_freq)
    ang = jnp.concatenate([row_pos[:, None] * freqs, col_pos[:, None] * freqs], axis=-1)
    cos = jnp.cos(ang)[None, :, None, :]
    sin = jnp.sin(ang)[None, :, None, :]
    x1, x2 = jnp.split(x.astype(jnp.float32), 2, axis=-1)
    return jnp.concatenate([x1 * cos - x2 * sin, x1 * sin + x2 * cos], axis=-1).astype(x.dtype)


def neighbourhood_attention(q, k, v, k_ctx, v_ctx, rpb, rows):
    bsz, s, nh, dh = q.shape
    kh = min(NA_ROWS, rows)
    scale = dh ** -0.5
    qg = q.reshape(bsz, rows, GRID_W, nh, dh)
    kg = k.reshape(bsz, rows, GRID_W, nh, dh)
    vg = v.reshape(bsz, rows, GRID_W, nh, dh)
    col = np.arange(GRID_W)
    col_start = np.clip(col - NA_COLS // 2, 0, GRID_W - NA_COLS)
    col_idx = col_start[:, None] + np.arange(NA_COLS)
    dc = col_idx - col[:, None] + (NA_COLS - 1)
    n_loc = kh * NA_COLS

    def row_block(r):
        rs = jnp.clip(r - kh // 2, 0, rows - kh)
        k_win = lax.dynamic_slice_in_dim(kg, rs, kh, axis=1)[:, :, col_idx]
        v_win = lax.dynamic_slice_in_dim(vg, rs, kh, axis=1)[:, :, col_idx]
        q_r = lax.dynamic_index_in_dim(qg, r, axis=1, keepdims=False)
        dr = rs + jnp.arange(kh) - r + (NA_ROWS - 1)
        bias = rpb[:, dr[None, :, None], dc[:, None, :]]
        s_loc = jnp.einsum("bwhd,brwkhd->bhwrk", q_r, k_win) * scale + bias
        s_ctx = jnp.einsum("bwhd,bchd->bhwc", q_r, k_ctx) * scale
        scores = jnp.concatenate([s_loc.reshape(bsz, nh, GRID_W, n_loc), s_ctx], axis=-1)
        p = jax.nn.softmax(scores.astype(jnp.float32), axis=-1).astype(v.dtype)
        p_loc = p[..., :n_loc].reshape(bsz, nh, GRID_W, kh, NA_COLS)
        return (jnp.einsum("bhwrk,brwkhd->bwhd", p_loc, v_win)
                + jnp.einsum("bhwc,bchd->bwhd", p[..., n_loc:], v_ctx))

    out = lax.map(row_block, jnp.arange(rows))
    return out.transpose(1, 0, 2, 3, 4).reshape(bsz, s, nh * dh)


def context_attention(q, k, v):
    bsz, lc, nh, dh = q.shape
    s = jnp.einsum("bqhd,bkhd->bhqk", q, k) * dh ** -0.5
    p = jax.nn.softmax(s.astype(jnp.float32), axis=-1).astype(v.dtype)
    return jnp.einsum("bhqk,bkhd->bqhd", p, v).reshape(bsz, lc, nh * dh)


def ssd_prep(p_xbc_dt, conv_w, conv_b, dt_bias):
    bsz, l, _ = p_xbc_dt.shape
    xbc = jax.nn.silu(depthwise_conv(p_xbc_dt[..., :XBC_WIDTH], conv_w, conv_b))
    xs = xbc[..., :SSD_WIDTH].reshape(bsz, l, SSD_HEADS, SSD_HEAD_DIM)
    bm = xbc[..., SSD_WIDTH:SSD_WIDTH + GN].reshape(bsz, l, SSD_GROUPS, SSD_STATE)
    cm = xbc[..., SSD_WIDTH + GN:].reshape(bsz, l, SSD_GROUPS, SSD_STATE)
    dt_raw = p_xbc_dt[..., XBC_WIDTH:].reshape(bsz, l, 2, SSD_HEADS).astype(jnp.float32)
    dt = jax.nn.softplus(dt_raw + dt_bias.astype(jnp.float32))
    return xs, bm, cm, dt


def ssd_chunked(xs, dt, a, bm, cm, h0):
    bsz, l, nh, hp = xs.shape
    g, n = bm.shape[2], bm.shape[3]
    e = nh // g
    q = SSD_CHUNK
    nc = l // q
    xc = xs.reshape(bsz, nc, q, g, e, hp)
    bc = bm.reshape(bsz, nc, q, g, n)
    cc = cm.reshape(bsz, nc, q, g, n)
    dtc = dt.reshape(bsz, nc, q, g, e)
    cum = jnp.cumsum(dtc * a.reshape(g, e), axis=2)
    seg = cum[:, :, :, None] - cum[:, :, None, :]
    lower = np.tril(np.ones((q, q), dtype=bool))[:, :, None, None]
    decay = jnp.exp(jnp.where(lower, seg, -jnp.inf))
    cb = jnp.einsum("bcign,bcjgn->bcijg", cc, bc)
    m = cb[..., None] * decay * dtc[:, :, None]
    y_diag = jnp.einsum("bcijge,bcjgep->bcigep", m, xc)
    w_end = jnp.exp(cum[:, :, -1:] - cum) * dtc
    states = jnp.einsum("bcjgn,bcjge,bcjgep->bcgepn", bc, w_end, xc)
    chunk_decay = jnp.exp(cum[:, :, -1])

    def step(h, inp):
        dec, st = inp
        return h * dec[..., None, None] + st, h

    h_final, h_in = lax.scan(step, h0.reshape(bsz, g, e, hp, n).astype(states.dtype),
                             (jnp.moveaxis(chunk_decay, 1, 0), jnp.moveaxis(states, 1, 0)))
    h_in = jnp.moveaxis(h_in, 0, 1)
    y_off = jnp.einsum("bcign,bcgepn,bcige->bcigep", cc, h_in, jnp.exp(cum))
    return (y_diag + y_off).reshape(bsz, l, nh, hp), h_final.reshape(bsz, nh, hp, n)


def ssd_final_state(xs, dt, a, bm):
    bsz, l, nh, hp = xs.shape
    g, n = bm.shape[2], bm.shape[3]
    e = nh // g
    cum = jnp.cumsum(dt * a, axis=1)
    w = (jnp.exp(cum[:, -1:] - cum) * dt).reshape(bsz, l, g, e)
    h = jnp.einsum("blgn,blge,blgep->bgepn", bm, w, xs.reshape(bsz, l, g, e, hp))
    return h.reshape(bsz, nh, hp, n)


def ssd_bidirectional(z, xs, bm, cm, dt, a, d_skip, norm_w, h0_fwd, h0_bwd):
    y_f, h_f = ssd_chunked(xs, dt[..., 0, :], a[0], bm, cm, h0_fwd)
    y_b, h_b = ssd_chunked(flip(xs), flip(dt[..., 1, :]), a[1], flip(bm), flip(cm), h0_bwd)
    y = y_f + flip(y_b) + d_skip[:, None] * xs
    bsz, l = xs.shape[:2]
    y = y.reshape(bsz, l, SSD_WIDTH) * jax.nn.silu(z)
    return rms_norm(y, norm_w), h_f, h_b


def conv_glu(h, w_up, conv_w, conv_b, w_down):
    gate, val = jnp.split(h @ w_up, 2, axis=-1)
    gate = depthwise_conv(gate, conv_w, conv_b)
    return (jax.nn.silu(gate) * val) @ w_down


def hybrid_layer(x, ctx, c, c_ctx, ada_w, ada_b, norm1_w, w_in, rpb, ssd_conv_w, ssd_conv_b,
                 dt_bias, a_log, ssd_d, ssd_norm_w, w_out, norm2_w, ffn_w_up, ffn_conv_w,
                 ffn_conv_b, ffn_w_down, rows, row_pos, col_pos, update_ctx):
    bsz, s, _ = x.shape
    lc = ctx.shape[1]
    mod = (jax.nn.silu(c) @ ada_w + ada_b)[:, None, :]
    mod_c = (jax.nn.silu(c_ctx) @ ada_w + ada_b)[None, None, :]
    sh1, sc1, g1, sh2, sc2, g2 = jnp.split(mod, 6, axis=-1)
    sh1c, sc1c, g1c, sh2c, sc2c, g2c = jnp.split(mod_c, 6, axis=-1)
    a = -jnp.exp(a_log.astype(jnp.float32))

    h = modulate(rms_norm(x, norm1_w), sh1, sc1)
    hc = modulate(rms_norm(ctx, norm1_w), sh1c, sc1c)
    p = h @ w_in
    col_ssd = 3 * NA_WIDTH + SSD_WIDTH
    if update_ctx:
        pc = hc @ w_in
        qc = pc[..., :NA_WIDTH].reshape(bsz, lc, NA_HEADS, HEAD_DIM)
        kvc = pc[..., NA_WIDTH:3 * NA_WIDTH]
        zc = pc[..., 3 * NA_WIDTH:col_ssd]
        sc_in = pc[..., col_ssd:]
    else:
        kvc = hc @ w_in[:, NA_WIDTH:3 * NA_WIDTH]
        sc_in = hc @ w_in[:, col_ssd:]
    kc = kvc[..., :NA_WIDTH].reshape(bsz, lc, NA_HEADS, HEAD_DIM)
    vc = kvc[..., NA_WIDTH:].reshape(bsz, lc, NA_HEADS, HEAD_DIM)

    q = axial_rope(p[..., :NA_WIDTH].reshape(bsz, s, NA_HEADS, HEAD_DIM), row_pos, col_pos)
    k = axial_rope(p[..., NA_WIDTH:2 * NA_WIDTH].reshape(bsz, s, NA_HEADS, HEAD_DIM), row_pos, col_pos)
    v = p[..., 2 * NA_WIDTH:3 * NA_WIDTH].reshape(bsz, s, NA_HEADS, HEAD_DIM)
    attn = neighbourhood_attention(q, k, v, kc, vc, rpb, rows)

    xs_c, bm_c, cm_c, dt_c = ssd_prep(sc_in, ssd_conv_w, ssd_conv_b, dt_bias)
    if update_ctx:
        zero = jnp.zeros((bsz, SSD_HEADS, SSD_HEAD_DIM, SSD_STATE), jnp.float32)
        y_c, h_fwd, h_bwd = ssd_bidirectional(zc, xs_c, bm_c, cm_c, dt_c, a, ssd_d, ssd_norm_w, zero, zero)
    else:
        h_fwd = ssd_final_state(xs_c, dt_c[..., 0, :], a[0], bm_c)
        h_bwd = ssd_final_state(flip(xs_c), flip(dt_c[..., 1, :]), a[1], flip(bm_c))
    xs, bm, cm, dt = ssd_prep(p[..., col_ssd:], ssd_conv_w, ssd_conv_b, dt_bias)
    y_ssd, _, _ = ssd_bidirectional(p[..., 3 * NA_WIDTH:col_ssd], xs, bm, cm, dt, a, ssd_d,
                                    ssd_norm_w, h_fwd, h_bwd)

    x = x + g1 * (jnp.concatenate([attn, y_ssd], axis=-1) @ w_out)
    x = x + g2 * conv_glu(modulate(rms_norm(x, norm2_w), sh2, sc2), ffn_w_up, ffn_conv_w, ffn_conv_b, ffn_w_down)

    if update_ctx:
        attn_c = context_attention(qc, kc, vc)
        ctx = ctx + g1c * (jnp.concatenate([attn_c, y_c], axis=-1) @ w_out)
        ctx = ctx + g2c * conv_glu(modulate(rms_norm(ctx, norm2_w), sh2c, sc2c), ffn_w_up, ffn_conv_w, ffn_conv_b, ffn_w_down)
    return x, ctx


def setup_inputs(seed: int = 0) -> dict:
    key = jax.random.key(seed)
    ks = jax.random.split(key, 24)
    f32 = jnp.float32

    def nrm(k, shape, s):
        return s * jax.random.normal(k, shape, f32)

    dt0 = jnp.exp(jax.random.uniform(ks[11], (DEPTH, 2, SSD_HEADS), f32, math.log(1e-3), math.log(1e-1)))
    return {
        "x": nrm(ks[0], (BATCH, SEQ, D_MODEL), 1.0),
        "c": nrm(ks[1], (BATCH, D_MODEL), 1.0),
        "ctx": nrm(ks[2], (BATCH, CTX_LEN, D_MODEL), 1.0),
        "c_ctx": nrm(ks[3], (D_MODEL,), 1.0),
        "ada_w": nrm(ks[4], (DEPTH, D_MODEL, 6 * D_MODEL), 0.5 * D_MODEL ** -0.5),
        "ada_b": nrm(ks[5], (DEPTH, 6 * D_MODEL), 0.02),
        "norm1_w": 1.0 + nrm(ks[6], (DEPTH, D_MODEL), 0.1),
        "w_in": nrm(ks[7], (DEPTH, D_MODEL, IN_WIDTH), D_MODEL ** -0.5),
        "rpb": nrm(ks[8], (DEPTH, NA_HEADS, 2 * NA_ROWS - 1, 2 * NA_COLS - 1), 0.5),
        "ssd_conv_w": nrm(ks[9], (DEPTH, SSD_CONV, XBC_WIDTH), SSD_CONV ** -0.5),
        "ssd_conv_b": nrm(ks[10], (DEPTH, XBC_WIDTH), 0.01),
        "dt_bias": dt0 + jnp.log(-jnp.expm1(-dt0)),
        "a_log": jnp.log(jax.random.uniform(ks[12], (DEPTH, 2, SSD_HEADS), f32, 1.0, 16.0)),
        "ssd_d": 1.0 + nrm(ks[13], (DEPTH, SSD_HEADS), 0.1),
        "ssd_norm_w": 1.0 + nrm(ks[14], (DEPTH, SSD_WIDTH), 0.1),
        "w_out": nrm(ks[15], (DEPTH, MIX_WIDTH, D_MODEL), MIX_WIDTH ** -0.5),
        "norm2_w": 1.0 + nrm(ks[16], (DEPTH, D_MODEL), 0.1),
        "ffn_w_up": nrm(ks[17], (DEPTH, D_MODEL, 2 * D_FF), D_MODEL ** -0.5),
        "ffn_conv_w": nrm(ks[18], (DEPTH, FFN_CONV, D_FF), FFN_CONV ** -0.5),
        "ffn_conv_b": nrm(ks[19], (DEPTH, D_FF), 0.01),
        "ffn_w_down": nrm(ks[20], (DEPTH, D_FF, D_MODEL), D_FF ** -0.5),
        "final_norm_w": 1.0 + nrm(ks[21], (D_MODEL,), 0.1),
    }


def reference(x, c, ctx, c_ctx, ada_w, ada_b, norm1_w, w_in, rpb, ssd_conv_w, ssd_conv_b,
              dt_bias, a_log, ssd_d, ssd_norm_w, w_out, norm2_w, ffn_w_up, ffn_conv_w,
              ffn_conv_b, ffn_w_down, final_norm_w):
    s = x.shape[1]
    rows = s // GRID_W
    t = jnp.arange(s)
    row_pos = (t // GRID_W).astype(jnp.float32)
    col_pos = (t % GRID_W).astype(jnp.float32)
    for i in range(DEPTH):
        x, ctx = hybrid_layer(x, ctx, c, c_ctx, ada_w[i], ada_b[i], norm1_w[i], w_in[i], rpb[i],
                              ssd_conv_w[i], ssd_conv_b[i], dt_bias[i], a_log[i], ssd_d[i],
                              ssd_norm_w[i], w_out[i], norm2_w[i], ffn_w_up[i], ffn_conv_w[i],
                              ffn_conv_b[i], ffn_w_down[i], rows, row_pos, col_pos,
                              update_ctx=(i < DEPTH - 1))
    return rms_norm(x, final_norm_w)
```

```python
import numpy as np
import concourse.bass as bass
import concourse.mybir as mybir

F32 = mybir.dt.float32
BF16 = mybir.dt.bfloat16
AF = mybir.ActivationFunctionType
ALU = mybir.AluOpType

ENGS = ("pe", "act", "dve", "pool", "sp")
SEM_LIMIT = 30000
N_DMA_LANES = 24


class T:
    __slots__ = ("name", "last_writer", "readers", "excl")

    def __init__(self, name="", excl=False):
        self.name = name
        self.last_writer = None
        self.readers = []
        self.excl = excl


class Op:
    __slots__ = ("eng", "fn", "idx", "deps", "signals", "sig", "is_dma", "lane", "lane_val",
                 "lane_prev", "name")

    def __init__(self, eng, fn, idx, is_dma, name):
        self.eng = eng
        self.fn = fn
        self.idx = idx
        self.deps = set()
        self.signals = False
        self.sig = None
        self.is_dma = is_dma
        self.lane = None
        self.lane_val = None
        self.lane_prev = None
        self.name = name


class Prog:
    def __init__(self, nc):
        self.nc = nc
        self.eng_ops = {e: [] for e in ENGS}
        self.n_dma = {"sp": 0, "pool": 0, "act": 0, "pe": 0, "dve": 0}
        self.lane_last = [None] * N_DMA_LANES
        self.lane_cnt = [0] * N_DMA_LANES

    def op(self, eng, fn, reads=(), writes=(), dma=False, name=None):
        lst = self.eng_ops[eng]
        o = Op(eng, fn, len(lst), dma, name)
        lst.append(o)
        deps = o.deps
        for t in reads:
            if t.last_writer is not None:
                deps.add(t.last_writer)
            if t.excl:
                for r in t.readers:
                    if r.eng != eng:
                        deps.add(r)
        for t in writes:
            if t.last_writer is not None:
                deps.add(t.last_writer)
            for r in t.readers:
                deps.add(r)
        for t in reads:
            t.readers.append(o)
        for t in writes:
            t.readers = []
            t.last_writer = o
        deps.discard(o)
        if dma:
            half = N_DMA_LANES // 2
            lane = (self.n_dma[eng] % half) + (half if eng == "pool" else 0)
            self.n_dma[eng] += 1
            o.lane = lane
            o.lane_prev = self.lane_last[lane]
            self.lane_cnt[lane] += 1
            o.lane_val = 16 * self.lane_cnt[lane]
            self.lane_last[lane] = o
        return o

    def dma(self, eng, out, in_, reads=(), writes=(), **kw):
        return self.op(eng, lambda e: e.dma_start(out=out, in_=in_, **kw), reads, writes, dma=True)

    def emit(self, final_wait_ops=()):
        nc = self.nc
        for o in final_wait_ops:
            if not o.is_dma:
                o.signals = True
        for e in ENGS:
            for o in self.eng_ops[e]:
                for d in o.deps:
                    if d.is_dma:
                        continue
                    if d.eng == "pe" and o.eng == "pe":
                        continue
                    d.signals = True
        n_sig = {}
        for e in ENGS:
            c = 0
            for o in self.eng_ops[e]:
                if o.signals and not o.is_dma:
                    c += 1
                    o.sig = c
            n_sig[e] = c
        sems = {}
        for e in ENGS:
            n = (n_sig[e] + SEM_LIMIT - 1) // SEM_LIMIT
            sems[e] = [nc.alloc_semaphore(f"s_{e}_{i}") for i in range(max(n, 1))]
        lane_sems = [nc.alloc_semaphore(f"s_dma_{i}") for i in range(N_DMA_LANES)]
        self._sems = sems
        self._lane_sems = lane_sems
        engobj = {"pe": "tensor", "act": "scalar", "dve": "vector", "pool": "gpsimd", "sp": "sync"}

        def run_engine(ename, eng):
            waited = {}
            ops = self.eng_ops[ename]
            for o in ops:
                need = {}
                for d in o.deps:
                    if d.is_dma:
                        key = ("l", d.lane)
                        val = d.lane_val
                    else:
                        if d.eng == "pe" and ename == "pe":
                            continue
                        key = ("e", d.eng, (d.sig - 1) // SEM_LIMIT)
                        val = (d.sig - 1) % SEM_LIMIT + 1
                    if need.get(key, 0) < val:
                        need[key] = val
                if o.is_dma and o.lane_prev is not None:
                    key = ("l", o.lane)
                    val = o.lane_prev.lane_val
                    if need.get(key, 0) < val:
                        need[key] = val
                for key, val in need.items():
                    if waited.get(key, 0) >= val:
                        continue
                    waited[key] = val
                    if key[0] == "l":
                        eng.wait_ge(lane_sems[key[1]], val)
                    else:
                        eng.wait_ge(sems[key[1]][key[2]], val)
                ins = o.fn(eng)
                if o.is_dma:
                    ins.then_inc(lane_sems[o.lane], 16)
                elif o.signals:
                    ins.then_inc(sems[ename][(o.sig - 1) // SEM_LIMIT], 1)
            if ename == "sp":
                for o in final_wait_ops:
                    if o.is_dma:
                        eng.wait_ge(lane_sems[o.lane], o.lane_val)
                    else:
                        eng.wait_ge(sems[o.eng][(o.sig - 1) // SEM_LIMIT], (o.sig - 1) % SEM_LIMIT + 1)

        with nc.Block() as block:
            @block.tensor
            def _(e):
                run_engine("pe", e)

            @block.scalar
            def _(e):
                run_engine("act", e)

            @block.vector
            def _(e):
                run_engine("dve", e)

            @block.gpsimd
            def _(e):
                run_engine("pool", e)

            @block.sync
            def _(e):
                run_engine("sp", e)


U8 = mybir.dt.uint8
_DT_SIZE = {F32: 4, BF16: 2, U8: 1, mybir.dt.int32: 4, mybir.dt.uint32: 4}


class Mem:
    def __init__(self, nc, nbytes=206 * 1024):
        self.big = nc.alloc_sbuf_tensor("bigmem", [128, nbytes], U8)
        self.nbytes = nbytes

    def view(self, off, shape, dtype, p0=0):
        sz = _DT_SIZE[dtype]
        n = 1
        for d in shape[1:]:
            n *= d
        assert off % 4 == 0 and off + n * sz <= self.nbytes, (off, shape, self.nbytes)
        v = self.big[p0:p0 + shape[0], off:off + n * sz]
        if dtype != U8:
            v = v.bitcast(dtype)
        if len(shape) == 3:
            v = v.rearrange("p (a b) -> p a b", a=shape[1])
        elif len(shape) == 4:
            v = v.rearrange("p (a b c) -> p a b c", a=shape[1], b=shape[2])
        return v
from concourse.bass_utils import run_bass_kernel_spmd


KB = 1024
S_LEN = 2048
D = 1024
NTT = 16
LC = 256
EPS = 1e-6
D_FF = 2816
NFF = 22
FF_GROUPS = [(0, 8), (8, 15), (15, 22)]

PV_NW1, PV_NW2, PV_SCW, PV_SCB, PV_FCW, PV_FCB, PV_N = 0, 8, 16, 40, 48, 114, 136
BR_FNW, BR_SNW, BR_DTB, BR_ALOG, BR_D, BR_N = 0, 1024, 1536, 1552, 1568, 1576


def _add_barrier(P):
    lasts = []
    for e in ENGS:
        ops = P.eng_ops[e]
        if ops:
            lasts.append(ops[-1])
    if not hasattr(P, "bar_idx"):
        P.bar_idx = {e: 0 for e in ENGS}
    dmas = [o for e in ENGS for o in P.eng_ops[e][P.bar_idx[e]:] if o.is_dma]
    P.pending = set(lasts + dmas)
    P.pending_engs = set(ENGS)
    P.bar_idx = {e: len(P.eng_ops[e]) for e in ENGS}


_orig_op = Prog.op


def _op_with_barrier(self, eng, fn, reads=(), writes=(), dma=False, name=None):
    o = _orig_op(self, eng, fn, reads, writes, dma, name)
    pe = getattr(self, "pending_engs", None)
    if pe and eng in pe:
        o.deps |= self.pending
        o.deps.discard(o)
        pe.discard(eng)
    return o


Prog.op = _op_with_barrier
Prog.barrier = _add_barrier


def build(stage=99):
    nc = bass.Bass("TRN2", target_bir_lowering=False)
    P = Prog(nc)
    M = Mem(nc)

    def din(name, shape):
        return nc.dram_tensor(name, list(shape), F32, kind="ExternalInput").ap()

    x_d = din("x", [S_LEN, D])
    ctx_d = din("ctx", [LC, D])
    ccT_d = din("ccT", [128, 16])
    ada_w_d = din("ada_w", [D, 6 * D])
    ada_b2_d = din("ada_b2", [2, 6 * D])
    w_in_d = din("w_in", [D, 3088])
    w_out_d = din("w_out", [D, D])
    w_up_d = din("w_up", [NFF, 128, 2048])
    w_down_d = din("w_down", [D_FF, D])
    pvec_d = din("pvec", [128, PV_N])
    brow_d = din("brow", [128, BR_N])
    cmat_d = din("cmat", [128, 11 * 128])
    rope_d = din("rope", [128, 2 * S_LEN])
    gtab_d = din("gtab", [128, 8 * 14 * 64])
    mtab_d = din("mtab", [128, 2 * 14 * 64])
    out_d = nc.dram_tensor("out", [S_LEN, D], F32, kind="ExternalOutput").ap()
    dbg_d = None
    if stage < 99:
        dbg_d = nc.dram_tensor("dbg", [128, 8 * S_LEN], F32, kind="ExternalOutput").ap()

    PS = [nc.alloc_psum_tensor(f"psb{i}", [128, 512], F32) for i in range(8)]
    tPS = [T(f"ps{i}", excl=True) for i in range(8)]

    def psf(i):
        return PS[i][:, :]

    def psb(i):
        return PS[i][:, :].bitcast(BF16)

    def mm(out, lhsT, rhs, start, stop, reads, writes):
        return P.op("pe", lambda e: e.matmul(out, lhsT=lhsT, rhs=rhs, start=start, stop=stop), reads, writes)

    def tr(out, in_, ident, reads, writes):
        return P.op("pe", lambda e: e.transpose(out, in_, ident), reads, writes)

    def act(out, in_, func, reads, writes, scale=1.0, bias=0.0, accum_out=None):
        if accum_out is None:
            return P.op("act", lambda e: e.activation(out=out, in_=in_, func=func, scale=scale, bias=bias), reads, writes)
        return P.op("act", lambda e: e.activation(out=out, in_=in_, func=func, scale=scale, bias=bias,
                                                  accum_out=accum_out), reads, writes)

    def tt(out, in0, in1, op, reads, writes, eng="dve"):
        return P.op(eng, lambda e: e.tensor_tensor(out=out, in0=in0, in1=in1, op=op), reads, writes)

    def ts(out, in0, s1, op0, reads, writes, s2=None, op1=None, eng="dve"):
        if op1 is None:
            return P.op(eng, lambda e: e.tensor_scalar(out=out, in0=in0, scalar1=s1, scalar2=None, op0=op0), reads, writes)
        return P.op(eng, lambda e: e.tensor_scalar(out=out, in0=in0, scalar1=s1, scalar2=s2, op0=op0, op1=op1), reads, writes)

    def stt(out, in0, scalar, in1, op0, op1, reads, writes):
        return P.op("dve", lambda e: e.scalar_tensor_tensor(out=out, in0=in0, scalar=scalar, in1=in1, op0=op0, op1=op1),
                    reads, writes)

    def cp(eng, out, in_, reads, writes):
        if eng == "act":
            return act(out, in_, AF.Copy, reads, writes)
        return P.op("dve", lambda e: e.tensor_copy(out=out, in_=in_), reads, writes)

    def recip(out, in_, reads, writes):
        return P.op("dve", lambda e: e.reciprocal(out=out, in_=in_), reads, writes)

    def memset(out, val, writes):
        return P.op("dve", lambda e: e.memset(out, val), (), writes)

    def dma(eng, out, in_, reads, writes):
        return P.dma(eng, out, in_, reads, writes)

    ident_bf = M.view(0, [128, 128], BF16)
    rperm_bf = M.view(256, [128, 128], BF16)
    LE_bf = M.view(512, [128, 128], BF16)
    GE_bf = M.view(768, [128, 128], BF16)
    GT_bf = M.view(1024, [128, 128], BF16)
    LT_bf = M.view(1280, [128, 128], BF16)
    LE_f = M.view(1536, [128, 128], F32)
    ones_f = M.view(2048, [128, 128], F32)
    ident_f = M.view(2560, [128, 128], F32)
    pvec = M.view(3072, [128, PV_N], F32)
    brow = M.view(3616, [128, BR_N], F32)
    modT = M.view(9920, [128, 48, 2], F32)
    a1 = M.view(10304, [128, 8], F32)
    a1c = M.view(10336, [128, 8], F32)
    a2 = M.view(10368, [128, 8], F32)
    g1b = M.view(10400, [128, 1024], F32)
    g2b = M.view(14496, [128, 1024], F32)
    scT = M.view(18592, [128, 8, 2], BF16)
    ccs = M.view(18624, [128, 16], F32)
    stat = M.view(18688, [128, 64], F32)
    a_b = M.view(18944, [128, 16], F32)
    junk = M.view(20480, [128, 1024], BF16)
    sel_f = [M.view(22528 + i * 512, [128, 128], F32) for i in range(2)]
    neg_bf = [M.view(23552 + i * 256, [128, 128], BF16) for i in range(2)]
    t_const, t_pvec, t_brow, t_modT, t_a, t_g1b, t_g2b, t_scT, t_ccs, t_ab = [T(n) for n in
        "const pvec brow modT a g1b g2b scT ccs ab".split()]
    t_stat = [T(f"stat{i}") for i in range(64)]

    PH = 24 * KB
    import os as _os2
    _SALT = float(_os2.environ.get("KSALT", "0"))
    if _SALT:
        memset(junk[:, 0:8], _SALT, [])
    if _os2.environ.get("KPOISON", "0") == "1":
        memset(stat, 1.0e30, t_stat)
    if _os2.environ.get("KPOISON", "0") == "2":
        memset(M.view(22528 + 1024, [128, (206 * KB - 22528 - 1024) // 4], F32), 3.0e38, [])
        for i_ in range(8):
            memset(psf(i_), 3.0e38, [tPS[i_]])
        P.barrier()

    def finish_dbg(src_ap, ncols, trk, stage_off=24 * KB, direct=False):
        if direct:
            op = dma("sp", dbg_d[:, 0:ncols], src_ap, trk, [])
        else:
            dst = M.view(stage_off, [128, ncols], F32)
            t_d = T("dbgst")
            cp("dve", dst, src_ap, trk, [t_d])
            op = dma("sp", dbg_d[:, 0:ncols], dst, [t_d], [])
        P.emit(final_wait_ops=[op])
        return nc

    cstage = M.view(PH, [128, 11 * 128], F32)
    t_cst = T("cstage")
    dma("sp", cstage, cmat_d, (), [t_cst])
    dma("sp", pvec, pvec_d, (), [t_pvec])
    dma("sp", brow, brow_d, (), [t_brow])
    dma("sp", ccs, ccT_d, (), [t_ccs])
    for i, dst in enumerate([ident_bf, rperm_bf, LE_bf, GE_bf, GT_bf, LT_bf]):
        cp("dve", dst, cstage[:, i * 128:(i + 1) * 128], [t_cst], [t_const])
    cp("dve", LE_f, cstage[:, 2 * 128:3 * 128], [t_cst], [t_const])
    cp("dve", ones_f, cstage[:, 6 * 128:7 * 128], [t_cst], [t_const])
    cp("dve", ident_f, cstage[:, 0:128], [t_cst], [t_const])
    cp("dve", sel_f[0], cstage[:, 7 * 128:8 * 128], [t_cst], [t_const])
    cp("dve", sel_f[1], cstage[:, 8 * 128:9 * 128], [t_cst], [t_const])
    cp("dve", neg_bf[0], cstage[:, 9 * 128:10 * 128], [t_cst], [t_const])
    cp("dve", neg_bf[1], cstage[:, 10 * 128:11 * 128], [t_cst], [t_const])
    act(scT.rearrange("p a b -> p (a b)"), ccs, AF.Silu, [t_ccs], [t_scT])

    xs_all = [M.view(174 * KB + i * 4 * KB, [128, 1024], F32) for i in range(8)]
    t_xs_all = [T(f"xs{i}") for i in range(8)]
    x_v = x_d.rearrange("(t p) d -> t p d", p=128)
    ctx_v = ctx_d.rearrange("(t p) d -> t p d", p=128)
    for i in range(2):
        dma("sp", xs_all[i], ctx_v[i], (), [t_xs_all[i]])
    for t_ in range(6):
        dma("sp", xs_all[2 + t_], x_v[t_], (), [t_xs_all[2 + t_]])

    ada_w_v = ada_w_d.rearrange("(kc p) n -> p kc n", p=128)

    def mod_pieces(pcs, adab, t_adab, small_off, acc_bank, aux_bank):
        nb = len(adab)
        ab2p = [M.view(small_off + i * 2 * KB, [2, 512], F32) for i in range(2)]
        mrow = [M.view(small_off + 4 * KB + i * 2 * KB, [2, 512], F32) for i in range(2)]
        t_ab2p = [T(f"ab2p{i}") for i in range(2)]
        t_mrow = [T(f"mrow{i}") for i in range(2)]
        for n_, pc in enumerate(pcs[:nb]):
            dma("pool", adab[n_], ada_w_v[:, :, pc * 512:(pc + 1) * 512], (), [t_adab[n_]])
        for n_, pc in enumerate(pcs):
            b_, i_ = n_ % nb, n_ % 2
            dma("sp", ab2p[i_], ada_b2_d[:, pc * 512:(pc + 1) * 512], (), [t_ab2p[i_]])
            for kc in range(8):
                mm(PS[acc_bank][0:2, :], scT[:, kc, :], adab[b_][:, kc, :], kc == 0, kc == 7,
                   [t_scT, t_adab[b_]], [tPS[acc_bank]])
            if n_ + nb < len(pcs):
                pn = pcs[n_ + nb]
                dma("pool", adab[b_], ada_w_v[:, :, pn * 512:(pn + 1) * 512], (), [t_adab[b_]])
            tt(mrow[i_], PS[acc_bank][0:2, :], ab2p[i_], ALU.add, [tPS[acc_bank], t_ab2p[i_]], [t_mrow[i_]])
            pv = PS[aux_bank][:, 0:8].rearrange("p (a b) -> p a b", b=2)
            for jj in range(4):
                mm(pv[:, jj, :], mrow[i_][0:2, jj * 128:(jj + 1) * 128], ident_f[0:2, 0:2], True, True,
                   [t_mrow[i_], t_const], [tPS[aux_bank]])
            cp("dve", modT[:, pc * 4:(pc + 1) * 4, :], pv, [tPS[aux_bank]], [t_modT])
            if pc in (4, 5, 10, 11):
                dst, tdst = (g1b, t_g1b) if pc < 6 else (g2b, t_g2b)
                mm(psf(aux_bank), ones_f[0:1, 0:128], mrow[i_][0:1, :], True, True, [t_mrow[i_], t_const], [tPS[aux_bank]])
                cp("act", dst[:, (pc % 2) * 512:(pc % 2 + 1) * 512], psf(aux_bank), [tPS[aux_bank]], [tdst])

    adab0 = [M.view(PH + 6 * KB + i * 8 * KB, [128, 8, 512], BF16) for i in range(3)]
    t_adab0 = [T(f"adab{i}") for i in range(3)]
    mod_pieces([0, 1, 2, 3, 4, 5], adab0, t_adab0, PH + 30 * KB, 0, 2)
    stt(a1, modT[:, 8:16, 0], 1.0, pvec[:, PV_NW1:PV_NW1 + 8], ALU.add, ALU.mult, [t_modT, t_pvec], [t_a])
    stt(a1c, modT[:, 8:16, 1], 1.0, pvec[:, PV_NW1:PV_NW1 + 8], ALU.add, ALU.mult, [t_modT, t_pvec], [t_a])
    act(a_b, brow[:, BR_ALOG:BR_ALOG + 16], AF.Exp, [t_brow], [t_ab])
    ts(a_b, a_b, -1.0, ALU.mult, [t_ab], [t_ab])
    P.barrier()
    if stage == 0:
        return finish_dbg(g1b, 1024, [t_g1b], direct=True)

    stat_ctr = [0]

    def norm_group(src_tiles, src_trk, n, xn_bufs, t_xn, a_ap, sh_ap, dst_fn, dst_trk, t_par, bank0):
        for i in range(n):
            c = stat_ctr[0] % 32
            stat_ctr[0] += 1
            ss = stat[:, 2 * c:2 * c + 1]
            rs = stat[:, 2 * c + 1:2 * c + 2]
            tst = t_stat[c]
            act(junk, src_tiles[i], AF.Square, [src_trk[i]], [tst], accum_out=ss)
            act(rs, ss, AF.Ln, [tst], [tst], scale=1.0 / D, bias=EPS)
            act(rs, rs, AF.Exp, [tst], [tst], scale=-0.5)
            ts(xn_bufs[i], src_tiles[i], rs, ALU.mult, [src_trk[i], tst], [t_xn[i]])
        for k in range(8):
            bank = bank0 + (k % 2)
            pb = psb(bank)
            for i in range(n):
                tr(pb[:, i * 128:(i + 1) * 128], xn_bufs[i][:, k * 128:(k + 1) * 128], ident_bf,
                   [t_xn[i], t_const], [tPS[bank]])
            if k % 2 == 0:
                act(dst_fn(k), pb[:, 0:n * 128], AF.Identity, [tPS[bank], t_par, t_modT], [dst_trk[k]],
                    scale=a_ap[:, k:k + 1], bias=sh_ap[:, k:k + 1])
            else:
                ts(dst_fn(k), pb[:, 0:n * 128], a_ap[:, k:k + 1], ALU.mult, [tPS[bank], t_par, t_modT], [dst_trk[k]],
                   s2=sh_ap[:, k:k + 1], op1=ALU.add)

    hT = M.view(PH, [128, 8, S_LEN], BF16)
    t_hT = [[T(f"hT{k}_{tb}") for tb in range(4)] for k in range(8)]
    mixT = M.view(174 * KB, [128, 8, S_LEN], BF16)
    t_mix = [[T(f"mix{k}_{c}") for c in range(16)] for k in range(8)]
    hcT = M.view(112 * KB, [128, 8, LC], BF16)
    t_hcT = [T(f"hcT{k}") for k in range(8)]
    expB = [M.view(146 * KB + i * 14336, [128, 8, 14, 64], BF16) for i in range(2)]
    t_expB = T("expB")

    xn_b = [M.view(56 * KB + i * 2 * KB, [128, 1024], BF16) for i in range(8)]
    t_xn = [T(f"xn{i}") for i in range(8)]
    gst = M.view(72 * KB, [128, 8, 14, 64], F32)
    mst = M.view(100 * KB, [128, 2, 14 * 64], F32)
    t_gst, t_mst = T("gst"), T("mst")
    wb = [M.view(121 * KB + i * 8 * KB, [128, 8, 512], BF16) for i in range(3)]
    t_wb = [T(f"wb{i}") for i in range(3)]
    w_in_v = w_in_d.rearrange("(kc p) n -> p kc n", p=128)
    for g in range(3):
        dma("pool", wb[g], w_in_v[:, :, g * 512:(g + 1) * 512], (), [t_wb[g]])

    cosT = M.view(174 * KB, [128, S_LEN], F32)
    sinT = M.view(182 * KB, [128, S_LEN], F32)
    t_rope = T("rope")
    dma("sp", gst.rearrange("p a b c -> p (a b c)"), gtab_d, (), [t_gst])
    dma("sp", mst.rearrange("p a b -> p (a b)"), mtab_d, (), [t_mst])

    def build_expB():
        act(gst.rearrange("p a b c -> p (a b c)"), gst.rearrange("p a b c -> p (a b c)"), AF.Exp, [t_gst], [t_gst])
        for i in range(2):
            tt(expB[i].rearrange("p a b c -> p a (b c)"), gst.rearrange("p a b c -> p a (b c)"),
               mst[:, i:i + 1, :].broadcast_to([128, 8, 14 * 64]), ALU.mult, [t_gst, t_mst], [t_expB])

    sh1 = modT[:, 0:8, 0]
    sh1c = modT[:, 0:8, 1]
    sh2 = modT[:, 24:32, 0]
    t_par = T("par")
    norm_group(xs_all[0:2], t_xs_all[0:2], 2, xn_b[4:8], t_xn[4:8], a1c, sh1c, lambda k: hcT[:, k, :], t_hcT, t_a, 2)
    for tb in range(4):
        if tb < 3:
            for t_ in range(max(6, 4 * tb + 4), 4 * tb + 8):
                dma("sp", xs_all[(2 + t_) % 8], x_v[t_], (), [t_xs_all[(2 + t_) % 8]])
        bufs = [(2 + 4 * tb + i) % 8 for i in range(4)]
        xo = 4 * (tb % 2)
        norm_group([xs_all[j_] for j_ in bufs], [t_xs_all[j_] for j_ in bufs], 4, xn_b[xo:xo + 4], t_xn[xo:xo + 4], a1, sh1,
                   (lambda tb: (lambda k: hT[:, k, tb * 512:(tb + 1) * 512]))(tb),
                   [t_hT[k][tb] for k in range(8)], t_a, 2 * (tb % 2))
        if tb == 0:
            build_expB()
        if tb == 2:
            dma("sp", sinT, rope_d[:, S_LEN:2 * S_LEN], (), [t_xs_all[2], t_xs_all[3], t_rope])
        if tb == 3:
            dma("sp", cosT, rope_d[:, 0:S_LEN], (), [t_xs_all[0], t_xs_all[1], t_rope])
    P.barrier()
    if stage == 0.5:
        return finish_dbg(hT.rearrange("p a b -> p (a b)"), 16384, [t_hT[k][tb] for k in range(8) for tb in range(4)], stage_off=130 * KB)
    if stage == 0.6:
        return finish_dbg(expB[0].rearrange("p a b c -> p (a b c)"), 8 * 14 * 64, [t_expB], stage_off=56 * KB)

    qT = M.view(56 * KB, [128, 4, S_LEN], BF16)
    kT = M.view(72 * KB, [128, 4, S_LEN], BF16)
    t_qT = [[T(f"qT{hp}_{tb}") for tb in range(4)] for hp in range(4)]
    t_kT = [[T(f"kT{hp}_{tb}") for tb in range(4)] for hp in range(4)]
    v_aug = M.view(88 * KB, [128, NTT, 4, 192], BF16)
    t_v = [T(f"v{tt_}") for tt_ in range(NTT)]
    kcT = M.view(116 * KB, [128, 4, LC], BF16)
    t_kcT = T("kcT")
    vc_aug = M.view(118 * KB, [128, 2, 4, 192], BF16)
    t_vc = T("vc")
    qb = [M.view(190 * KB + i * KB, [128, 512], BF16) for i in range(2)]
    t_qb = [T(f"qb{i}") for i in range(2)]
    rt1 = [M.view(192 * KB + i * 2 * KB, [128, 512], F32) for i in range(2)]
    rt2 = [M.view(196 * KB + i * 2 * KB, [128, 512], F32) for i in range(2)]
    t_rt1 = [T(f"rt1{i}") for i in range(2)]
    t_rt2 = [T(f"rt2{i}") for i in range(2)]

    memset(v_aug[:, :, :, 64:128], 1.0, t_v)
    memset(vc_aug[:, :, :, 64:128], 1.0, [t_vc])

    if stage == 0.65:
        return finish_dbg(cosT, 2048, [t_rope] + t_wb + t_v + [t_vc], direct=True)
    rope_ctr = [0]

    def proj_fm(wbuf, t_w, cb, src, t_src_k, ntok, tok0, bank):
        for kc in range(8):
            mm(PS[bank][:, 0:ntok], wbuf[:, kc, cb * 128:(cb + 1) * 128], src[:, kc, tok0:tok0 + ntok],
               kc == 0, kc == 7, [t_w, t_src_k[kc]], [tPS[bank]])

    import os as _os
    _NOROPE = _os.environ.get("NOROPE", "0")

    def rope_evac(bank, dst, t_dst, tb):
        if _NOROPE == "1":
            cp("act", dst, psf(bank), [tPS[bank]], [t_dst])
            return
        i = rope_ctr[0] % 2
        rope_ctr[0] += 1
        b2 = 4 + i
        act(qb[i], psf(bank), AF.Copy, [tPS[bank]], [t_qb[i]])
        if _NOROPE == "4":
            tt(rt1[i], psf(bank), cosT[:, tb * 512:(tb + 1) * 512], ALU.mult, [tPS[bank], t_rope], [t_rt1[i]])
            cp("dve", dst, rt1[i], [t_rt1[i]], [t_dst])
            return
        if _NOROPE == "5":
            cp("dve", rt1[i], psf(bank), [tPS[bank]], [t_rt1[i]])
            cp("dve", rt2[i], psf(bank), [tPS[bank]], [t_rt2[i]])
            tt(dst, rt1[i], rt2[i], ALU.add, [t_rt1[i], t_rt2[i]], [t_dst])
            return
        if _NOROPE == "3":
            tt(rt1[i], psf(bank), cosT[:, tb * 512:(tb + 1) * 512], ALU.mult, [tPS[bank], t_rope], [t_rt1[i]])
            tt(rt2[i], psf(bank), sinT[:, tb * 512:(tb + 1) * 512], ALU.mult, [tPS[bank], t_rope], [t_rt2[i]])
            tt(dst, rt1[i], rt2[i], ALU.add, [t_rt1[i], t_rt2[i]], [t_dst])
            return
        mm(psf(b2), rperm_bf, qb[i], True, True, [t_qb[i], t_const], [tPS[b2]])
        if _NOROPE == "2":
            cp("dve", dst, psf(b2), [tPS[b2]], [t_dst])
            return
        tt(rt1[i], psf(bank), cosT[:, tb * 512:(tb + 1) * 512], ALU.mult, [tPS[bank], t_rope], [t_rt1[i]])
        tt(rt2[i], psf(b2), sinT[:, tb * 512:(tb + 1) * 512], ALU.mult, [tPS[b2], t_rope], [t_rt2[i]])
        tt(dst, rt1[i], rt2[i], ALU.add, [t_rt1[i], t_rt2[i]], [t_dst])

    blk = 0
    for (g, dstT, t_dst) in ((0, qT, t_qT), (1, kT, t_kT)):
        for hp in range(4):
            for tb in range(4):
                bank = blk % 4
                blk += 1
                proj_fm(wb[g], t_wb[g], hp, hT, [t_hT[k][tb] for k in range(8)], 512, tb * 512, bank)
                rope_evac(bank, dstT[:, hp, tb * 512:(tb + 1) * 512], t_dst[hp][tb], tb)
    if stage == 0.66:
        return finish_dbg(qT.rearrange("p a b -> p (a b)"), 8192, [t_qT[a][b] for a in range(4) for b in range(4)], stage_off=130 * KB)
    for hp in range(4):
        bank = blk % 4
        blk += 1
        proj_fm(wb[1], t_wb[1], hp, hcT, t_hcT, LC, 0, bank)
        cp("act", kcT[:, hp, :], PS[bank][:, 0:LC], [tPS[bank]], [t_kcT])

    def v_evac(bank, dst4, t_dst):
        src = psf(bank).rearrange("p (a b c) -> p a b c", a=4, b=2)
        cp("act", dst4[:, :, 0:64], src[:, :, 0, :], [tPS[bank]], [t_dst])
        cp("dve", dst4[:, :, 128:192], src[:, :, 1, :], [tPS[bank]], [t_dst])

    for tt_ in range(NTT):
        bank = blk % 4
        blk += 1
        for kc in range(8):
            mm(psf(bank), hT[:, kc, tt_ * 128:(tt_ + 1) * 128], wb[2][:, kc, :], kc == 0, kc == 7,
               [t_hT[kc][tt_ // 4], t_wb[2]], [tPS[bank]])
        v_evac(bank, v_aug[:, tt_], t_v[tt_])
    for ct in range(2):
        bank = blk % 4
        blk += 1
        for kc in range(8):
            mm(psf(bank), hcT[:, kc, ct * 128:(ct + 1) * 128], wb[2][:, kc, :], kc == 0, kc == 7,
               [t_hcT[kc], t_wb[2]], [tPS[bank]])
        v_evac(bank, vc_aug[:, ct], t_vc)
    P.barrier()
    if stage == 0.7:
        return finish_dbg(qT.rearrange("p a b -> p (a b)"), 8192, [t_qT[a][b] for a in range(4) for b in range(4)], stage_off=130 * KB)
    if stage == 0.8:
        return finish_dbg(v_aug[:, 0:8].rearrange("p a b c -> p (a b c)"), 8 * 768, [t_v[a] for a in range(8)], stage_off=130 * KB)

    Etb = [[M.view(121 * KB + (par * 3 + i) * KB, [128, 512], BF16) for i in range(3)] for par in range(2)]
    Ptb = [[M.view(127 * KB + (par * 4 + i) * KB, [128, 512], BF16) for i in range(4)] for par in range(2)]
    t_Etb = [[T(f"Et{par}{i}") for i in range(3)] for par in range(2)]
    t_Ptb = [[T(f"Pt{par}{i}") for i in range(4)] for par in range(2)]
    rec = [M.view(135 * KB + i * KB, [128, 256], F32) for i in range(4)]
    tmpO = [M.view(139 * KB + i * KB, [128, 256], F32) for i in range(4)]
    lnS = [M.view(143 * KB + i * KB, [128, 256], F32) for i in range(2)]
    t_rec = [T(f"rec{i}") for i in range(4)]
    t_tmpO = [T(f"tmpO{i}") for i in range(4)]
    t_lnS = [T(f"lnS{i}") for i in range(2)]
    for i in range(4):
        memset(rec[i], 0.0, [t_rec[i]])

    groups = []
    for j in range(8):
        if j == 0:
            kts, tab = [0, 1, 2, 3], 1
        elif j == 7:
            kts, tab = [12, 13, 14, 15], 1
        else:
            kts, tab = list(range(2 * j - 2, 2 * j + 4)), 0
        for h in range(8):
            groups.append((j, h, tab, [("loc", kt) for kt in kts] + [("ctx", 0), ("ctx", 1)]))

    def emit_S(gi):
        j, h, tab, lst = groups[gi]
        par = gi % 2
        hp, hb = h // 2, (h % 2) * 64
        qs = qT[hb:hb + 64, hp, j * 256:(j + 1) * 256]
        nb = len(lst) // 2
        for b_ in range(nb):
            for half in range(2):
                kind, kt = lst[2 * b_ + half]
                if kind == "loc":
                    ks, rk = kT[hb:hb + 64, hp, kt * 128:(kt + 1) * 128], t_kT[hp][kt // 4]
                else:
                    ks, rk = kcT[hb:hb + 64, hp, kt * 128:(kt + 1) * 128], t_kcT
                mm(PS[b_][:, half * 256:(half + 1) * 256], ks, qs, True, True, [rk, t_qT[hp][j // 2]], [tPS[b_]])
            if lst[2 * b_][0] == "loc":
                act(Etb[par][b_], psf(b_), AF.Exp, [tPS[b_]], [t_Etb[par][b_]], scale=0.125)
                for half in range(2):
                    kt = lst[2 * b_ + half][1]
                    e0 = 6 - (2 * kt - 4 * j)
                    bview = expB[tab][:, h, e0:e0 + 4, :].rearrange("p a b -> p (a b)")
                    tt(Ptb[par][b_][:, half * 256:(half + 1) * 256], Etb[par][b_][:, half * 256:(half + 1) * 256], bview,
                       ALU.mult, [t_Etb[par][b_], t_expB], [t_Ptb[par][b_]])
            else:
                act(Ptb[par][b_], psf(b_), AF.Exp, [tPS[b_]], [t_Ptb[par][b_]], scale=0.125)

    def emit_PV(gi):
        j, h, tab, lst = groups[gi]
        par = gi % 2
        hp, odd = h // 2, h % 2
        ob = 4 + par
        c0 = 64 if odd else 0
        n = len(lst)
        for i, (kind, kt) in enumerate(lst):
            if kind == "loc":
                lhs, rv = v_aug[:, kt, hp, c0:c0 + 128], t_v[kt]
            else:
                lhs, rv = vc_aug[:, kt, hp, c0:c0 + 128], t_vc
            mm(PS[ob][:, 0:256], lhs, Ptb[par][i // 2][:, (i % 2) * 256:(i % 2 + 1) * 256], i == 0, i == n - 1,
               [rv, t_Ptb[par][i // 2]], [tPS[ob]])
        r = gi % 4
        sr = 0 if odd else 64
        obp = 64 if odd else 0
        act(lnS[par][sr:sr + 1, :], PS[ob][sr:sr + 1, 0:256], AF.Ln, [tPS[ob]], [t_lnS[par]])
        act(rec[r][sr:sr + 1, :], lnS[par][sr:sr + 1, :], AF.Exp, [t_lnS[par]], [t_rec[r]], scale=-1.0)
        cp("dve", tmpO[r][obp:obp + 64, :], PS[ob][obp:obp + 64, 0:256], [tPS[ob]], [t_tmpO[r]])

    def emit_norm(gi):
        j, h, tab, lst = groups[gi]
        par = gi % 2
        hp, odd = h // 2, h % 2
        r = gi % 4
        sr = 0 if odd else 64
        obp = 64 if odd else 0
        mm(PS[6][:, par * 256:(par + 1) * 256], sel_f[1 if sr else 0], rec[r], True, True,
           [t_rec[r], t_const], [tPS[6]])
        tt(mixT[obp:obp + 64, hp, j * 256:(j + 1) * 256], tmpO[r][obp:obp + 64, :],
           PS[6][obp:obp + 64, par * 256:(par + 1) * 256], ALU.mult, [t_tmpO[r], tPS[6]],
           [t_mix[hp][2 * j], t_mix[hp][2 * j + 1]])

    ng_ = len(groups)
    for step in range(ng_ + 2):
        if step < ng_:
            emit_S(step)
        if 0 <= step - 1 < ng_:
            emit_PV(step - 1)
        if 0 <= step - 2 < ng_:
            emit_norm(step - 2)
    P.barrier()


    if stage == 1:
        return finish_dbg(mixT[:, 0:4, :].rearrange("p a b -> p (a b)"), 4 * S_LEN,
                          [t_mix[k][c] for k in range(4) for c in range(16)])

    zs = M.view(56 * KB, [128, NTT, 512], BF16)
    t_zs = [T(f"zs{i}") for i in range(NTT)]
    xbc = M.view(72 * KB, [128, 8, S_LEN], BF16)
    t_xbc = [T(f"xbc{i}") for i in range(8)]
    ctxx = M.view(104 * KB, [128, 6, LC], BF16)
    t_ctxx = [T(f"ctxx{i}") for i in range(6)]
    TB0 = 107 * KB
    dt_all, dta, CFs, TOTs, D1, ecum, wend, cdt, tmpA = [M.view(TB0 + i * 1152, [128, 18, 16], F32) for i in range(9)]
    t_dt, t_dta, t_cf, t_tot, t_d1, t_ecum, t_wend, t_cd, t_tmpA = [T(n) for n in
        "dt dta cf tot d1 ecum wend cd tmpA".split()]
    wdt = M.view(145 * KB, [128, 8, 16], BF16)
    t_wdt = T("wdt")
    pad = [M.view(146 * KB + i * 4104, [128, 2052], BF16) for i in range(2)]
    t_pad = [T(f"pad{i}") for i in range(2)]
    cpad = M.view(155 * KB, [128, 260], BF16)
    t_cpad = T("cpad")
    diag = M.view(156 * KB, [128, 24, 128], BF16)
    t_diag = T("diag")
    for i in range(24):
        ts(diag[:, i, :], ident_bf, pvec[:, PV_SCW + i:PV_SCW + i + 1], ALU.mult, [t_const, t_pvec], [t_diag])

    for g in range(3):
        dma("pool", wb[g], w_in_v[:, :, 1536 + g * 512:1536 + (g + 1) * 512], (), [t_wb[g]])
    dma("pool", wdt, w_in_v[:, :, 3072:3088], (), [t_wdt])
    for i in range(2):
        memset(pad[i][:, 0:1], 0.0, [t_pad[i]])
        memset(pad[i][:, 2049:2050], 0.0, [t_pad[i]])
    memset(cpad[:, 0:1], 0.0, [t_cpad])
    memset(cpad[:, 257:258], 0.0, [t_cpad])

    blk = 0
    for tt_ in range(NTT):
        bank = blk % 4
        blk += 1
        for kc in range(8):
            mm(psf(bank), hT[:, kc, tt_ * 128:(tt_ + 1) * 128], wb[0][:, kc, :], kc == 0, kc == 7,
               [t_hT[kc][tt_ // 4], t_wb[0]], [tPS[bank]])
        act(zs[:, tt_, :], psf(bank), AF.Silu, [tPS[bank]], [t_zs[tt_]])
    for c in range(18):
        for kc in range(8):
            if c < 16:
                lhs, rd = hT[:, kc, c * 128:(c + 1) * 128], t_hT[kc][c // 4]
            else:
                lhs, rd = hcT[:, kc, (c - 16) * 128:(c - 15) * 128], t_hcT[kc]
            mm(PS[7][:, c * 16:(c + 1) * 16], lhs, wdt[:, kc, :], kc == 0, kc == 7, [rd, t_wdt], [tPS[7]])
    ps7v = PS[7][:, 0:288].rearrange("p (a b) -> p a b", b=16)
    tt(dt_all, ps7v, brow[:, BR_DTB:BR_DTB + 16].unsqueeze(1).broadcast_to([128, 18, 16]), ALU.add,
       [tPS[7], t_brow], [t_dt])
    dt_flat = dt_all.rearrange("p a b -> p (a b)")
    act(dt_flat, dt_flat, AF.Exp, [t_dt], [t_dt])
    act(dt_flat, dt_flat, AF.Ln, [t_dt], [t_dt], bias=1.0)

    conv_ctr = [0]

    def conv_silu(padb, t_padb, n, cc, dst, t_dst):
        bb = pvec[:, PV_SCB + cc:PV_SCB + cc + 1]
        nblk = max(1, n // 512)
        w_ = n // nblk
        for tb in range(nblk):
            bank = 4 + conv_ctr[0] % 2
            conv_ctr[0] += 1
            for k in range(3):
                mm(PS[bank][:, 0:w_], diag[:, cc * 3 + k, :], padb[:, tb * w_ + k:tb * w_ + k + w_], k == 0, k == 2,
                   [t_diag, t_padb], [tPS[bank]])
            act(dst[:, tb * w_:(tb + 1) * w_], PS[bank][:, 0:w_], AF.Silu, [tPS[bank], t_pvec], [t_dst], bias=bb)

    for cc in range(8):
        wsel, cb = (1, cc) if cc < 4 else (2, cc - 4)
        pb_ = cc % 2
        for tb in range(4):
            bank = blk % 4
            blk += 1
            proj_fm(wb[wsel], t_wb[wsel], cb, hT, [t_hT[k][tb] for k in range(8)], 512, tb * 512, bank)
            cp("act" if tb % 2 == 0 else "dve", pad[pb_][:, 1 + tb * 512:1 + (tb + 1) * 512], psf(bank),
               [tPS[bank]], [t_pad[pb_]])
        conv_silu(pad[pb_], t_pad[pb_], S_LEN, cc, xbc[:, cc, :], t_xbc[cc])
    for cc in range(6):
        wsel, cb = (1, cc) if cc < 4 else (2, cc - 4)
        bank = blk % 4
        blk += 1
        proj_fm(wb[wsel], t_wb[wsel], cb, hcT, t_hcT, LC, 0, bank)
        cp("act", cpad[:, 1:1 + LC], PS[bank][:, 0:LC], [tPS[bank]], [t_cpad])
        conv_silu(cpad, t_cpad, LC, cc, ctxx[:, cc, :], t_ctxx[cc])
    P.barrier()

    xsB = M.view(121 * KB, [128, 18, 768], BF16)
    t_xsB = [T(f"xsB{i}") for i in range(18)]
    for c in range(18):
        bank = c % 2
        pb = psb(bank)
        for cc in range(6):
            if c < 16:
                src, rd = xbc[:, cc, c * 128:(c + 1) * 128], t_xbc[cc]
            else:
                src, rd = ctxx[:, cc, (c - 16) * 128:(c - 15) * 128], t_ctxx[cc]
            tr(pb[:, cc * 128:(cc + 1) * 128], src, ident_bf, [rd, t_const], [tPS[bank]])
        cp("act" if c % 2 == 0 else "dve", xsB[:, c, :], pb[:, 0:768], [tPS[bank]], [t_xsB[c]])
    P.barrier()

    tt(dta, dt_all, a_b.unsqueeze(1).broadcast_to([128, 18, 16]), ALU.mult, [t_dt, t_ab], [t_dta])
    for c in range(18):
        mm(PS[0][:, c * 16:(c + 1) * 16], LE_f, dta[:, c, :], True, True, [t_dta, t_const], [tPS[0]])
    for c in range(18):
        mm(PS[1][:, c * 16:(c + 1) * 16], ones_f, dta[:, c, :], True, True, [t_dta, t_const], [tPS[1]])
    fl = lambda a: a.rearrange("p a b -> p (a b)")
    cp("dve", fl(CFs), PS[0][:, 0:288], [tPS[0]], [t_cf])
    cp("dve", fl(TOTs), PS[1][:, 0:288], [tPS[1]], [t_tot])
    tt(fl(D1), fl(TOTs), fl(CFs), ALU.subtract, [t_tot, t_cf], [t_d1])
    act(fl(cdt), fl(TOTs), AF.Exp, [t_tot], [t_cd])
    F_, B_ = slice(0, 8), slice(8, 16)
    act(ecum[:, :, F_], CFs[:, :, F_], AF.Exp, [t_cf], [t_ecum])
    act(wend[:, :, F_], D1[:, :, F_], AF.Exp, [t_d1], [t_wend])
    tt(tmpA[:, :, B_], D1[:, :, B_], dta[:, :, B_], ALU.add, [t_d1, t_dta], [t_tmpA])
    act(ecum[:, :, B_], tmpA[:, :, B_], AF.Exp, [t_tmpA], [t_ecum])
    tt(tmpA[:, :, F_], CFs[:, :, B_], dta[:, :, B_], ALU.subtract, [t_cf, t_dta, t_tmpA], [t_tmpA])
    act(wend[:, :, B_], tmpA[:, :, F_], AF.Exp, [t_tmpA], [t_wend])
    tt(fl(wend), fl(wend), fl(dt_all), ALU.mult, [t_wend, t_dt], [t_wend])

    Hst = [M.view(148 * KB + i * 2 * KB, [128, 512], F32) for i in range(2)]
    t_H = [T(f"H{i}") for i in range(2)]
    Hin = M.view(24 * KB, [128, 16, 2, 512], BF16)
    t_Hin = [[T(f"Hin{c}_{d}") for d in range(2)] for c in range(16)]
    xwb = [M.view(152 * KB + i * KB, [128, 512], BF16) for i in range(4)]
    t_xwb = [T(f"xw{i}") for i in range(4)]
    v8 = lambda a: a.rearrange("p (h d) -> p h d", h=8)
    bc8 = lambda a: a.unsqueeze(2).broadcast_to([128, 8, 64])
    for d_ in range(2):
        memset(Hst[d_], 0.0, [t_H[d_]])
    xw_ctr = [0]

    def state_step(d_, c):
        i = xw_ctr[0] % 4
        xw_ctr[0] += 1
        bank = 2 + d_
        hs = slice(d_ * 8, d_ * 8 + 8)
        tt(v8(xwb[i]), v8(xsB[:, c, 0:512]), bc8(wend[:, c, hs]), ALU.mult, [t_xsB[c], t_wend], [t_xwb[i]], eng="pool")
        for g in range(2):
            mm(PS[bank][:, g * 256:(g + 1) * 256], xsB[:, c, 512 + g * 128:512 + (g + 1) * 128],
               xwb[i][:, g * 256:(g + 1) * 256], True, True, [t_xsB[c], t_xwb[i]], [tPS[bank]])
        tt(v8(Hst[d_]), v8(Hst[d_]), bc8(cdt[:, c, hs]), ALU.mult, [t_H[d_], t_cd], [t_H[d_]])
        tt(Hst[d_], Hst[d_], psf(bank), ALU.add, [t_H[d_], tPS[bank]], [t_H[d_]])

    state_step(0, 16)
    state_step(0, 17)
    state_step(1, 17)
    state_step(1, 16)
    for s_ in range(16):
        for d_, c in ((0, s_), (1, 15 - s_)):
            cp("act", Hin[:, c, d_, :], Hst[d_], [t_H[d_]], [t_Hin[c][d_]])
            if s_ < 15:
                state_step(d_, c)
    if stage == 1.5:
        return finish_dbg(Hin[:, 0, :, :].rearrange("p a b -> p (a b)"), 1024, [t_Hin[0][0], t_Hin[0][1]], stage_off=56 * KB)

    xdt = [[M.view(156 * KB + (d_ * 2 + i) * KB, [128, 512], BF16) for i in range(2)] for d_ in range(2)]
    t_xdt = [[T(f"xdt{d_}{i}") for i in range(2)] for d_ in range(2)]
    Abuf = [M.view(160 * KB + d_ * 2 * KB, [128, 8, 128], BF16) for d_ in range(2)]
    t_A = [T(f"A{d_}") for d_ in range(2)]
    Eb = [M.view(164 * KB + i * KB, [128, 512], BF16) for i in range(4)]
    Mb = [M.view(168 * KB + i * KB, [128, 512], BF16) for i in range(4)]
    t_E = [T(f"E{i}") for i in range(4)]
    t_M = [T(f"M{i}") for i in range(4)]
    CBm = [M.view(172 * KB + i * 512, [128, 2, 128], BF16) for i in range(2)]
    t_CBm = [T(f"CBm{i}") for i in range(2)]
    ynb = M.view(173 * KB, [128, 512], BF16)
    t_yn = T("yn")
    ytmp = [M.view(72 * KB + i * 2 * KB, [128, 512], F32) for i in range(4)]
    t_yt = [T(f"yt{i}") for i in range(4)]
    tri = [LE_bf, GE_bf]
    stri = [GT_bf, LT_bf]
    Mb2 = [Mb, [M.view(80 * KB + i * KB, [128, 512], BF16) for i in range(4)]]
    t_M2 = [t_M, [T(f"Mx{i}") for i in range(4)]]
    ynb2 = [ynb, M.view(84 * KB, [128, 512], BF16)]
    xDb = [M.view(85 * KB + i * KB, [128, 512], BF16) for i in range(2)]
    t_xD = [T(f"xD{i}") for i in range(2)]
    t_yn2 = [t_yn, T("yn2")]

    def ssd_s1(c):
        cs = slice(c * 128, (c + 1) * 128)
        pi = c % 2
        cbk = 0 if pi == 0 else 7
        for g in range(2):
            mm(PS[cbk][:, g * 128:(g + 1) * 128], xbc[:, 4 + g, cs], xbc[:, 6 + g, cs], True, True,
               [t_xbc[4 + g], t_xbc[6 + g]], [tPS[cbk]])
        cbv = PS[cbk][:, 0:256].rearrange("p (a b) -> p a b", a=2)
        for d_ in range(2):
            hs = slice(d_ * 8, d_ * 8 + 8)
            tt(Abuf[d_], stri[d_].unsqueeze(1).broadcast_to([128, 8, 128]),
               dta[:, c, hs].unsqueeze(2).broadcast_to([128, 8, 128]), ALU.mult, [t_const, t_dta], [t_A[d_]], eng="pool")
            tt(v8(xdt[d_][pi]), v8(xsB[:, c, 0:512]), bc8(dt_all[:, c, hs]), ALU.mult, [t_xsB[c], t_dt], [t_xdt[d_][pi]])
            if d_ == 0:
                tt(v8(xDb[pi]), v8(xsB[:, c, 0:512]), bc8(brow[:, BR_D:BR_D + 8]), ALU.mult, [t_xsB[c], t_brow], [t_xD[pi]],
                   eng="pool")
            for g in range(2):
                bank = 2 + g
                e_i = d_ * 2 + g
                for hh in range(4):
                    mm(PS[bank][:, hh * 128:(hh + 1) * 128], Abuf[d_][:, g * 4 + hh, :], tri[d_], True, False,
                       [t_A[d_], t_const], [tPS[bank]])
                    mm(PS[bank][:, hh * 128:(hh + 1) * 128], ident_bf, neg_bf[d_], False, True,
                       [t_const], [tPS[bank]])
                act(Eb[e_i], psf(bank), AF.Exp, [tPS[bank]], [t_E[e_i]])
                tt(Mb2[pi][e_i].rearrange("p (a b) -> p a b", a=4), Eb[e_i].rearrange("p (a b) -> p a b", a=4),
                   cbv[:, g:g + 1, :].broadcast_to([128, 4, 128]), ALU.mult, [t_E[e_i], tPS[cbk]], [t_M2[pi][e_i]])

    yn_info = {}

    def ssd_s2(c):
        cs = slice(c * 128, (c + 1) * 128)
        pi = c % 2
        mm(psf(4), ident_bf, xDb[pi], True, False, [t_const, t_xD[pi]], [tPS[4]])
        for h in range(8):
            g, hh = h // 4, h % 4
            mm(PS[4][:, h * 64:(h + 1) * 64], Mb2[pi][g][:, hh * 128:(hh + 1) * 128], xdt[0][pi][:, h * 64:(h + 1) * 64],
               False, False, [t_M2[pi][g], t_xdt[0][pi]], [tPS[4]])
            mm(PS[4][:, h * 64:(h + 1) * 64], Mb2[pi][2 + g][:, hh * 128:(hh + 1) * 128], xdt[1][pi][:, h * 64:(h + 1) * 64],
               False, h == 7, [t_M2[pi][2 + g], t_xdt[1][pi]], [tPS[4]])
        for d_ in range(2):
            for g in range(2):
                mm(PS[5 + d_][:, g * 256:(g + 1) * 256], xbc[:, 6 + g, cs], Hin[:, c, d_, g * 256:(g + 1) * 256],
                   True, True, [t_xbc[6 + g], t_Hin[c][d_]], [tPS[5 + d_]])
        tt(v8(ytmp[0]), v8(psf(5)), bc8(ecum[:, c, 0:8]), ALU.mult, [tPS[5], t_ecum], [t_yt[0]])
        tt(v8(ytmp[1]), v8(psf(6)), bc8(ecum[:, c, 8:16]), ALU.mult, [tPS[6], t_ecum], [t_yt[1]])
        tt(ytmp[0], ytmp[0], ytmp[1], ALU.add, [t_yt[0], t_yt[1]], [t_yt[0]])
        tt(ytmp[0], ytmp[0], psf(4), ALU.add, [t_yt[0], tPS[4]], [t_yt[0]])
        tt(ytmp[3], ytmp[0], zs[:, c, :], ALU.mult, [t_yt[0], t_zs[c]], [t_yt[3]])
        k_ = stat_ctr[0] % 32
        stat_ctr[0] += 1
        ss, rs, tst = stat[:, 2 * k_:2 * k_ + 1], stat[:, 2 * k_ + 1:2 * k_ + 2], t_stat[k_]
        act(junk[:, 0:512], ytmp[3], AF.Square, [t_yt[3]], [tst], accum_out=ss)
        act(rs, ss, AF.Ln, [tst], [tst], scale=1.0 / 512, bias=EPS)
        act(rs, rs, AF.Exp, [tst], [tst], scale=-0.5)
        yn_info[c] = (rs, tst)

    def ssd_s3(c):
        cs = slice(c * 128, (c + 1) * 128)
        pi = c % 2
        rs, tst = yn_info[c]
        stt(ynb2[pi], ytmp[3], rs, brow[:, BR_SNW:BR_SNW + 512], ALU.mult, ALU.mult, [t_yt[3], tst, t_brow], [t_yn2[pi]])
        pb = psb(1)
        for q in range(4):
            tr(pb[:, q * 128:(q + 1) * 128], ynb2[pi][:, q * 128:(q + 1) * 128], ident_bf, [t_yn2[pi], t_const], [tPS[1]])
        cp("act", mixT[:, 4:8, cs], pb[:, 0:512].rearrange("p (a b) -> p a b", a=4), [tPS[1]],
           [t_mix[4 + q][c] for q in range(4)])

    for it in range(16 + 2):
        if it < 16:
            ssd_s1(it)
        if 0 <= it - 2 < 16:
            ssd_s3(it - 2)
        if 0 <= it - 1 < 16:
            ssd_s2(it - 1)
    P.barrier()
    if stage == 2:
        return finish_dbg(mixT[:, 4:8, :].rearrange("p a b -> p (a b)"), 4 * S_LEN,
                          [t_mix[k][c] for k in range(4, 8) for c in range(16)])

    w_outb = M.view(146 * KB, [128, 8, D], BF16)
    t_wout = T("wout")
    dma("pool", w_outb, w_out_d.rearrange("(kc p) n -> p kc n", p=128), (), [t_wout])
    x1 = M.view(24 * KB, [128, NTT, D], F32)
    t_x1 = [T(f"x1_{i}") for i in range(NTT)]
    for tt_ in range(NTT):
        dma("sp", x1[:, tt_, :], x_v[tt_], (), [t_x1[tt_]])
    tmpw = [M.view(162 * KB + i * 2 * KB, [128, 512], F32) for i in range(2)]
    t_tmpw = [T(f"tmpw{i}") for i in range(2)]
    blk = 0
    for tt_ in range(NTT):
        for ch in range(2):
            bank = blk % 4
            i = blk % 2
            blk += 1
            for kc in range(8):
                mm(psf(bank), mixT[:, kc, tt_ * 128:(tt_ + 1) * 128], w_outb[:, kc, ch * 512:(ch + 1) * 512],
                   kc == 0, kc == 7, [t_mix[kc][tt_], t_wout], [tPS[bank]])
            tt(tmpw[i], psf(bank), g1b[:, ch * 512:(ch + 1) * 512], ALU.mult, [tPS[bank], t_g1b], [t_tmpw[i]])
            tt(x1[:, tt_, ch * 512:(ch + 1) * 512], x1[:, tt_, ch * 512:(ch + 1) * 512], tmpw[i], ALU.add,
               [t_x1[tt_], t_tmpw[i]], [t_x1[tt_]])
    adab1 = [M.view(128 * KB + i * 8 * KB, [128, 8, 512], BF16) for i in range(2)]
    t_adab1 = [T(f"adabx{i}") for i in range(2)]
    mod_pieces([6, 7, 8, 9, 10, 11], adab1, t_adab1, 166 * KB, 6, 7)
    t_a2 = T("a2")
    stt(a2, modT[:, 32:40, 0], 1.0, pvec[:, PV_NW2:PV_NW2 + 8], ALU.add, ALU.mult, [t_modT, t_pvec], [t_a, t_a2])
    if stage == 3:
        return finish_dbg(x1[:, 0:4, :].rearrange("p a b -> p (a b)"), 4096, [t_x1[i] for i in range(4)], direct=True)

    h2T = M.view(88 * KB, [128, 8, S_LEN], BF16)
    t_h2T = [[T(f"h2T{k}_{tb}") for tb in range(4)] for k in range(8)]
    xn2 = [M.view(120 * KB + i * 2 * KB, [128, 1024], BF16) for i in range(4)]
    t_xn2 = [T(f"xn2{i}") for i in range(4)]
    for tb in range(4):
        norm_group([x1[:, tb * 4 + i, :] for i in range(4)], [t_x1[tb * 4 + i] for i in range(4)], 4, xn2, t_xn2, a2, sh2,
                   (lambda tb: (lambda k: h2T[:, k, tb * 512:(tb + 1) * 512]))(tb),
                   [t_h2T[k][tb] for k in range(8)], t_a, 4)
    P.barrier()

    uT = M.view(120 * KB, [128, 8, S_LEN], BF16)
    t_uT = [T(f"uT{i}") for i in range(8)]
    wdn = M.view(152 * KB, [128, 8, D], BF16)
    t_wdn = T("wdn")
    wup = [M.view(168 * KB + i * 4 * KB, [128, 8, 256], BF16) for i in range(2)]
    t_wup = [T(f"wup{i}") for i in range(2)]
    gpad = [M.view(176 * KB + i * 8208, [128, 2052], F32) for i in range(2)]
    t_gpad = [T(f"gpad{i}") for i in range(2)]
    valb = M.view(197120, [128, S_LEN], BF16)
    t_val = T("val")
    facc = M.view(197120 + 4096, [128, S_LEN], F32)
    t_facc = T("facc")
    tmpd = [g1b[:, 0:512], g1b[:, 512:1024]]
    t_tmpd = [T("tmpd0"), T("tmpd1")]
    for i in range(2):
        memset(gpad[i][:, 0:1], 0.0, [t_gpad[i]])
        memset(gpad[i][:, 2049:2050], 0.0, [t_gpad[i]])
    w_down_v = w_down_d.rearrange("(f p) n -> p f n", p=128)
    out_v = out_d.rearrange("(t p) d -> t p d", p=128)
    out_ops = []
    fin_pending = []
    dblk = 0
    for gi, (f0, f1) in enumerate(FF_GROUPS):
        ng = f1 - f0
        for f in range(f0, f1):
            sl = f - f0
            wi = f % 2
            dma("pool", wup[wi].rearrange("p a b -> p (a b)"), w_up_d[f], (), [t_wup[wi]])
            if sl == 1:
                dma("pool", wdn[:, 0:ng, :], w_down_v[:, f0:f1, :], (), [t_wdn])
            if sl >= 2:
                for s2_ in ([0, 1, 2] if sl == 2 else [sl]):
                    tt(wdn[:, s2_, :], wdn[:, s2_, :], g2b, ALU.mult, [t_wdn, t_g2b], [t_wdn])
            for tb in range(4):
                bg, bv = tb % 2, 2 + tb % 2
                for kc in range(8):
                    mm(psf(bg), wup[wi][:, kc, 0:128], h2T[:, kc, tb * 512:(tb + 1) * 512], kc == 0, kc == 7,
                       [t_wup[wi], t_h2T[kc][tb]], [tPS[bg]])
                cp("act", gpad[wi][:, 1 + tb * 512:1 + (tb + 1) * 512], psf(bg), [tPS[bg]], [t_gpad[wi]])
                for kc in range(8):
                    mm(psf(bv), wup[wi][:, kc, 128:256], h2T[:, kc, tb * 512:(tb + 1) * 512], kc == 0, kc == 7,
                       [t_wup[wi], t_h2T[kc][tb]], [tPS[bv]])
                cp("act", valb[:, tb * 512:(tb + 1) * 512], psf(bv), [tPS[bv]], [t_val])
            w0 = pvec[:, PV_FCW + f * 3 + 0:PV_FCW + f * 3 + 1]
            w1 = pvec[:, PV_FCW + f * 3 + 1:PV_FCW + f * 3 + 2]
            w2 = pvec[:, PV_FCW + f * 3 + 2:PV_FCW + f * 3 + 3]
            bb = pvec[:, PV_FCB + f:PV_FCB + f + 1]
            ts(facc, gpad[wi][:, 1:1 + S_LEN], w1, ALU.mult, [t_gpad[wi], t_pvec], [t_facc], s2=bb, op1=ALU.add)
            stt(facc, gpad[wi][:, 0:S_LEN], w0, facc, ALU.mult, ALU.add, [t_gpad[wi], t_pvec, t_facc], [t_facc])
            stt(facc, gpad[wi][:, 2:2 + S_LEN], w2, facc, ALU.mult, ALU.add, [t_gpad[wi], t_pvec, t_facc], [t_facc])
            act(uT[:, sl, :], facc, AF.Silu, [t_facc], [t_uT[sl]])
            tt(uT[:, sl, :], uT[:, sl, :], valb, ALU.mult, [t_uT[sl], t_val], [t_uT[sl]])
        last = gi == len(FF_GROUPS) - 1
        for tt_ in range(NTT):
            for ch in range(2):
                bank = 4 + dblk % 4
                i = dblk % 2
                dblk += 1
                for sl in range(ng):
                    mm(psf(bank), uT[:, sl, tt_ * 128:(tt_ + 1) * 128], wdn[:, sl, ch * 512:(ch + 1) * 512],
                       sl == 0, sl == ng - 1, [t_uT[sl], t_wdn], [tPS[bank]])
                tt(x1[:, tt_, ch * 512:(ch + 1) * 512], x1[:, tt_, ch * 512:(ch + 1) * 512], psf(bank), ALU.add,
                   [t_x1[tt_], tPS[bank]], [t_x1[tt_]])
            if last:
                k_ = stat_ctr[0] % 32
                stat_ctr[0] += 1
                ss, rs, tst = stat[:, 2 * k_:2 * k_ + 1], stat[:, 2 * k_ + 1:2 * k_ + 2], t_stat[k_]
                act(junk, x1[:, tt_, :], AF.Square, [t_x1[tt_]], [tst], accum_out=ss)
                act(rs, ss, AF.Ln, [tst], [tst], scale=1.0 / D, bias=EPS)
                act(rs, rs, AF.Exp, [tst], [tst], scale=-0.5)
                fin_pending.append((tt_, rs, tst))
                while len(fin_pending) > (1 if tt_ < NTT - 1 else 0):
                    t2_, rs2, tst2 = fin_pending.pop(0)
                    stt(x1[:, t2_, :], x1[:, t2_, :], rs2, brow[:, BR_FNW:BR_FNW + D], ALU.mult, ALU.mult,
                        [t_x1[t2_], tst2, t_brow], [t_x1[t2_]])
                    out_ops.append(dma("sp", out_v[t2_], x1[:, t2_, :], [t_x1[t2_]], []))
    P.emit(final_wait_ops=out_ops)
    return nc


def _const_tables():
    a = np.arange(128)
    ident = (a[:, None] == a[None, :])
    swap = np.where((a % 64) < 32, a + 32, a - 32)
    rperm = (a[:, None] == swap[None, :])
    LE = a[:, None] <= a[None, :]
    GE = a[:, None] >= a[None, :]
    GT = a[:, None] > a[None, :]
    LT = a[:, None] < a[None, :]
    ones = np.ones((128, 128), bool)
    sel0 = np.broadcast_to((a == 0)[:, None], (128, 128))
    sel64 = np.broadcast_to((a == 64)[:, None], (128, 128))
    negf = np.where(a[None, :] < a[:, None], -30000.0, 0.0)
    negb = np.where(a[None, :] > a[:, None], -30000.0, 0.0)
    cmat = np.concatenate([m.astype(np.float32) for m in (ident, rperm, LE, GE, GT, LT, ones, sel0, sel64, negf, negb)], axis=1)
    t = np.arange(S_LEN)
    row_pos = (t // 64).astype(np.float32)
    col_pos = (t % 64).astype(np.float32)
    freqs = (np.float32(10000.0) ** (-np.arange(16, dtype=np.float32) / np.float32(16))).astype(np.float32)
    ang = np.concatenate([row_pos[:, None] * freqs, col_pos[:, None] * freqs], axis=-1).astype(np.float32)
    cos = np.cos(ang).astype(np.float32).T
    sin = np.sin(ang).astype(np.float32).T
    idx = a % 32
    sign = np.where((a % 64) < 32, -1.0, 1.0).astype(np.float32)
    rope = np.concatenate([cos[idx], sin[idx] * sign[:, None]], axis=1).astype(np.float32)
    p = np.arange(128)
    ip, cp_ = p // 64, p % 64
    e = np.arange(14)
    w = np.arange(64)
    dr = 13 - e[None, :, None] + ip[:, None, None] + 0 * w[None, None, :]
    dc = cp_[:, None, None] - w[None, None, :] + 15 + 0 * e[None, :, None]
    cs = np.clip(w - 8, 0, 48)
    colv = (cp_[:, None, None] >= cs[None, None, :]) & (cp_[:, None, None] < cs[None, None, :] + 16)
    colv = colv & (dr >= -99)
    drv = (dr >= 0) & (dr <= 14)
    m_int = colv & (dr >= 3) & (dr <= 10)
    m_edge = colv & drv
    mtab = np.stack([m_int, m_edge], axis=1).astype(np.float32).reshape(128, 2 * 14 * 64)
    dr_c = np.clip(dr, 0, 14)
    dc_c = np.clip(dc, 0, 30)
    return cmat, rope, mtab, dr_c, dc_c


def _prep_shared(inp):
    cmat, rope, mtab, dr_c, dc_c = _const_tables()
    rpb = inp["rpb"][0]
    gtab = rpb[:, dr_c, dc_c]
    gtab = np.ascontiguousarray(gtab.transpose(1, 0, 2, 3)).reshape(128, 8 * 14 * 64).astype(np.float32)

    def pl(v, n):
        return np.ascontiguousarray(v.reshape(n, 128).T)

    pvec = np.zeros((128, PV_N), np.float32)
    pvec[:, PV_NW1:PV_NW1 + 8] = pl(inp["norm1_w"][0], 8)
    pvec[:, PV_NW2:PV_NW2 + 8] = pl(inp["norm2_w"][0], 8)
    scw = inp["ssd_conv_w"][0]
    pvec[:, PV_SCW:PV_SCW + 24] = np.stack([pl(scw[k], 8) for k in range(3)], axis=-1).reshape(128, 24)
    pvec[:, PV_SCB:PV_SCB + 8] = pl(inp["ssd_conv_b"][0], 8)
    fcw = inp["ffn_conv_w"][0]
    pvec[:, PV_FCW:PV_FCW + 66] = np.stack([pl(fcw[k], 22) for k in range(3)], axis=-1).reshape(128, 66)
    pvec[:, PV_FCB:PV_FCB + 22] = pl(inp["ffn_conv_b"][0], 22)
    brow = np.zeros((BR_N,), np.float32)
    brow[BR_FNW:BR_FNW + 1024] = inp["final_norm_w"]
    brow[BR_SNW:BR_SNW + 512] = inp["ssd_norm_w"][0]
    brow[BR_DTB:BR_DTB + 16] = inp["dt_bias"][0].reshape(16)
    brow[BR_ALOG:BR_ALOG + 16] = inp["a_log"][0].reshape(16)
    brow[BR_D:BR_D + 8] = inp["ssd_d"][0]
    brow = np.ascontiguousarray(np.broadcast_to(brow[None, :], (128, BR_N)))
    w_up = inp["ffn_w_up"][0]
    gate = w_up[:, :D_FF].reshape(8, 128, NFF, 128)
    val = w_up[:, D_FF:].reshape(8, 128, NFF, 128)
    w_up_l = np.stack([gate, val], axis=3)
    w_up_l = np.ascontiguousarray(w_up_l.transpose(2, 1, 0, 3, 4)).reshape(NFF, 128, 2048)
    shared = {
        "ada_w": np.ascontiguousarray(inp["ada_w"][0]),
        "ada_b2": np.ascontiguousarray(np.broadcast_to(inp["ada_b"][0][None, :], (2, 6 * D))),
        "w_in": np.ascontiguousarray(inp["w_in"][0]),
        "w_out": np.ascontiguousarray(inp["w_out"][0]),
        "w_up": w_up_l,
        "w_down": np.ascontiguousarray(inp["ffn_w_down"][0]),
        "pvec": pvec, "brow": brow, "cmat": cmat, "rope": rope, "gtab": gtab, "mtab": mtab,
    }
    return shared


def _in_maps(inp, cores):
    inp = {k: np.asarray(v, dtype=np.float32) for k, v in inp.items()}
    shared = _prep_shared(inp)
    maps = []
    for b in cores:
        cc = np.stack([inp["c"][b], inp["c_ctx"]], axis=-1)
        ccT = np.ascontiguousarray(cc.reshape(8, 128, 2).transpose(1, 0, 2)).reshape(128, 16)
        m = dict(shared)
        m["x"] = np.ascontiguousarray(inp["x"][b])
        m["ctx"] = np.ascontiguousarray(inp["ctx"][b])
        m["ccT"] = ccT
        maps.append(m)
    return maps


_NC_CACHE = {}


def kernel(**inputs):
    if "nc" not in _NC_CACHE:
        _NC_CACHE["nc"] = build()
    nc = _NC_CACHE["nc"]
    maps = _in_maps(inputs, list(range(8)))
    res = run_bass_kernel_spmd(nc, maps, core_ids=list(range(8)))
    return np.stack([np.asarray(r["out"], dtype=np.float32) for r in res.results], axis=0)
```

```python
import numpy as np
import concourse.bass as bass
import concourse.mybir as mybir

F32 = mybir.dt.float32
BF16 = mybir.dt.bfloat16
AF = mybir.ActivationFunctionType
ALU = mybir.AluOpType

ENGS = ("pe", "act", "dve", "pool", "sp")
SEM_LIMIT = 30000
N_DMA_LANES = 24


class T:
    __slots__ = ("name", "last_writer", "readers", "excl")

    def __init__(self, name="", excl=False):
        self.name = name
        self.last_writer = None
        self.readers = []
        self.excl = excl


class Op:
    __slots__ = ("eng", "fn", "idx", "deps", "signals", "sig", "is_dma", "lane", "lane_val",
                 "lane_prev", "name")

    def __init__(self, eng, fn, idx, is_dma, name):
        self.eng = eng
        self.fn = fn
        self.idx = idx
        self.deps = set()
        self.signals = False
        self.sig = None
        self.is_dma = is_dma
        self.lane = None
        self.lane_val = None
        self.lane_prev = None
        self.name = name


class Prog:
    def __init__(self, nc):
        self.nc = nc
        self.eng_ops = {e: [] for e in ENGS}
        self.n_dma = {"sp": 0, "pool": 0, "act": 0, "pe": 0, "dve": 0}
        self.lane_last = [None] * N_DMA_LANES
        self.lane_cnt = [0] * N_DMA_LANES

    def op(self, eng, fn, reads=(), writes=(), dma=False, name=None):
        lst = self.eng_ops[eng]
        o = Op(eng, fn, len(lst), dma, name)
        lst.append(o)
        deps = o.deps
        for t in reads:
            if t.last_writer is not None:
                deps.add(t.last_writer)
            if t.excl:
                for r in t.readers:
                    if r.eng != eng:
                        deps.add(r)
        for t in writes:
            if t.last_writer is not None:
                deps.add(t.last_writer)
            for r in t.readers:
                deps.add(r)
        for t in reads:
            t.readers.append(o)
        for t in writes:
            t.readers = []
            t.last_writer = o
        deps.discard(o)
        if dma:
            half = N_DMA_LANES // 2
            lane = (self.n_dma[eng] % half) + (half if eng == "pool" else 0)
            self.n_dma[eng] += 1
            o.lane = lane
            o.lane_prev = self.lane_last[lane]
            self.lane_cnt[lane] += 1
            o.lane_val = 16 * self.lane_cnt[lane]
            self.lane_last[lane] = o
        return o

    def dma(self, eng, out, in_, reads=(), writes=(), **kw):
        return self.op(eng, lambda e: e.dma_start(out=out, in_=in_, **kw), reads, writes, dma=True)

    def emit(self, final_wait_ops=()):
        nc = self.nc
        for o in final_wait_ops:
            if not o.is_dma:
                o.signals = True
        for e in ENGS:
            for o in self.eng_ops[e]:
                for d in o.deps:
                    if d.is_dma:
                        continue
                    if d.eng == "pe" and o.eng == "pe":
                        continue
                    d.signals = True
        n_sig = {}
        for e in ENGS:
            c = 0
            for o in self.eng_ops[e]:
                if o.signals and not o.is_dma:
                    c += 1
                    o.sig = c
            n_sig[e] = c
        sems = {}
        for e in ENGS:
            n = (n_sig[e] + SEM_LIMIT - 1) // SEM_LIMIT
            sems[e] = [nc.alloc_semaphore(f"s_{e}_{i}") for i in range(max(n, 1))]
        lane_sems = [nc.alloc_semaphore(f"s_dma_{i}") for i in range(N_DMA_LANES)]
        self._sems = sems
        self._lane_sems = lane_sems
        engobj = {"pe": "tensor", "act": "scalar", "dve": "vector", "pool": "gpsimd", "sp": "sync"}

        def run_engine(ename, eng):
            waited = {}
            ops = self.eng_ops[ename]
            for o in ops:
                need = {}
                for d in o.deps:
                    if d.is_dma:
                        key = ("l", d.lane)
                        val = d.lane_val
                    else:
                        if d.eng == "pe" and ename == "pe":
                            continue
                        key = ("e", d.eng, (d.sig - 1) // SEM_LIMIT)
                        val = (d.sig - 1) % SEM_LIMIT + 1
                    if need.get(key, 0) < val:
                        need[key] = val
                if o.is_dma and o.lane_prev is not None:
                    key = ("l", o.lane)
                    val = o.lane_prev.lane_val
                    if need.get(key, 0) < val:
                        need[key] = val
                for key, val in need.items():
                    if waited.get(key, 0) >= val:
                        continue
                    waited[key] = val
                    if key[0] == "l":
                        eng.wait_ge(lane_sems[key[1]], val)
                    else:
                        eng.wait_ge(sems[key[1]][key[2]], val)
                ins = o.fn(eng)
                if o.is_dma:
                    ins.then_inc(lane_sems[o.lane], 16)
                elif o.signals:
                    ins.then_inc(sems[ename][(o.sig - 1) // SEM_LIMIT], 1)
            if ename == "sp":
                for o in final_wait_ops:
                    if o.is_dma:
                        eng.wait_ge(lane_sems[o.lane], o.lane_val)
                    else:
                        eng.wait_ge(sems[o.eng][(o.sig - 1) // SEM_LIMIT], (o.sig - 1) % SEM_LIMIT + 1)

        with nc.Block() as block:
            @block.tensor
            def _(e):
                run_engine("pe", e)

            @block.scalar
            def _(e):
                run_engine("act", e)

            @block.vector
            def _(e):
                run_engine("dve", e)

            @block.gpsimd
            def _(e):
                run_engine("pool", e)

            @block.sync
            def _(e):
                run_engine("sp", e)


U8 = mybir.dt.uint8
_DT_SIZE = {F32: 4, BF16: 2, U8: 1, mybir.dt.int32: 4, mybir.dt.uint32: 4}


class Mem:
    def __init__(self, nc, nbytes=206 * 1024):
        self.big = nc.alloc_sbuf_tensor("bigmem", [128, nbytes], U8)
        self.nbytes = nbytes

    def view(self, off, shape, dtype, p0=0):
        sz = _DT_SIZE[dtype]
        n = 1
        for d in shape[1:]:
            n *= d
        assert off % 4 == 0 and off + n * sz <= self.nbytes, (off, shape, self.nbytes)
        v = self.big[p0:p0 + shape[0], off:off + n * sz]
        if dtype != U8:
            v = v.bitcast(dtype)
        if len(shape) == 3:
            v = v.rearrange("p (a b) -> p a b", a=shape[1])
        elif len(shape) == 4:
            v = v.rearrange("p (a b c) -> p a b c", a=shape[1], b=shape[2])
        return v
from concourse.bass_utils import run_bass_kernel_spmd


KB = 1024
S_LEN = 2048
D = 1024
NTT = 16
LC = 256
EPS = 1e-6
D_FF = 2816
NFF = 22
FF_GROUPS = [(0, 8), (8, 15), (15, 22)]

PV_NW1, PV_NW2, PV_SCW, PV_SCB, PV_FCW, PV_FCB, PV_N = 0, 8, 16, 40, 48, 114, 136
BR_FNW, BR_SNW, BR_DTB, BR_ALOG, BR_D, BR_N = 0, 1024, 1536, 1552, 1568, 1576


def _add_barrier(P):
    lasts = []
    for e in ENGS:
        ops = P.eng_ops[e]
        if ops:
            lasts.append(ops[-1])
    if not hasattr(P, "bar_idx"):
        P.bar_idx = {e: 0 for e in ENGS}
    dmas = [o for e in ENGS for o in P.eng_ops[e][P.bar_idx[e]:] if o.is_dma]
    P.pending = set(lasts + dmas)
    P.pending_engs = set(ENGS)
    P.bar_idx = {e: len(P.eng_ops[e]) for e in ENGS}


_orig_op = Prog.op


def _op_with_barrier(self, eng, fn, reads=(), writes=(), dma=False, name=None):
    o = _orig_op(self, eng, fn, reads, writes, dma, name)
    pe = getattr(self, "pending_engs", None)
    if pe and eng in pe:
        o.deps |= self.pending
        o.deps.discard(o)
        pe.discard(eng)
    return o


Prog.op = _op_with_barrier
Prog.barrier = _add_barrier


def build(stage=99):
    nc = bass.Bass("TRN2", target_bir_lowering=False)
    P = Prog(nc)
    M = Mem(nc)

    def din(name, shape):
        return nc.dram_tensor(name, list(shape), F32, kind="ExternalInput").ap()

    x_d = din("x", [S_LEN, D])
    ctx_d = din("ctx", [LC, D])
    ccT_d = din("ccT", [128, 16])
    ada_w_d = din("ada_w", [D, 6 * D])
    ada_b2_d = din("ada_b2", [2, 6 * D])
    w_in_d = din("w_in", [D, 3088])
    w_out_d = din("w_out", [D, D])
    w_up_d = din("w_up", [NFF, 128, 2048])
    w_down_d = din("w_down", [D_FF, D])
    pvec_d = din("pvec", [128, PV_N])
    brow_d = din("brow", [128, BR_N])
    cmat_d = din("cmat", [128, 11 * 128])
    rope_d = din("rope", [128, 2 * S_LEN])
    gtab_d = din("gtab", [128, 8 * 14 * 64])
    mtab_d = din("mtab", [128, 2 * 14 * 64])
    out_d = nc.dram_tensor("out", [S_LEN, D], F32, kind="ExternalOutput").ap()
    dbg_d = None
    if stage < 99:
        dbg_d = nc.dram_tensor("dbg", [128, 8 * S_LEN], F32, kind="ExternalOutput").ap()

    PS = [nc.alloc_psum_tensor(f"psb{i}", [128, 512], F32) for i in range(8)]
    tPS = [T(f"ps{i}", excl=True) for i in range(8)]

    def psf(i):
        return PS[i][:, :]

    def psb(i):
        return PS[i][:, :].bitcast(BF16)

    def mm(out, lhsT, rhs, start, stop, reads, writes):
        return P.op("pe", lambda e: e.matmul(out, lhsT=lhsT, rhs=rhs, start=start, stop=stop), reads, writes)

    def tr(out, in_, ident, reads, writes):
        return P.op("pe", lambda e: e.transpose(out, in_, ident), reads, writes)

    def act(out, in_, func, reads, writes, scale=1.0, bias=0.0, accum_out=None):
        if accum_out is None:
            return P.op("act", lambda e: e.activation(out=out, in_=in_, func=func, scale=scale, bias=bias), reads, writes)
        return P.op("act", lambda e: e.activation(out=out, in_=in_, func=func, scale=scale, bias=bias,
                                                  accum_out=accum_out), reads, writes)

    def tt(out, in0, in1, op, reads, writes, eng="dve"):
        return P.op(eng, lambda e: e.tensor_tensor(out=out, in0=in0, in1=in1, op=op), reads, writes)

    def ts(out, in0, s1, op0, reads, writes, s2=None, op1=None, eng="dve"):
        if op1 is None:
            return P.op(eng, lambda e: e.tensor_scalar(out=out, in0=in0, scalar1=s1, scalar2=None, op0=op0), reads, writes)
        return P.op(eng, lambda e: e.tensor_scalar(out=out, in0=in0, scalar1=s1, scalar2=s2, op0=op0, op1=op1), reads, writes)

    def stt(out, in0, scalar, in1, op0, op1, reads, writes):
        return P.op("dve", lambda e: e.scalar_tensor_tensor(out=out, in0=in0, scalar=scalar, in1=in1, op0=op0, op1=op1),
                    reads, writes)

    def cp(eng, out, in_, reads, writes):
        if eng == "act":
            return act(out, in_, AF.Copy, reads, writes)
        return P.op("dve", lambda e: e.tensor_copy(out=out, in_=in_), reads, writes)

    def recip(out, in_, reads, writes):
        return P.op("dve", lambda e: e.reciprocal(out=out, in_=in_), reads, writes)

    def memset(out, val, writes):
        return P.op("dve", lambda e: e.memset(out, val), (), writes)

    def dma(eng, out, in_, reads, writes):
        return P.dma(eng, out, in_, reads, writes)

    ident_bf = M.view(0, [128, 128], BF16)
    rperm_bf = M.view(256, [128, 128], BF16)
    LE_bf = M.view(512, [128, 128], BF16)
    GE_bf = M.view(768, [128, 128], BF16)
    GT_bf = M.view(1024, [128, 128], BF16)
    LT_bf = M.view(1280, [128, 128], BF16)
    LE_f = M.view(1536, [128, 128], F32)
    ones_f = M.view(2048, [128, 128], F32)
    ident_f = M.view(2560, [128, 128], F32)
    pvec = M.view(3072, [128, PV_N], F32)
    brow = M.view(3616, [128, BR_N], F32)
    modT = M.view(9920, [128, 48, 2], F32)
    a1 = M.view(10304, [128, 8], F32)
    a1c = M.view(10336, [128, 8], F32)
    a2 = M.view(10368, [128, 8], F32)
    g1b = M.view(10400, [128, 1024], F32)
    g2b = M.view(14496, [128, 1024], F32)
    scT = M.view(18592, [128, 8, 2], BF16)
    ccs = M.view(18624, [128, 16], F32)
    stat = M.view(18688, [128, 64], F32)
    a_b = M.view(18944, [128, 16], F32)
    junk = M.view(20480, [128, 1024], BF16)
    sel_f = [M.view(22528 + i * 512, [128, 128], F32) for i in range(2)]
    neg_bf = [M.view(23552 + i * 256, [128, 128], BF16) for i in range(2)]
    t_const, t_pvec, t_brow, t_modT, t_a, t_g1b, t_g2b, t_scT, t_ccs, t_ab = [T(n) for n in
        "const pvec brow modT a g1b g2b scT ccs ab".split()]
    t_stat = [T(f"stat{i}") for i in range(64)]

    PH = 24 * KB
    import os as _os2
    _SALT = float(_os2.environ.get("KSALT", "0"))
    if _SALT:
        memset(junk[:, 0:8], _SALT, [])
    if _os2.environ.get("KPOISON", "0") == "1":
        memset(stat, 1.0e30, t_stat)
    if _os2.environ.get("KPOISON", "0") == "2":
        memset(M.view(22528 + 1024, [128, (206 * KB - 22528 - 1024) // 4], F32), 3.0e38, [])
        for i_ in range(8):
            memset(psf(i_), 3.0e38, [tPS[i_]])
        P.barrier()

    def finish_dbg(src_ap, ncols, trk, stage_off=24 * KB, direct=False):
        if direct:
            op = dma("sp", dbg_d[:, 0:ncols], src_ap, trk, [])
        else:
            dst = M.view(stage_off, [128, ncols], F32)
            t_d = T("dbgst")
            cp("dve", dst, src_ap, trk, [t_d])
            op = dma("sp", dbg_d[:, 0:ncols], dst, [t_d], [])
        P.emit(final_wait_ops=[op])
        return nc

    cstage = M.view(PH, [128, 11 * 128], F32)
    t_cst = T("cstage")
    dma("sp", cstage, cmat_d, (), [t_cst])
    dma("sp", pvec, pvec_d, (), [t_pvec])
    dma("sp", brow, brow_d, (), [t_brow])
    dma("sp", ccs, ccT_d, (), [t_ccs])
    for i, dst in enumerate([ident_bf, rperm_bf, LE_bf, GE_bf, GT_bf, LT_bf]):
        cp("dve", dst, cstage[:, i * 128:(i + 1) * 128], [t_cst], [t_const])
    cp("dve", LE_f, cstage[:, 2 * 128:3 * 128], [t_cst], [t_const])
    cp("dve", ones_f, cstage[:, 6 * 128:7 * 128], [t_cst], [t_const])
    cp("dve", ident_f, cstage[:, 0:128], [t_cst], [t_const])
    cp("dve", sel_f[0], cstage[:, 7 * 128:8 * 128], [t_cst], [t_const])
    cp("dve", sel_f[1], cstage[:, 8 * 128:9 * 128], [t_cst], [t_const])
    cp("dve", neg_bf[0], cstage[:, 9 * 128:10 * 128], [t_cst], [t_const])
    cp("dve", neg_bf[1], cstage[:, 10 * 128:11 * 128], [t_cst], [t_const])
    act(scT.rearrange("p a b -> p (a b)"), ccs, AF.Silu, [t_ccs], [t_scT])

    xs_all = [M.view(174 * KB + i * 4 * KB, [128, 1024], F32) for i in range(8)]
    t_xs_all = [T(f"xs{i}") for i in range(8)]
    x_v = x_d.rearrange("(t p) d -> t p d", p=128)
    ctx_v = ctx_d.rearrange("(t p) d -> t p d", p=128)
    for i in range(2):
        dma("sp", xs_all[i], ctx_v[i], (), [t_xs_all[i]])
    for t_ in range(6):
        dma("sp", xs_all[2 + t_], x_v[t_], (), [t_xs_all[2 + t_]])

    ada_w_v = ada_w_d.rearrange("(kc p) n -> p kc n", p=128)

    def mod_pieces(pcs, adab, t_adab, small_off, acc_bank, aux_bank):
        nb = len(adab)
        ab2p = [M.view(small_off + i * 2 * KB, [2, 512], F32) for i in range(2)]
        mrow = [M.view(small_off + 4 * KB + i * 2 * KB, [2, 512], F32) for i in range(2)]
        t_ab2p = [T(f"ab2p{i}") for i in range(2)]
        t_mrow = [T(f"mrow{i}") for i in range(2)]
        for n_, pc in enumerate(pcs[:nb]):
            dma("pool", adab[n_], ada_w_v[:, :, pc * 512:(pc + 1) * 512], (), [t_adab[n_]])
        for n_, pc in enumerate(pcs):
            b_, i_ = n_ % nb, n_ % 2
            dma("sp", ab2p[i_], ada_b2_d[:, pc * 512:(pc + 1) * 512], (), [t_ab2p[i_]])
            for kc in range(8):
                mm(PS[acc_bank][0:2, :], scT[:, kc, :], adab[b_][:, kc, :], kc == 0, kc == 7,
                   [t_scT, t_adab[b_]], [tPS[acc_bank]])
            if n_ + nb < len(pcs):
                pn = pcs[n_ + nb]
                dma("pool", adab[b_], ada_w_v[:, :, pn * 512:(pn + 1) * 512], (), [t_adab[b_]])
            tt(mrow[i_], PS[acc_bank][0:2, :], ab2p[i_], ALU.add, [tPS[acc_bank], t_ab2p[i_]], [t_mrow[i_]])
            pv = PS[aux_bank][:, 0:8].rearrange("p (a b) -> p a b", b=2)
            for jj in range(4):
                mm(pv[:, jj, :], mrow[i_][0:2, jj * 128:(jj + 1) * 128], ident_f[0:2, 0:2], True, True,
                   [t_mrow[i_], t_const], [tPS[aux_bank]])
            cp("dve", modT[:, pc * 4:(pc + 1) * 4, :], pv, [tPS[aux_bank]], [t_modT])
            if pc in (4, 5, 10, 11):
                dst, tdst = (g1b, t_g1b) if pc < 6 else (g2b, t_g2b)
                mm(psf(aux_bank), ones_f[0:1, 0:128], mrow[i_][0:1, :], True, True, [t_mrow[i_], t_const], [tPS[aux_bank]])
                cp("act", dst[:, (pc % 2) * 512:(pc % 2 + 1) * 512], psf(aux_bank), [tPS[aux_bank]], [tdst])

    adab0 = [M.view(PH + 6 * KB + i * 8 * KB, [128, 8, 512], BF16) for i in range(3)]
    t_adab0 = [T(f"adab{i}") for i in range(3)]
    mod_pieces([0, 1, 2, 3, 4, 5], adab0, t_adab0, PH + 30 * KB, 0, 2)
    stt(a1, modT[:, 8:16, 0], 1.0, pvec[:, PV_NW1:PV_NW1 + 8], ALU.add, ALU.mult, [t_modT, t_pvec], [t_a])
    stt(a1c, modT[:, 8:16, 1], 1.0, pvec[:, PV_NW1:PV_NW1 + 8], ALU.add, ALU.mult, [t_modT, t_pvec], [t_a])
    act(a_b, brow[:, BR_ALOG:BR_ALOG + 16], AF.Exp, [t_brow], [t_ab])
    ts(a_b, a_b, -1.0, ALU.mult, [t_ab], [t_ab])
    P.barrier()
    if stage == 0:
        return finish_dbg(g1b, 1024, [t_g1b], direct=True)

    stat_ctr = [0]

    def norm_group(src_tiles, src_trk, n, xn_bufs, t_xn, a_ap, sh_ap, dst_fn, dst_trk, t_par, bank0):
        for i in range(n):
            c = stat_ctr[0] % 32
            stat_ctr[0] += 1
            ss = stat[:, 2 * c:2 * c + 1]
            rs = stat[:, 2 * c + 1:2 * c + 2]
            tst = t_stat[c]
            act(junk, src_tiles[i], AF.Square, [src_trk[i]], [tst], accum_out=ss)
            act(rs, ss, AF.Ln, [tst], [tst], scale=1.0 / D, bias=EPS)
            act(rs, rs, AF.Exp, [tst], [tst], scale=-0.5)
            ts(xn_bufs[i], src_tiles[i], rs, ALU.mult, [src_trk[i], tst], [t_xn[i]])
        for k in range(8):
            bank = bank0 + (k % 2)
            pb = psb(bank)
            for i in range(n):
                tr(pb[:, i * 128:(i + 1) * 128], xn_bufs[i][:, k * 128:(k + 1) * 128], ident_bf,
                   [t_xn[i], t_const], [tPS[bank]])
            if k % 2 == 0:
                act(dst_fn(k), pb[:, 0:n * 128], AF.Identity, [tPS[bank], t_par, t_modT], [dst_trk[k]],
                    scale=a_ap[:, k:k + 1], bias=sh_ap[:, k:k + 1])
            else:
                ts(dst_fn(k), pb[:, 0:n * 128], a_ap[:, k:k + 1], ALU.mult, [tPS[bank], t_par, t_modT], [dst_trk[k]],
                   s2=sh_ap[:, k:k + 1], op1=ALU.add)

    hT = M.view(PH, [128, 8, S_LEN], BF16)
    t_hT = [[T(f"hT{k}_{tb}") for tb in range(4)] for k in range(8)]
    mixT = M.view(174 * KB, [128, 8, S_LEN], BF16)
    t_mix = [[T(f"mix{k}_{c}") for c in range(16)] for k in range(8)]
    hcT = M.view(112 * KB, [128, 8, LC], BF16)
    t_hcT = [T(f"hcT{k}") for k in range(8)]
    expB = [M.view(146 * KB + i * 14336, [128, 8, 14, 64], BF16) for i in range(2)]
    t_expB = T("expB")

    xn_b = [M.view(56 * KB + i * 2 * KB, [128, 1024], BF16) for i in range(8)]
    t_xn = [T(f"xn{i}") for i in range(8)]
    gst = M.view(72 * KB, [128, 8, 14, 64], F32)
    mst = M.view(100 * KB, [128, 2, 14 * 64], F32)
    t_gst, t_mst = T("gst"), T("mst")
    wb = [M.view(121 * KB + i * 8 * KB, [128, 8, 512], BF16) for i in range(3)]
    t_wb = [T(f"wb{i}") for i in range(3)]
    w_in_v = w_in_d.rearrange("(kc p) n -> p kc n", p=128)
    for g in range(3):
        dma("pool", wb[g], w_in_v[:, :, g * 512:(g + 1) * 512], (), [t_wb[g]])

    cosT = M.view(174 * KB, [128, S_LEN], F32)
    sinT = M.view(182 * KB, [128, S_LEN], F32)
    t_rope = T("rope")
    dma("sp", gst.rearrange("p a b c -> p (a b c)"), gtab_d, (), [t_gst])
    dma("sp", mst.rearrange("p a b -> p (a b)"), mtab_d, (), [t_mst])

    def build_expB():
        act(gst.rearrange("p a b c -> p (a b c)"), gst.rearrange("p a b c -> p (a b c)"), AF.Exp, [t_gst], [t_gst])
        for i in range(2):
            tt(expB[i].rearrange("p a b c -> p a (b c)"), gst.rearrange("p a b c -> p a (b c)"),
               mst[:, i:i + 1, :].broadcast_to([128, 8, 14 * 64]), ALU.mult, [t_gst, t_mst], [t_expB])

    sh1 = modT[:, 0:8, 0]
    sh1c = modT[:, 0:8, 1]
    sh2 = modT[:, 24:32, 0]
    t_par = T("par")
    norm_group(xs_all[0:2], t_xs_all[0:2], 2, xn_b[4:8], t_xn[4:8], a1c, sh1c, lambda k: hcT[:, k, :], t_hcT, t_a, 2)
    for tb in range(4):
        if tb < 3:
            for t_ in range(max(6, 4 * tb + 4), 4 * tb + 8):
                dma("sp", xs_all[(2 + t_) % 8], x_v[t_], (), [t_xs_all[(2 + t_) % 8]])
        bufs = [(2 + 4 * tb + i) % 8 for i in range(4)]
        xo = 4 * (tb % 2)
        norm_group([xs_all[j_] for j_ in bufs], [t_xs_all[j_] for j_ in bufs], 4, xn_b[xo:xo + 4], t_xn[xo:xo + 4], a1, sh1,
                   (lambda tb: (lambda k: hT[:, k, tb * 512:(tb + 1) * 512]))(tb),
                   [t_hT[k][tb] for k in range(8)], t_a, 2 * (tb % 2))
        if tb == 0:
            build_expB()
        if tb == 2:
            dma("sp", sinT, rope_d[:, S_LEN:2 * S_LEN], (), [t_xs_all[2], t_xs_all[3], t_rope])
        if tb == 3:
            dma("sp", cosT, rope_d[:, 0:S_LEN], (), [t_xs_all[0], t_xs_all[1], t_rope])
    P.barrier()
    if stage == 0.5:
        return finish_dbg(hT.rearrange("p a b -> p (a b)"), 16384, [t_hT[k][tb] for k in range(8) for tb in range(4)], stage_off=130 * KB)
    if stage == 0.6:
        return finish_dbg(expB[0].rearrange("p a b c -> p (a b c)"), 8 * 14 * 64, [t_expB], stage_off=56 * KB)

    qT = M.view(56 * KB, [128, 4, S_LEN], BF16)
    kT = M.view(72 * KB, [128, 4, S_LEN], BF16)
    t_qT = [[T(f"qT{hp}_{tb}") for tb in range(4)] for hp in range(4)]
    t_kT = [[T(f"kT{hp}_{tb}") for tb in range(4)] for hp in range(4)]
    v_aug = M.view(88 * KB, [128, NTT, 4, 192], BF16)
    t_v = [T(f"v{tt_}") for tt_ in range(NTT)]
    kcT = M.view(116 * KB, [128, 4, LC], BF16)
    t_kcT = T("kcT")
    vc_aug = M.view(118 * KB, [128, 2, 4, 192], BF16)
    t_vc = T("vc")
    qb = [M.view(190 * KB + i * KB, [128, 512], BF16) for i in range(2)]
    t_qb = [T(f"qb{i}") for i in range(2)]
    rt1 = [M.view(192 * KB + i * 2 * KB, [128, 512], F32) for i in range(2)]
    rt2 = [M.view(196 * KB + i * 2 * KB, [128, 512], F32) for i in range(2)]
    t_rt1 = [T(f"rt1{i}") for i in range(2)]
    t_rt2 = [T(f"rt2{i}") for i in range(2)]

    memset(v_aug[:, :, :, 64:128], 1.0, t_v)
    memset(vc_aug[:, :, :, 64:128], 1.0, [t_vc])

    if stage == 0.65:
        return finish_dbg(cosT, 2048, [t_rope] + t_wb + t_v + [t_vc], direct=True)
    rope_ctr = [0]

    def proj_fm(wbuf, t_w, cb, src, t_src_k, ntok, tok0, bank):
        for kc in range(8):
            mm(PS[bank][:, 0:ntok], wbuf[:, kc, cb * 128:(cb + 1) * 128], src[:, kc, tok0:tok0 + ntok],
               kc == 0, kc == 7, [t_w, t_src_k[kc]], [tPS[bank]])

    import os as _os
    _NOROPE = _os.environ.get("NOROPE", "0")

    def rope_evac(bank, dst, t_dst, tb):
        if _NOROPE == "1":
            cp("act", dst, psf(bank), [tPS[bank]], [t_dst])
            return
        i = rope_ctr[0] % 2
        rope_ctr[0] += 1
        b2 = 4 + i
        act(qb[i], psf(bank), AF.Copy, [tPS[bank]], [t_qb[i]])
        if _NOROPE == "4":
            tt(rt1[i], psf(bank), cosT[:, tb * 512:(tb + 1) * 512], ALU.mult, [tPS[bank], t_rope], [t_rt1[i]])
            cp("dve", dst, rt1[i], [t_rt1[i]], [t_dst])
            return
        if _NOROPE == "5":
            cp("dve", rt1[i], psf(bank), [tPS[bank]], [t_rt1[i]])
            cp("dve", rt2[i], psf(bank), [tPS[bank]], [t_rt2[i]])
            tt(dst, rt1[i], rt2[i], ALU.add, [t_rt1[i], t_rt2[i]], [t_dst])
            return
        if _NOROPE == "3":
            tt(rt1[i], psf(bank), cosT[:, tb * 512:(tb + 1) * 512], ALU.mult, [tPS[bank], t_rope], [t_rt1[i]])
            tt(rt2[i], psf(bank), sinT[:, tb * 512:(tb + 1) * 512], ALU.mult, [tPS[bank], t_rope], [t_rt2[i]])
            tt(dst, rt1[i], rt2[i], ALU.add, [t_rt1[i], t_rt2[i]], [t_dst])
            return
        mm(psf(b2), rperm_bf, qb[i], True, True, [t_qb[i], t_const], [tPS[b2]])
        if _NOROPE == "2":
            cp("dve", dst, psf(b2), [tPS[b2]], [t_dst])
            return
        tt(rt1[i], psf(bank), cosT[:, tb * 512:(tb + 1) * 512], ALU.mult, [tPS[bank], t_rope], [t_rt1[i]])
        tt(rt2[i], psf(b2), sinT[:, tb * 512:(tb + 1) * 512], ALU.mult, [tPS[b2], t_rope], [t_rt2[i]])
        tt(dst, rt1[i], rt2[i], ALU.add, [t_rt1[i], t_rt2[i]], [t_dst])

    blk = 0
    for (g, dstT, t_dst) in ((0, qT, t_qT), (1, kT, t_kT)):
        for hp in range(4):
            for tb in range(4):
                bank = blk % 4
                blk += 1
                proj_fm(wb[g], t_wb[g], hp, hT, [t_hT[k][tb] for k in range(8)], 512, tb * 512, bank)
                rope_evac(bank, dstT[:, hp, tb * 512:(tb + 1) * 512], t_dst[hp][tb], tb)
    if stage == 0.66:
        return finish_dbg(qT.rearrange("p a b -> p (a b)"), 8192, [t_qT[a][b] for a in range(4) for b in range(4)], stage_off=130 * KB)
    for hp in range(4):
        bank = blk % 4
        blk += 1
        proj_fm(wb[1], t_wb[1], hp, hcT, t_hcT, LC, 0, bank)
        cp("act", kcT[:, hp, :], PS[bank][:, 0:LC], [tPS[bank]], [t_kcT])

    def v_evac(bank, dst4, t_dst):
        src = psf(bank).rearrange("p (a b c) -> p a b c", a=4, b=2)
        cp("act", dst4[:, :, 0:64], src[:, :, 0, :], [tPS[bank]], [t_dst])
        cp("dve", dst4[:, :, 128:192], src[:, :, 1, :], [tPS[bank]], [t_dst])

    for tt_ in range(NTT):
        bank = blk % 4
        blk += 1
        for kc in range(8):
            mm(psf(bank), hT[:, kc, tt_ * 128:(tt_ + 1) * 128], wb[2][:, kc, :], kc == 0, kc == 7,
               [t_hT[kc][tt_ // 4], t_wb[2]], [tPS[bank]])
        v_evac(bank, v_aug[:, tt_], t_v[tt_])
    for ct in range(2):
        bank = blk % 4
        blk += 1
        for kc in range(8):
            mm(psf(bank), hcT[:, kc, ct * 128:(ct + 1) * 128], wb[2][:, kc, :], kc == 0, kc == 7,
               [t_hcT[kc], t_wb[2]], [tPS[bank]])
        v_evac(bank, vc_aug[:, ct], t_vc)
    P.barrier()
    if stage == 0.7:
        return finish_dbg(qT.rearrange("p a b -> p (a b)"), 8192, [t_qT[a][b] for a in range(4) for b in range(4)], stage_off=130 * KB)
    if stage == 0.8:
        return finish_dbg(v_aug[:, 0:8].rearrange("p a b c -> p (a b c)"), 8 * 768, [t_v[a] for a in range(8)], stage_off=130 * KB)

    Etb = [[M.view(121 * KB + (par * 3 + i) * KB, [128, 512], BF16) for i in range(3)] for par in range(2)]
    Ptb = [[M.view(127 * KB + (par * 4 + i) * KB, [128, 512], BF16) for i in range(4)] for par in range(2)]
    t_Etb = [[T(f"Et{par}{i}") for i in range(3)] for par in range(2)]
    t_Ptb = [[T(f"Pt{par}{i}") for i in range(4)] for par in range(2)]
    rec = [M.view(135 * KB + i * KB, [128, 256], F32) for i in range(4)]
    tmpO = [M.view(139 * KB + i * KB, [128, 256], F32) for i in range(4)]
    lnS = [M.view(143 * KB + i * KB, [128, 256], F32) for i in range(2)]
    t_rec = [T(f"rec{i}") for i in range(4)]
    t_tmpO = [T(f"tmpO{i}") for i in range(4)]
    t_lnS = [T(f"lnS{i}") for i in range(2)]
    for i in range(4):
        memset(rec[i], 0.0, [t_rec[i]])

    groups = []
    for j in range(8):
        if j == 0:
            kts, tab = [0, 1, 2, 3], 1
        elif j == 7:
            kts, tab = [12, 13, 14, 15], 1
        else:
            kts, tab = list(range(2 * j - 2, 2 * j + 4)), 0
        for h in range(8):
            groups.append((j, h, tab, [("loc", kt) for kt in kts] + [("ctx", 0), ("ctx", 1)]))

    def emit_S(gi):
        j, h, tab, lst = groups[gi]
        par = gi % 2
        hp, hb = h // 2, (h % 2) * 64
        qs = qT[hb:hb + 64, hp, j * 256:(j + 1) * 256]
        nb = len(lst) // 2
        for b_ in range(nb):
            for half in range(2):
                kind, kt = lst[2 * b_ + half]
                if kind == "loc":
                    ks, rk = kT[hb:hb + 64, hp, kt * 128:(kt + 1) * 128], t_kT[hp][kt // 4]
                else:
                    ks, rk = kcT[hb:hb + 64, hp, kt * 128:(kt + 1) * 128], t_kcT
                mm(PS[b_][:, half * 256:(half + 1) * 256], ks, qs, True, True, [rk, t_qT[hp][j // 2]], [tPS[b_]])
            if lst[2 * b_][0] == "loc":
                act(Etb[par][b_], psf(b_), AF.Exp, [tPS[b_]], [t_Etb[par][b_]], scale=0.125)
                for half in range(2):
                    kt = lst[2 * b_ + half][1]
                    e0 = 6 - (2 * kt - 4 * j)
                    bview = expB[tab][:, h, e0:e0 + 4, :].rearrange("p a b -> p (a b)")
                    tt(Ptb[par][b_][:, half * 256:(half + 1) * 256], Etb[par][b_][:, half * 256:(half + 1) * 256], bview,
                       ALU.mult, [t_Etb[par][b_], t_expB], [t_Ptb[par][b_]])
            else:
                act(Ptb[par][b_], psf(b_), AF.Exp, [tPS[b_]], [t_Ptb[par][b_]], scale=0.125)

    def emit_PV(gi):
        j, h, tab, lst = groups[gi]
        par = gi % 2
        hp, odd = h // 2, h % 2
        ob = 4 + par
        c0 = 64 if odd else 0
        n = len(lst)
        for i, (kind, kt) in enumerate(lst):
            if kind == "loc":
                lhs, rv = v_aug[:, kt, hp, c0:c0 + 128], t_v[kt]
            else:
                lhs, rv = vc_aug[:, kt, hp, c0:c0 + 128], t_vc
            mm(PS[ob][:, 0:256], lhs, Ptb[par][i // 2][:, (i % 2) * 256:(i % 2 + 1) * 256], i == 0, i == n - 1,
               [rv, t_Ptb[par][i // 2]], [tPS[ob]])
        r = gi % 4
        sr = 0 if odd else 64
        obp = 64 if odd else 0
        act(lnS[par][sr:sr + 1, :], PS[ob][sr:sr + 1, 0:256], AF.Ln, [tPS[ob]], [t_lnS[par]])
        act(rec[r][sr:sr + 1, :], lnS[par][sr:sr + 1, :], AF.Exp, [t_lnS[par]], [t_rec[r]], scale=-1.0)
        cp("dve", tmpO[r][obp:obp + 64, :], PS[ob][obp:obp + 64, 0:256], [tPS[ob]], [t_tmpO[r]])

    def emit_norm(gi):
        j, h, tab, lst = groups[gi]
        par = gi % 2
        hp, odd = h // 2, h % 2
        r = gi % 4
        sr = 0 if odd else 64
        obp = 64 if odd else 0
        mm(PS[6][:, par * 256:(par + 1) * 256], sel_f[1 if sr else 0], rec[r], True, True,
           [t_rec[r], t_const], [tPS[6]])
        tt(mixT[obp:obp + 64, hp, j * 256:(j + 1) * 256], tmpO[r][obp:obp + 64, :],
           PS[6][obp:obp + 64, par * 256:(par + 1) * 256], ALU.mult, [t_tmpO[r], tPS[6]],
           [t_mix[hp][2 * j], t_mix[hp][2 * j + 1]])

    ng_ = len(groups)
    for step in range(ng_ + 2):
        if step < ng_:
            emit_S(step)
        if 0 <= step - 1 < ng_:
            emit_PV(step - 1)
        if 0 <= step - 2 < ng_:
            emit_norm(step - 2)
    P.barrier()


    if stage == 1:
        return finish_dbg(mixT[:, 0:4, :].rearrange("p a b -> p (a b)"), 4 * S_LEN,
                          [t_mix[k][c] for k in range(4) for c in range(16)])

    zs = M.view(56 * KB, [128, NTT, 512], BF16)
    t_zs = [T(f"zs{i}") for i in range(NTT)]
    xbc = M.view(72 * KB, [128, 8, S_LEN], BF16)
    t_xbc = [T(f"xbc{i}") for i in range(8)]
    ctxx = M.view(104 * KB, [128, 6, LC], BF16)
    t_ctxx = [T(f"ctxx{i}") for i in range(6)]
    TB0 = 107 * KB
    dt_all, dta, CFs, TOTs, D1, ecum, wend, cdt, tmpA = [M.view(TB0 + i * 1152, [128, 18, 16], F32) for i in range(9)]
    t_dt, t_dta, t_cf, t_tot, t_d1, t_ecum, t_wend, t_cd, t_tmpA = [T(n) for n in
        "dt dta cf tot d1 ecum wend cd tmpA".split()]
    wdt = M.view(145 * KB, [128, 8, 16], BF16)
    t_wdt = T("wdt")
    pad = [M.view(146 * KB + i * 4104, [128, 2052], BF16) for i in range(2)]
    t_pad = [T(f"pad{i}") for i in range(2)]
    cpad = M.view(155 * KB, [128, 260], BF16)
    t_cpad = T("cpad")
    diag = M.view(156 * KB, [128, 24, 128], BF16)
    t_diag = T("diag")
    for i in range(24):
        ts(diag[:, i, :], ident_bf, pvec[:, PV_SCW + i:PV_SCW + i + 1], ALU.mult, [t_const, t_pvec], [t_diag])

    for g in range(3):
        dma("pool", wb[g], w_in_v[:, :, 1536 + g * 512:1536 + (g + 1) * 512], (), [t_wb[g]])
    dma("pool", wdt, w_in_v[:, :, 3072:3088], (), [t_wdt])
    for i in range(2):
        memset(pad[i][:, 0:1], 0.0, [t_pad[i]])
        memset(pad[i][:, 2049:2050], 0.0, [t_pad[i]])
    memset(cpad[:, 0:1], 0.0, [t_cpad])
    memset(cpad[:, 257:258], 0.0, [t_cpad])

    blk = 0
    for tt_ in range(NTT):
        bank = blk % 4
        blk += 1
        for kc in range(8):
            mm(psf(bank), hT[:, kc, tt_ * 128:(tt_ + 1) * 128], wb[0][:, kc, :], kc == 0, kc == 7,
               [t_hT[kc][tt_ // 4], t_wb[0]], [tPS[bank]])
        act(zs[:, tt_, :], psf(bank), AF.Silu, [tPS[bank]], [t_zs[tt_]])
    for c in range(18):
        for kc in range(8):
            if c < 16:
                lhs, rd = hT[:, kc, c * 128:(c + 1) * 128], t_hT[kc][c // 4]
            else:
                lhs, rd = hcT[:, kc, (c - 16) * 128:(c - 15) * 128], t_hcT[kc]
            mm(PS[7][:, c * 16:(c + 1) * 16], lhs, wdt[:, kc, :], kc == 0, kc == 7, [rd, t_wdt], [tPS[7]])
    ps7v = PS[7][:, 0:288].rearrange("p (a b) -> p a b", b=16)
    tt(dt_all, ps7v, brow[:, BR_DTB:BR_DTB + 16].unsqueeze(1).broadcast_to([128, 18, 16]), ALU.add,
       [tPS[7], t_brow], [t_dt])
    dt_flat = dt_all.rearrange("p a b -> p (a b)")
    act(dt_flat, dt_flat, AF.Exp, [t_dt], [t_dt])
    act(dt_flat, dt_flat, AF.Ln, [t_dt], [t_dt], bias=1.0)

    conv_ctr = [0]

    def conv_silu(padb, t_padb, n, cc, dst, t_dst):
        bb = pvec[:, PV_SCB + cc:PV_SCB + cc + 1]
        nblk = max(1, n // 512)
        w_ = n // nblk
        for tb in range(nblk):
            bank = 4 + conv_ctr[0] % 2
            conv_ctr[0] += 1
            for k in range(3):
                mm(PS[bank][:, 0:w_], diag[:, cc * 3 + k, :], padb[:, tb * w_ + k:tb * w_ + k + w_], k == 0, k == 2,
                   [t_diag, t_padb], [tPS[bank]])
            act(dst[:, tb * w_:(tb + 1) * w_], PS[bank][:, 0:w_], AF.Silu, [tPS[bank], t_pvec], [t_dst], bias=bb)

    for cc in range(8):
        wsel, cb = (1, cc) if cc < 4 else (2, cc - 4)
        pb_ = cc % 2
        for tb in range(4):
            bank = blk % 4
            blk += 1
            proj_fm(wb[wsel], t_wb[wsel], cb, hT, [t_hT[k][tb] for k in range(8)], 512, tb * 512, bank)
            cp("act" if tb % 2 == 0 else "dve", pad[pb_][:, 1 + tb * 512:1 + (tb + 1) * 512], psf(bank),
               [tPS[bank]], [t_pad[pb_]])
        conv_silu(pad[pb_], t_pad[pb_], S_LEN, cc, xbc[:, cc, :], t_xbc[cc])
    for cc in range(6):
        wsel, cb = (1, cc) if cc < 4 else (2, cc - 4)
        bank = blk % 4
        blk += 1
        proj_fm(wb[wsel], t_wb[wsel], cb, hcT, t_hcT, LC, 0, bank)
        cp("act", cpad[:, 1:1 + LC], PS[bank][:, 0:LC], [tPS[bank]], [t_cpad])
        conv_silu(cpad, t_cpad, LC, cc, ctxx[:, cc, :], t_ctxx[cc])
    P.barrier()

    xsB = M.view(121 * KB, [128, 18, 768], BF16)
    t_xsB = [T(f"xsB{i}") for i in range(18)]
    for c in range(18):
        bank = c % 2
        pb = psb(bank)
        for cc in range(6):
            if c < 16:
                src, rd = xbc[:, cc, c * 128:(c + 1) * 128], t_xbc[cc]
            else:
                src, rd = ctxx[:, cc, (c - 16) * 128:(c - 15) * 128], t_ctxx[cc]
            tr(pb[:, cc * 128:(cc + 1) * 128], src, ident_bf, [rd, t_const], [tPS[bank]])
        cp("act" if c % 2 == 0 else "dve", xsB[:, c, :], pb[:, 0:768], [tPS[bank]], [t_xsB[c]])
    P.barrier()

    tt(dta, dt_all, a_b.unsqueeze(1).broadcast_to([128, 18, 16]), ALU.mult, [t_dt, t_ab], [t_dta])
    for c in range(18):
        mm(PS[0][:, c * 16:(c + 1) * 16], LE_f, dta[:, c, :], True, True, [t_dta, t_const], [tPS[0]])
    for c in range(18):
        mm(PS[1][:, c * 16:(c + 1) * 16], ones_f, dta[:, c, :], True, True, [t_dta, t_const], [tPS[1]])
    fl = lambda a: a.rearrange("p a b -> p (a b)")
    cp("dve", fl(CFs), PS[0][:, 0:288], [tPS[0]], [t_cf])
    cp("dve", fl(TOTs), PS[1][:, 0:288], [tPS[1]], [t_tot])
    tt(fl(D1), fl(TOTs), fl(CFs), ALU.subtract, [t_tot, t_cf], [t_d1])
    act(fl(cdt), fl(TOTs), AF.Exp, [t_tot], [t_cd])
    F_, B_ = slice(0, 8), slice(8, 16)
    act(ecum[:, :, F_], CFs[:, :, F_], AF.Exp, [t_cf], [t_ecum])
    act(wend[:, :, F_], D1[:, :, F_], AF.Exp, [t_d1], [t_wend])
    tt(tmpA[:, :, B_], D1[:, :, B_], dta[:, :, B_], ALU.add, [t_d1, t_dta], [t_tmpA])
    act(ecum[:, :, B_], tmpA[:, :, B_], AF.Exp, [t_tmpA], [t_ecum])
    tt(tmpA[:, :, F_], CFs[:, :, B_], dta[:, :, B_], ALU.subtract, [t_cf, t_dta, t_tmpA], [t_tmpA])
    act(wend[:, :, B_], tmpA[:, :, F_], AF.Exp, [t_tmpA], [t_wend])
    tt(fl(wend), fl(wend), fl(dt_all), ALU.mult, [t_wend, t_dt], [t_wend])

    Hst = [M.view(148 * KB + i * 2 * KB, [128, 512], F32) for i in range(2)]
    t_H = [T(f"H{i}") for i in range(2)]
    Hin = M.view(24 * KB, [128, 16, 2, 512], BF16)
    t_Hin = [[T(f"Hin{c}_{d}") for d in range(2)] for c in range(16)]
    xwb = [M.view(152 * KB + i * KB, [128, 512], BF16) for i in range(4)]
    t_xwb = [T(f"xw{i}") for i in range(4)]
    v8 = lambda a: a.rearrange("p (h d) -> p h d", h=8)
    bc8 = lambda a: a.unsqueeze(2).broadcast_to([128, 8, 64])
    for d_ in range(2):
        memset(Hst[d_], 0.0, [t_H[d_]])
    xw_ctr = [0]

    def state_step(d_, c):
        i = xw_ctr[0] % 4
        xw_ctr[0] += 1
        bank = 2 + d_
        hs = slice(d_ * 8, d_ * 8 + 8)
        tt(v8(xwb[i]), v8(xsB[:, c, 0:512]), bc8(wend[:, c, hs]), ALU.mult, [t_xsB[c], t_wend], [t_xwb[i]], eng="pool")
        for g in range(2):
            mm(PS[bank][:, g * 256:(g + 1) * 256], xsB[:, c, 512 + g * 128:512 + (g + 1) * 128],
               xwb[i][:, g * 256:(g + 1) * 256], True, True, [t_xsB[c], t_xwb[i]], [tPS[bank]])
        tt(v8(Hst[d_]), v8(Hst[d_]), bc8(cdt[:, c, hs]), ALU.mult, [t_H[d_], t_cd], [t_H[d_]])
        tt(Hst[d_], Hst[d_], psf(bank), ALU.add, [t_H[d_], tPS[bank]], [t_H[d_]])

    state_step(0, 16)
    state_step(0, 17)
    state_step(1, 17)
    state_step(1, 16)
    for s_ in range(16):
        for d_, c in ((0, s_), (1, 15 - s_)):
            cp("act", Hin[:, c, d_, :], Hst[d_], [t_H[d_]], [t_Hin[c][d_]])
            if s_ < 15:
                state_step(d_, c)
    if stage == 1.5:
        return finish_dbg(Hin[:, 0, :, :].rearrange("p a b -> p (a b)"), 1024, [t_Hin[0][0], t_Hin[0][1]], stage_off=56 * KB)

    xdt = [[M.view(156 * KB + (d_ * 2 + i) * KB, [128, 512], BF16) for i in range(2)] for d_ in range(2)]
    t_xdt = [[T(f"xdt{d_}{i}") for i in range(2)] for d_ in range(2)]
    Abuf = [M.view(160 * KB + d_ * 2 * KB, [128, 8, 128], BF16) for d_ in range(2)]
    t_A = [T(f"A{d_}") for d_ in range(2)]
    Eb = [M.view(164 * KB + i * KB, [128, 512], BF16) for i in range(4)]
    Mb = [M.view(168 * KB + i * KB, [128, 512], BF16) for i in range(4)]
    t_E = [T(f"E{i}") for i in range(4)]
    t_M = [T(f"M{i}") for i in range(4)]
    CBm = [M.view(172 * KB + i * 512, [128, 2, 128], BF16) for i in range(2)]
    t_CBm = [T(f"CBm{i}") for i in range(2)]
    ynb = M.view(173 * KB, [128, 512], BF16)
    t_yn = T("yn")
    ytmp = [M.view(72 * KB + i * 2 * KB, [128, 512], F32) for i in range(4)]
    t_yt = [T(f"yt{i}") for i in range(4)]
    tri = [LE_bf, GE_bf]
    stri = [GT_bf, LT_bf]
    Mb2 = [Mb, [M.view(80 * KB + i * KB, [128, 512], BF16) for i in range(4)]]
    t_M2 = [t_M, [T(f"Mx{i}") for i in range(4)]]
    ynb2 = [ynb, M.view(84 * KB, [128, 512], BF16)]
    xDb = [M.view(85 * KB + i * KB, [128, 512], BF16) for i in range(2)]
    t_xD = [T(f"xD{i}") for i in range(2)]
    t_yn2 = [t_yn, T("yn2")]

    def ssd_s1(c):
        cs = slice(c * 128, (c + 1) * 128)
        pi = c % 2
        cbk = 0 if pi == 0 else 7
        for g in range(2):
            mm(PS[cbk][:, g * 128:(g + 1) * 128], xbc[:, 4 + g, cs], xbc[:, 6 + g, cs], True, True,
               [t_xbc[4 + g], t_xbc[6 + g]], [tPS[cbk]])
        cbv = PS[cbk][:, 0:256].rearrange("p (a b) -> p a b", a=2)
        for d_ in range(2):
            hs = slice(d_ * 8, d_ * 8 + 8)
            tt(Abuf[d_], stri[d_].unsqueeze(1).broadcast_to([128, 8, 128]),
               dta[:, c, hs].unsqueeze(2).broadcast_to([128, 8, 128]), ALU.mult, [t_const, t_dta], [t_A[d_]], eng="pool")
            tt(v8(xdt[d_][pi]), v8(xsB[:, c, 0:512]), bc8(dt_all[:, c, hs]), ALU.mult, [t_xsB[c], t_dt], [t_xdt[d_][pi]])
            if d_ == 0:
                tt(v8(xDb[pi]), v8(xsB[:, c, 0:512]), bc8(brow[:, BR_D:BR_D + 8]), ALU.mult, [t_xsB[c], t_brow], [t_xD[pi]],
                   eng="pool")
            for g in range(2):
                bank = 2 + g
                e_i = d_ * 2 + g
                for hh in range(4):
                    mm(PS[bank][:, hh * 128:(hh + 1) * 128], Abuf[d_][:, g * 4 + hh, :], tri[d_], True, False,
                       [t_A[d_], t_const], [tPS[bank]])
                    mm(PS[bank][:, hh * 128:(hh + 1) * 128], ident_bf, neg_bf[d_], False, True,
                       [t_const], [tPS[bank]])
                act(Eb[e_i], psf(bank), AF.Exp, [tPS[bank]], [t_E[e_i]])
                tt(Mb2[pi][e_i].rearrange("p (a b) -> p a b", a=4), Eb[e_i].rearrange("p (a b) -> p a b", a=4),
                   cbv[:, g:g + 1, :].broadcast_to([128, 4, 128]), ALU.mult, [t_E[e_i], tPS[cbk]], [t_M2[pi][e_i]])

    yn_info = {}

    def ssd_s2(c):
        cs = slice(c * 128, (c + 1) * 128)
        pi = c % 2
        mm(psf(4), ident_bf, xDb[pi], True, False, [t_const, t_xD[pi]], [tPS[4]])
        for h in range(8):
            g, hh = h // 4, h % 4
            mm(PS[4][:, h * 64:(h + 1) * 64], Mb2[pi][g][:, hh * 128:(hh + 1) * 128], xdt[0][pi][:, h * 64:(h + 1) * 64],
               False, False, [t_M2[pi][g], t_xdt[0][pi]], [tPS[4]])
            mm(PS[4][:, h * 64:(h + 1) * 64], Mb2[pi][2 + g][:, hh * 128:(hh + 1) * 128], xdt[1][pi][:, h * 64:(h + 1) * 64],
               False, h == 7, [t_M2[pi][2 + g], t_xdt[1][pi]], [tPS[4]])
        for d_ in range(2):
            for g in range(2):
                mm(PS[5 + d_][:, g * 256:(g + 1) * 256], xbc[:, 6 + g, cs], Hin[:, c, d_, g * 256:(g + 1) * 256],
                   True, True, [t_xbc[6 + g], t_Hin[c][d_]], [tPS[5 + d_]])
        tt(v8(ytmp[0]), v8(psf(5)), bc8(ecum[:, c, 0:8]), ALU.mult, [tPS[5], t_ecum], [t_yt[0]])
        tt(v8(ytmp[1]), v8(psf(6)), bc8(ecum[:, c, 8:16]), ALU.mult, [tPS[6], t_ecum], [t_yt[1]])
        tt(ytmp[0], ytmp[0], ytmp[1], ALU.add, [t_yt[0], t_yt[1]], [t_yt[0]])
        tt(ytmp[0], ytmp[0], psf(4), ALU.add, [t_yt[0], tPS[4]], [t_yt[0]])
        tt(ytmp[2 + pi], ytmp[0], zs[:, c, :], ALU.mult, [t_yt[0], t_zs[c]], [t_yt[2 + pi]])
        k_ = stat_ctr[0] % 32
        stat_ctr[0] += 1
        ss, rs, tst = stat[:, 2 * k_:2 * k_ + 1], stat[:, 2 * k_ + 1:2 * k_ + 2], t_stat[k_]
        act(junk[:, 0:512], ytmp[2 + pi], AF.Square, [t_yt[2 + pi]], [tst], accum_out=ss)
        act(rs, ss, AF.Ln, [tst], [tst], scale=1.0 / 512, bias=EPS)
        act(rs, rs, AF.Exp, [tst], [tst], scale=-0.5)
        yn_info[c] = (rs, tst)

    def ssd_s3(c):
        cs = slice(c * 128, (c + 1) * 128)
        pi = c % 2
        rs, tst = yn_info[c]
        stt(ynb2[pi], ytmp[2 + pi], rs, brow[:, BR_SNW:BR_SNW + 512], ALU.mult, ALU.mult, [t_yt[2 + pi], tst, t_brow], [t_yn2[pi]])
        pb = psb(1)
        for q in range(4):
            tr(pb[:, q * 128:(q + 1) * 128], ynb2[pi][:, q * 128:(q + 1) * 128], ident_bf, [t_yn2[pi], t_const], [tPS[1]])
        cp("act", mixT[:, 4:8, cs], pb[:, 0:512].rearrange("p (a b) -> p a b", a=4), [tPS[1]],
           [t_mix[4 + q][c] for q in range(4)])

    for it in range(16 + 2):
        if it < 16:
            ssd_s1(it)
        if 0 <= it - 1 < 16:
            ssd_s2(it - 1)
        if 0 <= it - 2 < 16:
            ssd_s3(it - 2)
    P.barrier()
    if stage == 2:
        return finish_dbg(mixT[:, 4:8, :].rearrange("p a b -> p (a b)"), 4 * S_LEN,
                          [t_mix[k][c] for k in range(4, 8) for c in range(16)])

    w_outb = M.view(146 * KB, [128, 8, D], BF16)
    t_wout = T("wout")
    dma("pool", w_outb, w_out_d.rearrange("(kc p) n -> p kc n", p=128), (), [t_wout])
    x1 = M.view(24 * KB, [128, NTT, D], F32)
    t_x1 = [T(f"x1_{i}") for i in range(NTT)]
    for tt_ in range(NTT):
        dma("sp", x1[:, tt_, :], x_v[tt_], (), [t_x1[tt_]])
    tmpw = [M.view(162 * KB + i * 2 * KB, [128, 512], F32) for i in range(2)]
    t_tmpw = [T(f"tmpw{i}") for i in range(2)]
    blk = 0
    for tt_ in range(NTT):
        for ch in range(2):
            bank = blk % 4
            i = blk % 2
            blk += 1
            for kc in range(8):
                mm(psf(bank), mixT[:, kc, tt_ * 128:(tt_ + 1) * 128], w_outb[:, kc, ch * 512:(ch + 1) * 512],
                   kc == 0, kc == 7, [t_mix[kc][tt_], t_wout], [tPS[bank]])
            tt(tmpw[i], psf(bank), g1b[:, ch * 512:(ch + 1) * 512], ALU.mult, [tPS[bank], t_g1b], [t_tmpw[i]])
            tt(x1[:, tt_, ch * 512:(ch + 1) * 512], x1[:, tt_, ch * 512:(ch + 1) * 512], tmpw[i], ALU.add,
               [t_x1[tt_], t_tmpw[i]], [t_x1[tt_]])
    adab1 = [M.view(128 * KB + i * 8 * KB, [128, 8, 512], BF16) for i in range(2)]
    t_adab1 = [T(f"adabx{i}") for i in range(2)]
    mod_pieces([6, 7, 8, 9, 10, 11], adab1, t_adab1, 166 * KB, 6, 7)
    t_a2 = T("a2")
    stt(a2, modT[:, 32:40, 0], 1.0, pvec[:, PV_NW2:PV_NW2 + 8], ALU.add, ALU.mult, [t_modT, t_pvec], [t_a, t_a2])
    if stage == 3:
        return finish_dbg(x1[:, 0:4, :].rearrange("p a b -> p (a b)"), 4096, [t_x1[i] for i in range(4)], direct=True)

    h2T = M.view(88 * KB, [128, 8, S_LEN], BF16)
    t_h2T = [[T(f"h2T{k}_{tb}") for tb in range(4)] for k in range(8)]
    xn2 = [M.view(120 * KB + i * 2 * KB, [128, 1024], BF16) for i in range(4)]
    t_xn2 = [T(f"xn2{i}") for i in range(4)]
    for tb in range(4):
        norm_group([x1[:, tb * 4 + i, :] for i in range(4)], [t_x1[tb * 4 + i] for i in range(4)], 4, xn2, t_xn2, a2, sh2,
                   (lambda tb: (lambda k: h2T[:, k, tb * 512:(tb + 1) * 512]))(tb),
                   [t_h2T[k][tb] for k in range(8)], t_a, 4)
    P.barrier()

    uT = M.view(120 * KB, [128, 8, S_LEN], BF16)
    t_uT = [T(f"uT{i}") for i in range(8)]
    wdn = M.view(152 * KB, [128, 8, D], BF16)
    t_wdn = T("wdn")
    wup = [M.view(168 * KB + i * 4 * KB, [128, 8, 256], BF16) for i in range(2)]
    t_wup = [T(f"wup{i}") for i in range(2)]
    gpad = [M.view(176 * KB + i * 8208, [128, 2052], F32) for i in range(2)]
    t_gpad = [T(f"gpad{i}") for i in range(2)]
    valb = M.view(197120, [128, S_LEN], BF16)
    t_val = T("val")
    facc = M.view(197120 + 4096, [128, S_LEN], F32)
    t_facc = T("facc")
    tmpd = [g1b[:, 0:512], g1b[:, 512:1024]]
    t_tmpd = [T("tmpd0"), T("tmpd1")]
    for i in range(2):
        memset(gpad[i][:, 0:1], 0.0, [t_gpad[i]])
        memset(gpad[i][:, 2049:2050], 0.0, [t_gpad[i]])
    w_down_v = w_down_d.rearrange("(f p) n -> p f n", p=128)
    out_v = out_d.rearrange("(t p) d -> t p d", p=128)
    out_ops = []
    fin_pending = []
    dblk = 0
    for gi, (f0, f1) in enumerate(FF_GROUPS):
        ng = f1 - f0
        for f in range(f0, f1):
            sl = f - f0
            wi = f % 2
            dma("pool", wup[wi].rearrange("p a b -> p (a b)"), w_up_d[f], (), [t_wup[wi]])
            if sl == 1:
                dma("pool", wdn[:, 0:ng, :], w_down_v[:, f0:f1, :], (), [t_wdn])
            if sl >= 2:
                for s2_ in ([0, 1, 2] if sl == 2 else [sl]):
                    tt(wdn[:, s2_, :], wdn[:, s2_, :], g2b, ALU.mult, [t_wdn, t_g2b], [t_wdn])
            for tb in range(4):
                bg, bv = tb % 2, 2 + tb % 2
                for kc in range(8):
                    mm(psf(bg), wup[wi][:, kc, 0:128], h2T[:, kc, tb * 512:(tb + 1) * 512], kc == 0, kc == 7,
                       [t_wup[wi], t_h2T[kc][tb]], [tPS[bg]])
                cp("act", gpad[wi][:, 1 + tb * 512:1 + (tb + 1) * 512], psf(bg), [tPS[bg]], [t_gpad[wi]])
                for kc in range(8):
                    mm(psf(bv), wup[wi][:, kc, 128:256], h2T[:, kc, tb * 512:(tb + 1) * 512], kc == 0, kc == 7,
                       [t_wup[wi], t_h2T[kc][tb]], [tPS[bv]])
                cp("act", valb[:, tb * 512:(tb + 1) * 512], psf(bv), [tPS[bv]], [t_val])
            w0 = pvec[:, PV_FCW + f * 3 + 0:PV_FCW + f * 3 + 1]
            w1 = pvec[:, PV_FCW + f * 3 + 1:PV_FCW + f * 3 + 2]
            w2 = pvec[:, PV_FCW + f * 3 + 2:PV_FCW + f * 3 + 3]
            bb = pvec[:, PV_FCB + f:PV_FCB + f + 1]
            ts(facc, gpad[wi][:, 1:1 + S_LEN], w1, ALU.mult, [t_gpad[wi], t_pvec], [t_facc], s2=bb, op1=ALU.add)
            stt(facc, gpad[wi][:, 0:S_LEN], w0, facc, ALU.mult, ALU.add, [t_gpad[wi], t_pvec, t_facc], [t_facc])
            stt(facc, gpad[wi][:, 2:2 + S_LEN], w2, facc, ALU.mult, ALU.add, [t_gpad[wi], t_pvec, t_facc], [t_facc])
            act(uT[:, sl, :], facc, AF.Silu, [t_facc], [t_uT[sl]])
            tt(uT[:, sl, :], uT[:, sl, :], valb, ALU.mult, [t_uT[sl], t_val], [t_uT[sl]])
        last = gi == len(FF_GROUPS) - 1
        for tt_ in range(NTT):
            for ch in range(2):
                bank = 4 + dblk % 4
                i = dblk % 2
                dblk += 1
                for sl in range(ng):
                    mm(psf(bank), uT[:, sl, tt_ * 128:(tt_ + 1) * 128], wdn[:, sl, ch * 512:(ch + 1) * 512],
                       sl == 0, sl == ng - 1, [t_uT[sl], t_wdn], [tPS[bank]])
                tt(x1[:, tt_, ch * 512:(ch + 1) * 512], x1[:, tt_, ch * 512:(ch + 1) * 512], psf(bank), ALU.add,
                   [t_x1[tt_], tPS[bank]], [t_x1[tt_]])
            if last:
                k_ = stat_ctr[0] % 32
                stat_ctr[0] += 1
                ss, rs, tst = stat[:, 2 * k_:2 * k_ + 1], stat[:, 2 * k_ + 1:2 * k_ + 2], t_stat[k_]
                act(junk, x1[:, tt_, :], AF.Square, [t_x1[tt_]], [tst], accum_out=ss)
                act(rs, ss, AF.Ln, [tst], [tst], scale=1.0 / D, bias=EPS)
                act(rs, rs, AF.Exp, [tst], [tst], scale=-0.5)
                fin_pending.append((tt_, rs, tst))
                while len(fin_pending) > (1 if tt_ < NTT - 1 else 0):
                    t2_, rs2, tst2 = fin_pending.pop(0)
                    stt(x1[:, t2_, :], x1[:, t2_, :], rs2, brow[:, BR_FNW:BR_FNW + D], ALU.mult, ALU.mult,
                        [t_x1[t2_], tst2, t_brow], [t_x1[t2_]])
                    out_ops.append(dma("sp", out_v[t2_], x1[:, t2_, :], [t_x1[t2_]], []))
    P.emit(final_wait_ops=out_ops)
    return nc


def _const_tables():
    a = np.arange(128)
    ident = (a[:, None] == a[None, :])
    swap = np.where((a % 64) < 32, a + 32, a - 32)
    rperm = (a[:, None] == swap[None, :])
    LE = a[:, None] <= a[None, :]
    GE = a[:, None] >= a[None, :]
    GT = a[:, None] > a[None, :]
    LT = a[:, None] < a[None, :]
    ones = np.ones((128, 128), bool)
    sel0 = np.broadcast_to((a == 0)[:, None], (128, 128))
    sel64 = np.broadcast_to((a == 64)[:, None], (128, 128))
    negf = np.where(a[None, :] < a[:, None], -30000.0, 0.0)
    negb = np.where(a[None, :] > a[:, None], -30000.0, 0.0)
    cmat = np.concatenate([m.astype(np.float32) for m in (ident, rperm, LE, GE, GT, LT, ones, sel0, sel64, negf, negb)], axis=1)
    t = np.arange(S_LEN)
    row_pos = (t // 64).astype(np.float32)
    col_pos = (t % 64).astype(np.float32)
    freqs = (np.float32(10000.0) ** (-np.arange(16, dtype=np.float32) / np.float32(16))).astype(np.float32)
    ang = np.concatenate([row_pos[:, None] * freqs, col_pos[:, None] * freqs], axis=-1).astype(np.float32)
    cos = np.cos(ang).astype(np.float32).T
    sin = np.sin(ang).astype(np.float32).T
    idx = a % 32
    sign = np.where((a % 64) < 32, -1.0, 1.0).astype(np.float32)
    rope = np.concatenate([cos[idx], sin[idx] * sign[:, None]], axis=1).astype(np.float32)
    p = np.arange(128)
    ip, cp_ = p // 64, p % 64
    e = np.arange(14)
    w = np.arange(64)
    dr = 13 - e[None, :, None] + ip[:, None, None] + 0 * w[None, None, :]
    dc = cp_[:, None, None] - w[None, None, :] + 15 + 0 * e[None, :, None]
    cs = np.clip(w - 8, 0, 48)
    colv = (cp_[:, None, None] >= cs[None, None, :]) & (cp_[:, None, None] < cs[None, None, :] + 16)
    colv = colv & (dr >= -99)
    drv = (dr >= 0) & (dr <= 14)
    m_int = colv & (dr >= 3) & (dr <= 10)
    m_edge = colv & drv
    mtab = np.stack([m_int, m_edge], axis=1).astype(np.float32).reshape(128, 2 * 14 * 64)
    dr_c = np.clip(dr, 0, 14)
    dc_c = np.clip(dc, 0, 30)
    return cmat, rope, mtab, dr_c, dc_c


def _prep_shared(inp):
    cmat, rope, mtab, dr_c, dc_c = _const_tables()
    rpb = inp["rpb"][0]
    gtab = rpb[:, dr_c, dc_c]
    gtab = np.ascontiguousarray(gtab.transpose(1, 0, 2, 3)).reshape(128, 8 * 14 * 64).astype(np.float32)

    def pl(v, n):
        return np.ascontiguousarray(v.reshape(n, 128).T)

    pvec = np.zeros((128, PV_N), np.float32)
    pvec[:, PV_NW1:PV_NW1 + 8] = pl(inp["norm1_w"][0], 8)
    pvec[:, PV_NW2:PV_NW2 + 8] = pl(inp["norm2_w"][0], 8)
    scw = inp["ssd_conv_w"][0]
    pvec[:, PV_SCW:PV_SCW + 24] = np.stack([pl(scw[k], 8) for k in range(3)], axis=-1).reshape(128, 24)
    pvec[:, PV_SCB:PV_SCB + 8] = pl(inp["ssd_conv_b"][0], 8)
    fcw = inp["ffn_conv_w"][0]
    pvec[:, PV_FCW:PV_FCW + 66] = np.stack([pl(fcw[k], 22) for k in range(3)], axis=-1).reshape(128, 66)
    pvec[:, PV_FCB:PV_FCB + 22] = pl(inp["ffn_conv_b"][0], 22)
    brow = np.zeros((BR_N,), np.float32)
    brow[BR_FNW:BR_FNW + 1024] = inp["final_norm_w"]
    brow[BR_SNW:BR_SNW + 512] = inp["ssd_norm_w"][0]
    brow[BR_DTB:BR_DTB + 16] = inp["dt_bias"][0].reshape(16)
    brow[BR_ALOG:BR_ALOG + 16] = inp["a_log"][0].reshape(16)
    brow[BR_D:BR_D + 8] = inp["ssd_d"][0]
    brow = np.ascontiguousarray(np.broadcast_to(brow[None, :], (128, BR_N)))
    w_up = inp["ffn_w_up"][0]
    gate = w_up[:, :D_FF].reshape(8, 128, NFF, 128)
    val = w_up[:, D_FF:].reshape(8, 128, NFF, 128)
    w_up_l = np.stack([gate, val], axis=3)
    w_up_l = np.ascontiguousarray(w_up_l.transpose(2, 1, 0, 3, 4)).reshape(NFF, 128, 2048)
    shared = {
        "ada_w": np.ascontiguousarray(inp["ada_w"][0]),
        "ada_b2": np.ascontiguousarray(np.broadcast_to(inp["ada_b"][0][None, :], (2, 6 * D))),
        "w_in": np.ascontiguousarray(inp["w_in"][0]),
        "w_out": np.ascontiguousarray(inp["w_out"][0]),
        "w_up": w_up_l,
        "w_down": np.ascontiguousarray(inp["ffn_w_down"][0]),
        "pvec": pvec, "brow": brow, "cmat": cmat, "rope": rope, "gtab": gtab, "mtab": mtab,
    }
    return shared


def _in_maps(inp, cores):
    inp = {k: np.asarray(v, dtype=np.float32) for k, v in inp.items()}
    shared = _prep_shared(inp)
    maps = []
    for b in cores:
        cc = np.stack([inp["c"][b], inp["c_ctx"]], axis=-1)
        ccT = np.ascontiguousarray(cc.reshape(8, 128, 2).transpose(1, 0, 2)).reshape(128, 16)
        m = dict(shared)
        m["x"] = np.ascontiguousarray(inp["x"][b])
        m["ctx"] = np.ascontiguousarray(inp["ctx"][b])
        m["ccT"] = ccT
        maps.append(m)
    return maps


_NC_CACHE = {}


def kernel(**inputs):
    if "nc" not in _NC_CACHE:
        _NC_CACHE["nc"] = build()
    nc = _NC_CACHE["nc"]
    maps = _in_maps(inputs, list(range(8)))
    res = run_bass_kernel_spmd(nc, maps, core_ids=list(range(8)))
    return np.stack([np.asarray(r["out"], dtype=np.float32) for r in res.results], axis=0)
```

```python
import numpy as np
import concourse.bass as bass
import concourse.mybir as mybir

F32 = mybir.dt.float32
BF16 = mybir.dt.bfloat16
AF = mybir.ActivationFunctionType
ALU = mybir.AluOpType

ENGS = ("pe", "act", "dve", "pool", "sp")
SEM_LIMIT = 30000
N_DMA_LANES = 24


class T:
    __slots__ = ("name", "last_writer", "readers", "excl")

    def __init__(self, name="", excl=False):
        self.name = name
        self.last_writer = None
        self.readers = []
        self.excl = excl


class Op:
    __slots__ = ("eng", "fn", "idx", "deps", "signals", "sig", "is_dma", "lane", "lane_val",
                 "lane_prev", "name")

    def __init__(self, eng, fn, idx, is_dma, name):
        self.eng = eng
        self.fn = fn
        self.idx = idx
        self.deps = set()
        self.signals = False
        self.sig = None
        self.is_dma = is_dma
        self.lane = None
        self.lane_val = None
        self.lane_prev = None
        self.name = name


class Prog:
    def __init__(self, nc):
        self.nc = nc
        self.eng_ops = {e: [] for e in ENGS}
        self.n_dma = {"sp": 0, "pool": 0, "act": 0, "pe": 0, "dve": 0}
        self.lane_last = [None] * N_DMA_LANES
        self.lane_cnt = [0] * N_DMA_LANES

    def op(self, eng, fn, reads=(), writes=(), dma=False, name=None):
        lst = self.eng_ops[eng]
        o = Op(eng, fn, len(lst), dma, name)
        lst.append(o)
        deps = o.deps
        for t in reads:
            if t.last_writer is not None:
                deps.add(t.last_writer)
            if t.excl:
                for r in t.readers:
                    if r.eng != eng:
                        deps.add(r)
        for t in writes:
            if t.last_writer is not None:
                deps.add(t.last_writer)
            for r in t.readers:
                deps.add(r)
        for t in reads:
            t.readers.append(o)
        for t in writes:
            t.readers = []
            t.last_writer = o
        deps.discard(o)
        if dma:
            half = N_DMA_LANES // 2
            lane = (self.n_dma[eng] % half) + (half if eng == "pool" else 0)
            self.n_dma[eng] += 1
            o.lane = lane
            o.lane_prev = self.lane_last[lane]
            self.lane_cnt[lane] += 1
            o.lane_val = 16 * self.lane_cnt[lane]
            self.lane_last[lane] = o
        return o

    def dma(self, eng, out, in_, reads=(), writes=(), **kw):
        return self.op(eng, lambda e: e.dma_start(out=out, in_=in_, **kw), reads, writes, dma=True)

    def emit(self, final_wait_ops=()):
        nc = self.nc
        for o in final_wait_ops:
            if not o.is_dma:
                o.signals = True
        for e in ENGS:
            for o in self.eng_ops[e]:
                for d in o.deps:
                    if d.is_dma:
                        continue
                    if d.eng == "pe" and o.eng == "pe":
                        continue
                    d.signals = True
        n_sig = {}
        for e in ENGS:
            c = 0
            for o in self.eng_ops[e]:
                if o.signals and not o.is_dma:
                    c += 1
                    o.sig = c
            n_sig[e] = c
        sems = {}
        for e in ENGS:
            n = (n_sig[e] + SEM_LIMIT - 1) // SEM_LIMIT
            sems[e] = [nc.alloc_semaphore(f"s_{e}_{i}") for i in range(max(n, 1))]
        lane_sems = [nc.alloc_semaphore(f"s_dma_{i}") for i in range(N_DMA_LANES)]
        self._sems = sems
        self._lane_sems = lane_sems
        engobj = {"pe": "tensor", "act": "scalar", "dve": "vector", "pool": "gpsimd", "sp": "sync"}

        def run_engine(ename, eng):
            waited = {}
            ops = self.eng_ops[ename]
            for o in ops:
                need = {}
                for d in o.deps:
                    if d.is_dma:
                        key = ("l", d.lane)
                        val = d.lane_val
                    else:
                        if d.eng == "pe" and ename == "pe":
                            continue
                        key = ("e", d.eng, (d.sig - 1) // SEM_LIMIT)
                        val = (d.sig - 1) % SEM_LIMIT + 1
                    if need.get(key, 0) < val:
                        need[key] = val
                if o.is_dma and o.lane_prev is not None:
                    key = ("l", o.lane)
                    val = o.lane_prev.lane_val
                    if need.get(key, 0) < val:
                        need[key] = val
                for key, val in need.items():
                    if waited.get(key, 0) >= val:
                        continue
                    waited[key] = val
                    if key[0] == "l":
                        eng.wait_ge(lane_sems[key[1]], val)
                    else:
                        eng.wait_ge(sems[key[1]][key[2]], val)
                ins = o.fn(eng)
                if o.is_dma:
                    ins.then_inc(lane_sems[o.lane], 16)
                elif o.signals:
                    ins.then_inc(sems[ename][(o.sig - 1) // SEM_LIMIT], 1)
            if ename == "sp":
                for o in final_wait_ops:
                    if o.is_dma:
                        eng.wait_ge(lane_sems[o.lane], o.lane_val)
                    else:
                        eng.wait_ge(sems[o.eng][(o.sig - 1) // SEM_LIMIT], (o.sig - 1) % SEM_LIMIT + 1)

        with nc.Block() as block:
            @block.tensor
            def _(e):
                run_engine("pe", e)

            @block.scalar
            def _(e):
                run_engine("act", e)

            @block.vector
            def _(e):
                run_engine("dve", e)

            @block.gpsimd
            def _(e):
                run_engine("pool", e)

            @block.sync
            def _(e):
                run_engine("sp", e)


U8 = mybir.dt.uint8
_DT_SIZE = {F32: 4, BF16: 2, U8: 1, mybir.dt.int32: 4, mybir.dt.uint32: 4}


class Mem:
    def __init__(self, nc, nbytes=206 * 1024):
        self.big = nc.alloc_sbuf_tensor("bigmem", [128, nbytes], U8)
        self.nbytes = nbytes

    def view(self, off, shape, dtype, p0=0):
        sz = _DT_SIZE[dtype]
        n = 1
        for d in shape[1:]:
            n *= d
        assert off % 4 == 0 and off + n * sz <= self.nbytes, (off, shape, self.nbytes)
        v = self.big[p0:p0 + shape[0], off:off + n * sz]
        if dtype != U8:
            v = v.bitcast(dtype)
        if len(shape) == 3:
            v = v.rearrange("p (a b) -> p a b", a=shape[1])
        elif len(shape) == 4:
            v = v.rearrange("p (a b c) -> p a b c", a=shape[1], b=shape[2])
        return v
from concourse.bass_utils import run_bass_kernel_spmd


KB = 1024
S_LEN = 2048
D = 1024
NTT = 16
LC = 256
EPS = 1e-6
D_FF = 2816
NFF = 22
FF_GROUPS = [(0, 8), (8, 15), (15, 22)]

PV_NW1, PV_NW2, PV_SCW, PV_SCB, PV_FCW, PV_FCB, PV_N = 0, 8, 16, 40, 48, 114, 136
BR_FNW, BR_SNW, BR_DTB, BR_ALOG, BR_D, BR_N = 0, 1024, 1536, 1552, 1568, 1576


def _add_barrier(P):
    lasts = []
    for e in ENGS:
        ops = P.eng_ops[e]
        if ops:
            lasts.append(ops[-1])
    if not hasattr(P, "bar_idx"):
        P.bar_idx = {e: 0 for e in ENGS}
    dmas = [o for e in ENGS for o in P.eng_ops[e][P.bar_idx[e]:] if o.is_dma]
    P.pending = set(lasts + dmas)
    P.pending_engs = set(ENGS)
    P.bar_idx = {e: len(P.eng_ops[e]) for e in ENGS}


_orig_op = Prog.op


def _op_with_barrier(self, eng, fn, reads=(), writes=(), dma=False, name=None):
    o = _orig_op(self, eng, fn, reads, writes, dma, name)
    pe = getattr(self, "pending_engs", None)
    if pe and eng in pe:
        o.deps |= self.pending
        o.deps.discard(o)
        pe.discard(eng)
    return o


Prog.op = _op_with_barrier
Prog.barrier = _add_barrier


def build(stage=99):
    nc = bass.Bass("TRN2", target_bir_lowering=False)
    P = Prog(nc)
    M = Mem(nc)

    def din(name, shape):
        return nc.dram_tensor(name, list(shape), F32, kind="ExternalInput").ap()

    x_d = din("x", [S_LEN, D])
    ctx_d = din("ctx", [LC, D])
    ccT_d = din("ccT", [128, 16])
    ada_w_d = din("ada_w", [D, 6 * D])
    ada_b2_d = din("ada_b2", [2, 6 * D])
    w_in_d = din("w_in", [D, 3088])
    w_out_d = din("w_out", [D, D])
    w_up_d = din("w_up", [NFF, 128, 2048])
    w_down_d = din("w_down", [D_FF, D])
    pvec_d = din("pvec", [128, PV_N])
    brow_d = din("brow", [128, BR_N])
    cmat_d = din("cmat", [128, 11 * 128])
    rope_d = din("rope", [128, 2 * S_LEN])
    gtab_d = din("gtab", [128, 8 * 14 * 64])
    mtab_d = din("mtab", [128, 2 * 14 * 64])
    out_d = nc.dram_tensor("out", [S_LEN, D], F32, kind="ExternalOutput").ap()
    dbg_d = None
    if stage < 99:
        dbg_d = nc.dram_tensor("dbg", [128, 8 * S_LEN], F32, kind="ExternalOutput").ap()

    PS = [nc.alloc_psum_tensor(f"psb{i}", [128, 512], F32) for i in range(8)]
    tPS = [T(f"ps{i}", excl=True) for i in range(8)]

    def psf(i):
        return PS[i][:, :]

    def psb(i):
        return PS[i][:, :].bitcast(BF16)

    def mm(out, lhsT, rhs, start, stop, reads, writes):
        return P.op("pe", lambda e: e.matmul(out, lhsT=lhsT, rhs=rhs, start=start, stop=stop), reads, writes)

    def tr(out, in_, ident, reads, writes):
        return P.op("pe", lambda e: e.transpose(out, in_, ident), reads, writes)

    def act(out, in_, func, reads, writes, scale=1.0, bias=0.0, accum_out=None):
        if accum_out is None:
            return P.op("act", lambda e: e.activation(out=out, in_=in_, func=func, scale=scale, bias=bias), reads, writes)
        return P.op("act", lambda e: e.activation(out=out, in_=in_, func=func, scale=scale, bias=bias,
                                                  accum_out=accum_out), reads, writes)

    def tt(out, in0, in1, op, reads, writes, eng="dve"):
        return P.op(eng, lambda e: e.tensor_tensor(out=out, in0=in0, in1=in1, op=op), reads, writes)

    def ts(out, in0, s1, op0, reads, writes, s2=None, op1=None, eng="dve"):
        if op1 is None:
            return P.op(eng, lambda e: e.tensor_scalar(out=out, in0=in0, scalar1=s1, scalar2=None, op0=op0), reads, writes)
        return P.op(eng, lambda e: e.tensor_scalar(out=out, in0=in0, scalar1=s1, scalar2=s2, op0=op0, op1=op1), reads, writes)

    def stt(out, in0, scalar, in1, op0, op1, reads, writes):
        return P.op("dve", lambda e: e.scalar_tensor_tensor(out=out, in0=in0, scalar=scalar, in1=in1, op0=op0, op1=op1),
                    reads, writes)

    def cp(eng, out, in_, reads, writes):
        if eng == "act":
            return act(out, in_, AF.Copy, reads, writes)
        return P.op("dve", lambda e: e.tensor_copy(out=out, in_=in_), reads, writes)

    def recip(out, in_, reads, writes):
        return P.op("dve", lambda e: e.reciprocal(out=out, in_=in_), reads, writes)

    def memset(out, val, writes):
        return P.op("dve", lambda e: e.memset(out, val), (), writes)

    def dma(eng, out, in_, reads, writes):
        return P.dma(eng, out, in_, reads, writes)

    ident_bf = M.view(0, [128, 128], BF16)
    rperm_bf = M.view(256, [128, 128], BF16)
    LE_bf = M.view(512, [128, 128], BF16)
    GE_bf = M.view(768, [128, 128], BF16)
    GT_bf = M.view(1024, [128, 128], BF16)
    LT_bf = M.view(1280, [128, 128], BF16)
    LE_f = M.view(1536, [128, 128], F32)
    ones_f = M.view(2048, [128, 128], F32)
    ident_f = M.view(2560, [128, 128], F32)
    pvec = M.view(3072, [128, PV_N], F32)
    brow = M.view(3616, [128, BR_N], F32)
    modT = M.view(9920, [128, 48, 2], F32)
    a1 = M.view(10304, [128, 8], F32)
    a1c = M.view(10336, [128, 8], F32)
    a2 = M.view(10368, [128, 8], F32)
    g1b = M.view(10400, [128, 1024], F32)
    g2b = M.view(14496, [128, 1024], F32)
    scT = M.view(18592, [128, 8, 2], BF16)
    ccs = M.view(18624, [128, 16], F32)
    stat = M.view(18688, [128, 64], F32)
    a_b = M.view(18944, [128, 16], F32)
    junk = M.view(20480, [128, 1024], BF16)
    sel_f = [M.view(22528 + i * 512, [128, 128], F32) for i in range(2)]
    neg_bf = [M.view(23552 + i * 256, [128, 128], BF16) for i in range(2)]
    t_const, t_pvec, t_brow, t_modT, t_a, t_g1b, t_g2b, t_scT, t_ccs, t_ab = [T(n) for n in
        "const pvec brow modT a g1b g2b scT ccs ab".split()]
    t_stat = [T(f"stat{i}") for i in range(64)]

    PH = 24 * KB
    import os as _os2
    _SALT = float(_os2.environ.get("KSALT", "0"))
    if _SALT:
        memset(junk[:, 0:8], _SALT, [])
    if _os2.environ.get("KPOISON", "0") == "1":
        memset(stat, 1.0e30, t_stat)
    if _os2.environ.get("KPOISON", "0") == "2":
        memset(M.view(22528 + 1024, [128, (206 * KB - 22528 - 1024) // 4], F32), 3.0e38, [])
        for i_ in range(8):
            memset(psf(i_), 3.0e38, [tPS[i_]])
        P.barrier()

    def finish_dbg(src_ap, ncols, trk, stage_off=24 * KB, direct=False):
        if direct:
            op = dma("sp", dbg_d[:, 0:ncols], src_ap, trk, [])
        else:
            dst = M.view(stage_off, [128, ncols], F32)
            t_d = T("dbgst")
            cp("dve", dst, src_ap, trk, [t_d])
            op = dma("sp", dbg_d[:, 0:ncols], dst, [t_d], [])
        P.emit(final_wait_ops=[op])
        return nc

    cstage = M.view(PH, [128, 11 * 128], F32)
    t_cst = T("cstage")
    dma("sp", cstage, cmat_d, (), [t_cst])
    dma("sp", pvec, pvec_d, (), [t_pvec])
    dma("sp", brow, brow_d, (), [t_brow])
    dma("sp", ccs, ccT_d, (), [t_ccs])
    for i, dst in enumerate([ident_bf, rperm_bf, LE_bf, GE_bf, GT_bf, LT_bf]):
        cp("dve", dst, cstage[:, i * 128:(i + 1) * 128], [t_cst], [t_const])
    cp("dve", LE_f, cstage[:, 2 * 128:3 * 128], [t_cst], [t_const])
    cp("dve", ones_f, cstage[:, 6 * 128:7 * 128], [t_cst], [t_const])
    cp("dve", ident_f, cstage[:, 0:128], [t_cst], [t_const])
    cp("dve", sel_f[0], cstage[:, 7 * 128:8 * 128], [t_cst], [t_const])
    cp("dve", sel_f[1], cstage[:, 8 * 128:9 * 128], [t_cst], [t_const])
    cp("dve", neg_bf[0], cstage[:, 9 * 128:10 * 128], [t_cst], [t_const])
    cp("dve", neg_bf[1], cstage[:, 10 * 128:11 * 128], [t_cst], [t_const])
    act(scT.rearrange("p a b -> p (a b)"), ccs, AF.Silu, [t_ccs], [t_scT])

    xs_all = [M.view(174 * KB + i * 4 * KB, [128, 1024], F32) for i in range(8)]
    t_xs_all = [T(f"xs{i}") for i in range(8)]
    x_v = x_d.rearrange("(t p) d -> t p d", p=128)
    ctx_v = ctx_d.rearrange("(t p) d -> t p d", p=128)
    for i in range(2):
        dma("sp", xs_all[i], ctx_v[i], (), [t_xs_all[i]])
    for t_ in range(6):
        dma("sp", xs_all[2 + t_], x_v[t_], (), [t_xs_all[2 + t_]])

    ada_w_v = ada_w_d.rearrange("(kc p) n -> p kc n", p=128)

    def mod_pieces(pcs, adab, t_adab, small_off, acc_bank, aux_bank):
        nb = len(adab)
        ab2p = [M.view(small_off + i * 2 * KB, [2, 512], F32) for i in range(2)]
        mrow = [M.view(small_off + 4 * KB + i * 2 * KB, [2, 512], F32) for i in range(2)]
        t_ab2p = [T(f"ab2p{i}") for i in range(2)]
        t_mrow = [T(f"mrow{i}") for i in range(2)]
        for n_, pc in enumerate(pcs[:nb]):
            dma("pool", adab[n_], ada_w_v[:, :, pc * 512:(pc + 1) * 512], (), [t_adab[n_]])
        for n_, pc in enumerate(pcs):
            b_, i_ = n_ % nb, n_ % 2
            dma("sp", ab2p[i_], ada_b2_d[:, pc * 512:(pc + 1) * 512], (), [t_ab2p[i_]])
            for kc in range(8):
                mm(PS[acc_bank][0:2, :], scT[:, kc, :], adab[b_][:, kc, :], kc == 0, kc == 7,
                   [t_scT, t_adab[b_]], [tPS[acc_bank]])
            if n_ + nb < len(pcs):
                pn = pcs[n_ + nb]
                dma("pool", adab[b_], ada_w_v[:, :, pn * 512:(pn + 1) * 512], (), [t_adab[b_]])
            tt(mrow[i_], PS[acc_bank][0:2, :], ab2p[i_], ALU.add, [tPS[acc_bank], t_ab2p[i_]], [t_mrow[i_]])
            pv = PS[aux_bank][:, 0:8].rearrange("p (a b) -> p a b", b=2)
            for jj in range(4):
                mm(pv[:, jj, :], mrow[i_][0:2, jj * 128:(jj + 1) * 128], ident_f[0:2, 0:2], True, True,
                   [t_mrow[i_], t_const], [tPS[aux_bank]])
            cp("dve", modT[:, pc * 4:(pc + 1) * 4, :], pv, [tPS[aux_bank]], [t_modT])
            if pc in (4, 5, 10, 11):
                dst, tdst = (g1b, t_g1b) if pc < 6 else (g2b, t_g2b)
                mm(psf(aux_bank), ones_f[0:1, 0:128], mrow[i_][0:1, :], True, True, [t_mrow[i_], t_const], [tPS[aux_bank]])
                cp("act", dst[:, (pc % 2) * 512:(pc % 2 + 1) * 512], psf(aux_bank), [tPS[aux_bank]], [tdst])

    adab0 = [M.view(PH + 6 * KB + i * 8 * KB, [128, 8, 512], BF16) for i in range(3)]
    t_adab0 = [T(f"adab{i}") for i in range(3)]
    mod_pieces([0, 1, 2, 3, 4, 5], adab0, t_adab0, PH + 30 * KB, 0, 2)
    stt(a1, modT[:, 8:16, 0], 1.0, pvec[:, PV_NW1:PV_NW1 + 8], ALU.add, ALU.mult, [t_modT, t_pvec], [t_a])
    stt(a1c, modT[:, 8:16, 1], 1.0, pvec[:, PV_NW1:PV_NW1 + 8], ALU.add, ALU.mult, [t_modT, t_pvec], [t_a])
    act(a_b, brow[:, BR_ALOG:BR_ALOG + 16], AF.Exp, [t_brow], [t_ab])
    ts(a_b, a_b, -1.0, ALU.mult, [t_ab], [t_ab])
    P.barrier()
    if stage == 0:
        return finish_dbg(g1b, 1024, [t_g1b], direct=True)

    stat_ctr = [0]

    def norm_group(src_tiles, src_trk, n, xn_bufs, t_xn, a_ap, sh_ap, dst_fn, dst_trk, t_par, bank0, only_a=False):
        for i in range(n):
            c = stat_ctr[0] % 32
            stat_ctr[0] += 1
            ss = stat[:, 2 * c:2 * c + 1]
            rs = stat[:, 2 * c + 1:2 * c + 2]
            tst = t_stat[c]
            act(junk, src_tiles[i], AF.Square, [src_trk[i]], [tst], accum_out=ss)
            act(rs, ss, AF.Ln, [tst], [tst], scale=1.0 / D, bias=EPS)
            act(rs, rs, AF.Exp, [tst], [tst], scale=-0.5)
            ts(xn_bufs[i], src_tiles[i], rs, ALU.mult, [src_trk[i], tst], [t_xn[i]])
        if only_a:
            return
        norm_stage_b(n, xn_bufs, t_xn, a_ap, sh_ap, dst_fn, dst_trk, t_par, bank0)

    def norm_stage_b(n, xn_bufs, t_xn, a_ap, sh_ap, dst_fn, dst_trk, t_par, bank0):
        for k in range(8):
            bank = bank0 + (k % 2)
            pb = psb(bank)
            for i in range(n):
                tr(pb[:, i * 128:(i + 1) * 128], xn_bufs[i][:, k * 128:(k + 1) * 128], ident_bf,
                   [t_xn[i], t_const], [tPS[bank]])
            if k % 2 == 0:
                act(dst_fn(k), pb[:, 0:n * 128], AF.Identity, [tPS[bank], t_par, t_modT], [dst_trk[k]],
                    scale=a_ap[:, k:k + 1], bias=sh_ap[:, k:k + 1])
            else:
                ts(dst_fn(k), pb[:, 0:n * 128], a_ap[:, k:k + 1], ALU.mult, [tPS[bank], t_par, t_modT], [dst_trk[k]],
                   s2=sh_ap[:, k:k + 1], op1=ALU.add)

    hT = M.view(PH, [128, 8, S_LEN], BF16)
    t_hT = [[T(f"hT{k}_{tb}") for tb in range(4)] for k in range(8)]
    mixT = M.view(174 * KB, [128, 8, S_LEN], BF16)
    t_mix = [[T(f"mix{k}_{c}") for c in range(16)] for k in range(8)]
    hcT = M.view(112 * KB, [128, 8, LC], BF16)
    t_hcT = [T(f"hcT{k}") for k in range(8)]
    expB = [M.view(146 * KB + i * 14336, [128, 8, 14, 64], BF16) for i in range(2)]
    t_expB = T("expB")

    xn_b = [M.view(56 * KB + i * 2 * KB, [128, 1024], BF16) for i in range(8)]
    t_xn = [T(f"xn{i}") for i in range(8)]
    gst = M.view(72 * KB, [128, 8, 14, 64], F32)
    mst = M.view(100 * KB, [128, 2, 14 * 64], F32)
    t_gst, t_mst = T("gst"), T("mst")
    wb = [M.view(121 * KB + i * 8 * KB, [128, 8, 512], BF16) for i in range(3)]
    t_wb = [T(f"wb{i}") for i in range(3)]
    w_in_v = w_in_d.rearrange("(kc p) n -> p kc n", p=128)
    for g in range(3):
        dma("pool", wb[g], w_in_v[:, :, g * 512:(g + 1) * 512], (), [t_wb[g]])

    cosT = M.view(174 * KB, [128, S_LEN], F32)
    sinT = M.view(182 * KB, [128, S_LEN], F32)
    t_rope = T("rope")
    dma("sp", gst.rearrange("p a b c -> p (a b c)"), gtab_d, (), [t_gst])
    dma("sp", mst.rearrange("p a b -> p (a b)"), mtab_d, (), [t_mst])

    def build_expB():
        act(gst.rearrange("p a b c -> p (a b c)"), gst.rearrange("p a b c -> p (a b c)"), AF.Exp, [t_gst], [t_gst])
        for i in range(2):
            tt(expB[i].rearrange("p a b c -> p a (b c)"), gst.rearrange("p a b c -> p a (b c)"),
               mst[:, i:i + 1, :].broadcast_to([128, 8, 14 * 64]), ALU.mult, [t_gst, t_mst], [t_expB])

    sh1 = modT[:, 0:8, 0]
    sh1c = modT[:, 0:8, 1]
    sh2 = modT[:, 24:32, 0]
    t_par = T("par")
    def t1_args(g):
        if g < 0:
            return (xs_all[0:2], t_xs_all[0:2], 2, xn_b[4:8], t_xn[4:8], a1c, sh1c, (lambda k: hcT[:, k, :]), t_hcT, t_a, 2)
        bufs = [(2 + 4 * g + i) % 8 for i in range(4)]
        xo = 4 * (g % 2)
        return ([xs_all[j_] for j_ in bufs], [t_xs_all[j_] for j_ in bufs], 4, xn_b[xo:xo + 4], t_xn[xo:xo + 4], a1, sh1,
                (lambda tb: (lambda k: hT[:, k, tb * 512:(tb + 1) * 512]))(g), [t_hT[k][g] for k in range(8)], t_a,
                2 * (g % 2))

    def t1_a(g):
        norm_group(*t1_args(g), only_a=True)

    def t1_b(g):
        ar = t1_args(g)
        norm_stage_b(*ar[2:])

    def t1_dma(g):
        for t_ in range(max(6, 4 * g), 4 * g + 4):
            dma("sp", xs_all[(2 + t_) % 8], x_v[t_], (), [t_xs_all[(2 + t_) % 8]])

    t1_a(-1)
    t1_dma(1)
    t1_a(0)
    t1_b(-1)
    t1_dma(2)
    t1_a(1)
    t1_b(0)
    build_expB()
    t1_dma(3)
    t1_a(2)
    dma("sp", sinT, rope_d[:, S_LEN:2 * S_LEN], (), [t_xs_all[2], t_xs_all[3], t_rope])
    t1_b(1)
    t1_a(3)
    dma("sp", cosT, rope_d[:, 0:S_LEN], (), [t_xs_all[0], t_xs_all[1], t_rope])
    t1_b(2)
    t1_b(3)
    P.barrier()
    if stage == 0.5:
        return finish_dbg(hT.rearrange("p a b -> p (a b)"), 16384, [t_hT[k][tb] for k in range(8) for tb in range(4)], stage_off=130 * KB)
    if stage == 0.6:
        return finish_dbg(expB[0].rearrange("p a b c -> p (a b c)"), 8 * 14 * 64, [t_expB], stage_off=56 * KB)

    qT = M.view(56 * KB, [128, 4, S_LEN], BF16)
    kT = M.view(72 * KB, [128, 4, S_LEN], BF16)
    t_qT = [[T(f"qT{hp}_{tb}") for tb in range(4)] for hp in range(4)]
    t_kT = [[T(f"kT{hp}_{tb}") for tb in range(4)] for hp in range(4)]
    v_aug = M.view(88 * KB, [128, NTT, 4, 192], BF16)
    t_v = [T(f"v{tt_}") for tt_ in range(NTT)]
    kcT = M.view(116 * KB, [128, 4, LC], BF16)
    t_kcT = T("kcT")
    vc_aug = M.view(118 * KB, [128, 2, 4, 192], BF16)
    t_vc = T("vc")
    qb = [M.view(190 * KB + i * KB, [128, 512], BF16) for i in range(2)]
    t_qb = [T(f"qb{i}") for i in range(2)]
    rt1 = [M.view(192 * KB + i * 2 * KB, [128, 512], F32) for i in range(2)]
    rt2 = [M.view(196 * KB + i * 2 * KB, [128, 512], F32) for i in range(2)]
    t_rt1 = [T(f"rt1{i}") for i in range(2)]
    t_rt2 = [T(f"rt2{i}") for i in range(2)]

    memset(v_aug[:, :, :, 64:128], 1.0, t_v)
    memset(vc_aug[:, :, :, 64:128], 1.0, [t_vc])

    if stage == 0.65:
        return finish_dbg(cosT, 2048, [t_rope] + t_wb + t_v + [t_vc], direct=True)
    rope_ctr = [0]

    def proj_fm(wbuf, t_w, cb, src, t_src_k, ntok, tok0, bank):
        for kc in range(8):
            mm(PS[bank][:, 0:ntok], wbuf[:, kc, cb * 128:(cb + 1) * 128], src[:, kc, tok0:tok0 + ntok],
               kc == 0, kc == 7, [t_w, t_src_k[kc]], [tPS[bank]])

    import os as _os
    _NOROPE = _os.environ.get("NOROPE", "0")

    def rope_evac(bank, dst, t_dst, tb):
        if _NOROPE == "1":
            cp("act", dst, psf(bank), [tPS[bank]], [t_dst])
            return
        i = rope_ctr[0] % 2
        rope_ctr[0] += 1
        b2 = 4 + i
        act(qb[i], psf(bank), AF.Copy, [tPS[bank]], [t_qb[i]])
        if _NOROPE == "4":
            tt(rt1[i], psf(bank), cosT[:, tb * 512:(tb + 1) * 512], ALU.mult, [tPS[bank], t_rope], [t_rt1[i]])
            cp("dve", dst, rt1[i], [t_rt1[i]], [t_dst])
            return
        if _NOROPE == "5":
            cp("dve", rt1[i], psf(bank), [tPS[bank]], [t_rt1[i]])
            cp("dve", rt2[i], psf(bank), [tPS[bank]], [t_rt2[i]])
            tt(dst, rt1[i], rt2[i], ALU.add, [t_rt1[i], t_rt2[i]], [t_dst])
            return
        if _NOROPE == "3":
            tt(rt1[i], psf(bank), cosT[:, tb * 512:(tb + 1) * 512], ALU.mult, [tPS[bank], t_rope], [t_rt1[i]])
            tt(rt2[i], psf(bank), sinT[:, tb * 512:(tb + 1) * 512], ALU.mult, [tPS[bank], t_rope], [t_rt2[i]])
            tt(dst, rt1[i], rt2[i], ALU.add, [t_rt1[i], t_rt2[i]], [t_dst])
            return
        mm(psf(b2), rperm_bf, qb[i], True, True, [t_qb[i], t_const], [tPS[b2]])
        if _NOROPE == "2":
            cp("dve", dst, psf(b2), [tPS[b2]], [t_dst])
            return
        tt(rt1[i], psf(bank), cosT[:, tb * 512:(tb + 1) * 512], ALU.mult, [tPS[bank], t_rope], [t_rt1[i]])
        tt(rt2[i], psf(b2), sinT[:, tb * 512:(tb + 1) * 512], ALU.mult, [tPS[b2], t_rope], [t_rt2[i]])
        tt(dst, rt1[i], rt2[i], ALU.add, [t_rt1[i], t_rt2[i]], [t_dst])

    blk = 0
    for (g, dstT, t_dst) in ((0, qT, t_qT), (1, kT, t_kT)):
        for hp in range(4):
            for tb in range(4):
                bank = blk % 4
                blk += 1
                proj_fm(wb[g], t_wb[g], hp, hT, [t_hT[k][tb] for k in range(8)], 512, tb * 512, bank)
                rope_evac(bank, dstT[:, hp, tb * 512:(tb + 1) * 512], t_dst[hp][tb], tb)
    if stage == 0.66:
        return finish_dbg(qT.rearrange("p a b -> p (a b)"), 8192, [t_qT[a][b] for a in range(4) for b in range(4)], stage_off=130 * KB)
    for hp in range(4):
        bank = blk % 4
        blk += 1
        proj_fm(wb[1], t_wb[1], hp, hcT, t_hcT, LC, 0, bank)
        cp("act", kcT[:, hp, :], PS[bank][:, 0:LC], [tPS[bank]], [t_kcT])

    def v_evac(bank, dst4, t_dst):
        src = psf(bank).rearrange("p (a b c) -> p a b c", a=4, b=2)
        cp("act", dst4[:, :, 0:64], src[:, :, 0, :], [tPS[bank]], [t_dst])
        cp("dve", dst4[:, :, 128:192], src[:, :, 1, :], [tPS[bank]], [t_dst])

    for tt_ in range(NTT):
        bank = blk % 4
        blk += 1
        for kc in range(8):
            mm(psf(bank), hT[:, kc, tt_ * 128:(tt_ + 1) * 128], wb[2][:, kc, :], kc == 0, kc == 7,
               [t_hT[kc][tt_ // 4], t_wb[2]], [tPS[bank]])
        v_evac(bank, v_aug[:, tt_], t_v[tt_])
    for ct in range(2):
        bank = blk % 4
        blk += 1
        for kc in range(8):
            mm(psf(bank), hcT[:, kc, ct * 128:(ct + 1) * 128], wb[2][:, kc, :], kc == 0, kc == 7,
               [t_hcT[kc], t_wb[2]], [tPS[bank]])
        v_evac(bank, vc_aug[:, ct], t_vc)
    P.barrier()
    if stage == 0.7:
        return finish_dbg(qT.rearrange("p a b -> p (a b)"), 8192, [t_qT[a][b] for a in range(4) for b in range(4)], stage_off=130 * KB)
    if stage == 0.8:
        return finish_dbg(v_aug[:, 0:8].rearrange("p a b c -> p (a b c)"), 8 * 768, [t_v[a] for a in range(8)], stage_off=130 * KB)

    Etb = [[M.view(121 * KB + (par * 3 + i) * KB, [128, 512], BF16) for i in range(3)] for par in range(2)]
    Ptb = [[M.view(127 * KB + (par * 4 + i) * KB, [128, 512], BF16) for i in range(4)] for par in range(2)]
    t_Etb = [[T(f"Et{par}{i}") for i in range(3)] for par in range(2)]
    t_Ptb = [[T(f"Pt{par}{i}") for i in range(4)] for par in range(2)]
    rec = [M.view(135 * KB + i * KB, [128, 256], F32) for i in range(4)]
    tmpO = [M.view(139 * KB + i * KB, [128, 256], F32) for i in range(4)]
    lnS = [M.view(143 * KB + i * KB, [128, 256], F32) for i in range(2)]
    t_rec = [T(f"rec{i}") for i in range(4)]
    t_tmpO = [T(f"tmpO{i}") for i in range(4)]
    t_lnS = [T(f"lnS{i}") for i in range(2)]
    for i in range(4):
        memset(rec[i], 0.0, [t_rec[i]])

    groups = []
    for j in range(8):
        if j == 0:
            kts, tab = [0, 1, 2, 3], 1
        elif j == 7:
            kts, tab = [12, 13, 14, 15], 1
        else:
            kts, tab = list(range(2 * j - 2, 2 * j + 4)), 0
        for h in range(8):
            groups.append((j, h, tab, [("loc", kt) for kt in kts] + [("ctx", 0), ("ctx", 1)]))

    def emit_S(gi):
        j, h, tab, lst = groups[gi]
        par = gi % 2
        hp, hb = h // 2, (h % 2) * 64
        qs = qT[hb:hb + 64, hp, j * 256:(j + 1) * 256]
        nb = len(lst) // 2
        for b_ in range(nb):
            for half in range(2):
                kind, kt = lst[2 * b_ + half]
                if kind == "loc":
                    ks, rk = kT[hb:hb + 64, hp, kt * 128:(kt + 1) * 128], t_kT[hp][kt // 4]
                else:
                    ks, rk = kcT[hb:hb + 64, hp, kt * 128:(kt + 1) * 128], t_kcT
                mm(PS[b_][:, half * 256:(half + 1) * 256], ks, qs, True, True, [rk, t_qT[hp][j // 2]], [tPS[b_]])
            if lst[2 * b_][0] == "loc":
                act(Etb[par][b_], psf(b_), AF.Exp, [tPS[b_]], [t_Etb[par][b_]], scale=0.125)
                for half in range(2):
                    kt = lst[2 * b_ + half][1]
                    e0 = 6 - (2 * kt - 4 * j)
                    bview = expB[tab][:, h, e0:e0 + 4, :].rearrange("p a b -> p (a b)")
                    tt(Ptb[par][b_][:, half * 256:(half + 1) * 256], Etb[par][b_][:, half * 256:(half + 1) * 256], bview,
                       ALU.mult, [t_Etb[par][b_], t_expB], [t_Ptb[par][b_]])
            else:
                act(Ptb[par][b_], psf(b_), AF.Exp, [tPS[b_]], [t_Ptb[par][b_]], scale=0.125)

    def emit_PV(gi):
        j, h, tab, lst = groups[gi]
        par = gi % 2
        hp, odd = h // 2, h % 2
        ob = 4 + par
        c0 = 64 if odd else 0
        n = len(lst)
        for i, (kind, kt) in enumerate(lst):
            if kind == "loc":
                lhs, rv = v_aug[:, kt, hp, c0:c0 + 128], t_v[kt]
            else:
                lhs, rv = vc_aug[:, kt, hp, c0:c0 + 128], t_vc
            mm(PS[ob][:, 0:256], lhs, Ptb[par][i // 2][:, (i % 2) * 256:(i % 2 + 1) * 256], i == 0, i == n - 1,
               [rv, t_Ptb[par][i // 2]], [tPS[ob]])
        r = gi % 4
        sr = 0 if odd else 64
        obp = 64 if odd else 0
        act(lnS[par][sr:sr + 1, :], PS[ob][sr:sr + 1, 0:256], AF.Ln, [tPS[ob]], [t_lnS[par]])
        act(rec[r][sr:sr + 1, :], lnS[par][sr:sr + 1, :], AF.Exp, [t_lnS[par]], [t_rec[r]], scale=-1.0)
        cp("dve", tmpO[r][obp:obp + 64, :], PS[ob][obp:obp + 64, 0:256], [tPS[ob]], [t_tmpO[r]])

    def emit_norm(gi):
        j, h, tab, lst = groups[gi]
        par = gi % 2
        hp, odd = h // 2, h % 2
        r = gi % 4
        sr = 0 if odd else 64
        obp = 64 if odd else 0
        mm(PS[6][:, par * 256:(par + 1) * 256], sel_f[1 if sr else 0], rec[r], True, True,
           [t_rec[r], t_const], [tPS[6]])
        tt(mixT[obp:obp + 64, hp, j * 256:(j + 1) * 256], tmpO[r][obp:obp + 64, :],
           PS[6][obp:obp + 64, par * 256:(par + 1) * 256], ALU.mult, [t_tmpO[r], tPS[6]],
           [t_mix[hp][2 * j], t_mix[hp][2 * j + 1]])

    ng_ = len(groups)
    for step in range(ng_ + 2):
        if step < ng_:
            emit_S(step)
        if 0 <= step - 1 < ng_:
            emit_PV(step - 1)
        if 0 <= step - 2 < ng_:
            emit_norm(step - 2)
    P.barrier()


    if stage == 1:
        return finish_dbg(mixT[:, 0:4, :].rearrange("p a b -> p (a b)"), 4 * S_LEN,
                          [t_mix[k][c] for k in range(4) for c in range(16)])

    zs = M.view(56 * KB, [128, NTT, 512], BF16)
    t_zs = [T(f"zs{i}") for i in range(NTT)]
    xbc = M.view(72 * KB, [128, 8, S_LEN], BF16)
    t_xbc = [T(f"xbc{i}") for i in range(8)]
    ctxx = M.view(104 * KB, [128, 6, LC], BF16)
    t_ctxx = [T(f"ctxx{i}") for i in range(6)]
    TB0 = 107 * KB
    dt_all, dta, CFs, TOTs, D1, ecum, wend, cdt, tmpA = [M.view(TB0 + i * 1152, [128, 18, 16], F32) for i in range(9)]
    t_dt, t_dta, t_cf, t_tot, t_d1, t_ecum, t_wend, t_cd, t_tmpA = [T(n) for n in
        "dt dta cf tot d1 ecum wend cd tmpA".split()]
    wdt = M.view(145 * KB, [128, 8, 16], BF16)
    t_wdt = T("wdt")
    pad = [M.view(146 * KB + i * 4104, [128, 2052], BF16) for i in range(2)]
    t_pad = [T(f"pad{i}") for i in range(2)]
    cpad = M.view(155 * KB, [128, 260], BF16)
    t_cpad = T("cpad")
    diag = M.view(156 * KB, [128, 24, 128], BF16)
    t_diag = T("diag")
    for i in range(24):
        ts(diag[:, i, :], ident_bf, pvec[:, PV_SCW + i:PV_SCW + i + 1], ALU.mult, [t_const, t_pvec], [t_diag])

    for g in range(3):
        dma("pool", wb[g], w_in_v[:, :, 1536 + g * 512:1536 + (g + 1) * 512], (), [t_wb[g]])
    dma("pool", wdt, w_in_v[:, :, 3072:3088], (), [t_wdt])
    for i in range(2):
        memset(pad[i][:, 0:1], 0.0, [t_pad[i]])
        memset(pad[i][:, 2049:2050], 0.0, [t_pad[i]])
    memset(cpad[:, 0:1], 0.0, [t_cpad])
    memset(cpad[:, 257:258], 0.0, [t_cpad])

    blk = 0
    for tt_ in range(NTT):
        bank = blk % 4
        blk += 1
        for kc in range(8):
            mm(psf(bank), hT[:, kc, tt_ * 128:(tt_ + 1) * 128], wb[0][:, kc, :], kc == 0, kc == 7,
               [t_hT[kc][tt_ // 4], t_wb[0]], [tPS[bank]])
        act(zs[:, tt_, :], psf(bank), AF.Silu, [tPS[bank]], [t_zs[tt_]])
    for c in range(18):
        for kc in range(8):
            if c < 16:
                lhs, rd = hT[:, kc, c * 128:(c + 1) * 128], t_hT[kc][c // 4]
            else:
                lhs, rd = hcT[:, kc, (c - 16) * 128:(c - 15) * 128], t_hcT[kc]
            mm(PS[7][:, c * 16:(c + 1) * 16], lhs, wdt[:, kc, :], kc == 0, kc == 7, [rd, t_wdt], [tPS[7]])
    ps7v = PS[7][:, 0:288].rearrange("p (a b) -> p a b", b=16)
    tt(dt_all, ps7v, brow[:, BR_DTB:BR_DTB + 16].unsqueeze(1).broadcast_to([128, 18, 16]), ALU.add,
       [tPS[7], t_brow], [t_dt])
    dt_flat = dt_all.rearrange("p a b -> p (a b)")
    act(dt_flat, dt_flat, AF.Exp, [t_dt], [t_dt])
    act(dt_flat, dt_flat, AF.Ln, [t_dt], [t_dt], bias=1.0)

    conv_ctr = [0]

    def conv_silu(padb, t_padb, n, cc, dst, t_dst):
        bb = pvec[:, PV_SCB + cc:PV_SCB + cc + 1]
        nblk = max(1, n // 512)
        w_ = n // nblk
        for tb in range(nblk):
            bank = 4 + conv_ctr[0] % 2
            conv_ctr[0] += 1
            for k in range(3):
                mm(PS[bank][:, 0:w_], diag[:, cc * 3 + k, :], padb[:, tb * w_ + k:tb * w_ + k + w_], k == 0, k == 2,
                   [t_diag, t_padb], [tPS[bank]])
            act(dst[:, tb * w_:(tb + 1) * w_], PS[bank][:, 0:w_], AF.Silu, [tPS[bank], t_pvec], [t_dst], bias=bb)

    for cc in range(8):
        wsel, cb = (1, cc) if cc < 4 else (2, cc - 4)
        pb_ = cc % 2
        for tb in range(4):
            bank = blk % 4
            blk += 1
            proj_fm(wb[wsel], t_wb[wsel], cb, hT, [t_hT[k][tb] for k in range(8)], 512, tb * 512, bank)
            cp("act" if tb % 2 == 0 else "dve", pad[pb_][:, 1 + tb * 512:1 + (tb + 1) * 512], psf(bank),
               [tPS[bank]], [t_pad[pb_]])
        conv_silu(pad[pb_], t_pad[pb_], S_LEN, cc, xbc[:, cc, :], t_xbc[cc])
    for cc in range(6):
        wsel, cb = (1, cc) if cc < 4 else (2, cc - 4)
        bank = blk % 4
        blk += 1
        proj_fm(wb[wsel], t_wb[wsel], cb, hcT, t_hcT, LC, 0, bank)
        cp("act", cpad[:, 1:1 + LC], PS[bank][:, 0:LC], [tPS[bank]], [t_cpad])
        conv_silu(cpad, t_cpad, LC, cc, ctxx[:, cc, :], t_ctxx[cc])
    P.barrier()

    xsB = M.view(121 * KB, [128, 18, 768], BF16)
    t_xsB = [T(f"xsB{i}") for i in range(18)]
    for c in range(18):
        bank = c % 2
        pb = psb(bank)
        for cc in range(6):
            if c < 16:
                src, rd = xbc[:, cc, c * 128:(c + 1) * 128], t_xbc[cc]
            else:
                src, rd = ctxx[:, cc, (c - 16) * 128:(c - 15) * 128], t_ctxx[cc]
            tr(pb[:, cc * 128:(cc + 1) * 128], src, ident_bf, [rd, t_const], [tPS[bank]])
        cp("act" if c % 2 == 0 else "dve", xsB[:, c, :], pb[:, 0:768], [tPS[bank]], [t_xsB[c]])
    P.barrier()

    tt(dta, dt_all, a_b.unsqueeze(1).broadcast_to([128, 18, 16]), ALU.mult, [t_dt, t_ab], [t_dta])
    for c in range(18):
        mm(PS[0][:, c * 16:(c + 1) * 16], LE_f, dta[:, c, :], True, True, [t_dta, t_const], [tPS[0]])
    for c in range(18):
        mm(PS[1][:, c * 16:(c + 1) * 16], ones_f, dta[:, c, :], True, True, [t_dta, t_const], [tPS[1]])
    fl = lambda a: a.rearrange("p a b -> p (a b)")
    cp("dve", fl(CFs), PS[0][:, 0:288], [tPS[0]], [t_cf])
    cp("dve", fl(TOTs), PS[1][:, 0:288], [tPS[1]], [t_tot])
    tt(fl(D1), fl(TOTs), fl(CFs), ALU.subtract, [t_tot, t_cf], [t_d1])
    act(fl(cdt), fl(TOTs), AF.Exp, [t_tot], [t_cd])
    F_, B_ = slice(0, 8), slice(8, 16)
    act(ecum[:, :, F_], CFs[:, :, F_], AF.Exp, [t_cf], [t_ecum])
    act(wend[:, :, F_], D1[:, :, F_], AF.Exp, [t_d1], [t_wend])
    tt(tmpA[:, :, B_], D1[:, :, B_], dta[:, :, B_], ALU.add, [t_d1, t_dta], [t_tmpA])
    act(ecum[:, :, B_], tmpA[:, :, B_], AF.Exp, [t_tmpA], [t_ecum])
    tt(tmpA[:, :, F_], CFs[:, :, B_], dta[:, :, B_], ALU.subtract, [t_cf, t_dta, t_tmpA], [t_tmpA])
    act(wend[:, :, B_], tmpA[:, :, F_], AF.Exp, [t_tmpA], [t_wend])
    tt(fl(wend), fl(wend), fl(dt_all), ALU.mult, [t_wend, t_dt], [t_wend])

    Hst = [M.view(148 * KB + i * 2 * KB, [128, 512], F32) for i in range(2)]
    t_H = [T(f"H{i}") for i in range(2)]
    Hin = M.view(24 * KB, [128, 16, 2, 512], BF16)
    t_Hin = [[T(f"Hin{c}_{d}") for d in range(2)] for c in range(16)]
    xwb = [M.view(152 * KB + i * KB, [128, 512], BF16) for i in range(4)]
    t_xwb = [T(f"xw{i}") for i in range(4)]
    v8 = lambda a: a.rearrange("p (h d) -> p h d", h=8)
    bc8 = lambda a: a.unsqueeze(2).broadcast_to([128, 8, 64])
    for d_ in range(2):
        memset(Hst[d_], 0.0, [t_H[d_]])
    xw_ctr = [0]

    def state_step(d_, c):
        i = xw_ctr[0] % 4
        xw_ctr[0] += 1
        bank = 2 + d_
        hs = slice(d_ * 8, d_ * 8 + 8)
        tt(v8(xwb[i]), v8(xsB[:, c, 0:512]), bc8(wend[:, c, hs]), ALU.mult, [t_xsB[c], t_wend], [t_xwb[i]], eng="pool")
        for g in range(2):
            mm(PS[bank][:, g * 256:(g + 1) * 256], xsB[:, c, 512 + g * 128:512 + (g + 1) * 128],
               xwb[i][:, g * 256:(g + 1) * 256], True, True, [t_xsB[c], t_xwb[i]], [tPS[bank]])
        tt(v8(Hst[d_]), v8(Hst[d_]), bc8(cdt[:, c, hs]), ALU.mult, [t_H[d_], t_cd], [t_H[d_]])
        tt(Hst[d_], Hst[d_], psf(bank), ALU.add, [t_H[d_], tPS[bank]], [t_H[d_]])

    state_step(0, 16)
    state_step(0, 17)
    state_step(1, 17)
    state_step(1, 16)
    for s_ in range(16):
        for d_, c in ((0, s_), (1, 15 - s_)):
            cp("act", Hin[:, c, d_, :], Hst[d_], [t_H[d_]], [t_Hin[c][d_]])
            if s_ < 15:
                state_step(d_, c)
    if stage == 1.5:
        return finish_dbg(Hin[:, 0, :, :].rearrange("p a b -> p (a b)"), 1024, [t_Hin[0][0], t_Hin[0][1]], stage_off=56 * KB)

    xdt = [[M.view(156 * KB + (d_ * 2 + i) * KB, [128, 512], BF16) for i in range(2)] for d_ in range(2)]
    t_xdt = [[T(f"xdt{d_}{i}") for i in range(2)] for d_ in range(2)]
    Abuf = [M.view(160 * KB + d_ * 2 * KB, [128, 8, 128], BF16) for d_ in range(2)]
    t_A = [T(f"A{d_}") for d_ in range(2)]
    Eb = [M.view(164 * KB + i * KB, [128, 512], BF16) for i in range(4)]
    Mb = [M.view(168 * KB + i * KB, [128, 512], BF16) for i in range(4)]
    t_E = [T(f"E{i}") for i in range(4)]
    t_M = [T(f"M{i}") for i in range(4)]
    CBm = [M.view(172 * KB + i * 512, [128, 2, 128], BF16) for i in range(2)]
    t_CBm = [T(f"CBm{i}") for i in range(2)]
    ynb = M.view(173 * KB, [128, 512], BF16)
    t_yn = T("yn")
    ytmp = [M.view(72 * KB + i * 2 * KB, [128, 512], F32) for i in range(4)]
    t_yt = [T(f"yt{i}") for i in range(4)]
    tri = [LE_bf, GE_bf]
    stri = [GT_bf, LT_bf]
    Mb2 = [Mb, [M.view(80 * KB + i * KB, [128, 512], BF16) for i in range(4)]]
    t_M2 = [t_M, [T(f"Mx{i}") for i in range(4)]]
    ynb2 = [ynb, M.view(84 * KB, [128, 512], BF16)]
    xDb = [M.view(85 * KB + i * KB, [128, 512], BF16) for i in range(2)]
    t_xD = [T(f"xD{i}") for i in range(2)]
    t_yn2 = [t_yn, T("yn2")]

    def ssd_s1(c):
        cs = slice(c * 128, (c + 1) * 128)
        pi = c % 2
        cbk = 0 if pi == 0 else 7
        for g in range(2):
            mm(PS[cbk][:, g * 128:(g + 1) * 128], xbc[:, 4 + g, cs], xbc[:, 6 + g, cs], True, True,
               [t_xbc[4 + g], t_xbc[6 + g]], [tPS[cbk]])
        cbv = PS[cbk][:, 0:256].rearrange("p (a b) -> p a b", a=2)
        for d_ in range(2):
            hs = slice(d_ * 8, d_ * 8 + 8)
            tt(Abuf[d_], stri[d_].unsqueeze(1).broadcast_to([128, 8, 128]),
               dta[:, c, hs].unsqueeze(2).broadcast_to([128, 8, 128]), ALU.mult, [t_const, t_dta], [t_A[d_]], eng="pool")
            tt(v8(xdt[d_][pi]), v8(xsB[:, c, 0:512]), bc8(dt_all[:, c, hs]), ALU.mult, [t_xsB[c], t_dt], [t_xdt[d_][pi]])
            if d_ == 0:
                tt(v8(xDb[pi]), v8(xsB[:, c, 0:512]), bc8(brow[:, BR_D:BR_D + 8]), ALU.mult, [t_xsB[c], t_brow], [t_xD[pi]],
                   eng="pool")
            for g in range(2):
                bank = 2 + g
                e_i = d_ * 2 + g
                for hh in range(4):
                    mm(PS[bank][:, hh * 128:(hh + 1) * 128], Abuf[d_][:, g * 4 + hh, :], tri[d_], True, False,
                       [t_A[d_], t_const], [tPS[bank]])
                    mm(PS[bank][:, hh * 128:(hh + 1) * 128], ident_bf, neg_bf[d_], False, True,
                       [t_const], [tPS[bank]])
                act(Eb[e_i], psf(bank), AF.Exp, [tPS[bank]], [t_E[e_i]])
                tt(Mb2[pi][e_i].rearrange("p (a b) -> p a b", a=4), Eb[e_i].rearrange("p (a b) -> p a b", a=4),
                   cbv[:, g:g + 1, :].broadcast_to([128, 4, 128]), ALU.mult, [t_E[e_i], tPS[cbk]], [t_M2[pi][e_i]])

    yn_info = {}

    def ssd_s2(c):
        cs = slice(c * 128, (c + 1) * 128)
        pi = c % 2
        mm(psf(4), ident_bf, xDb[pi], True, False, [t_const, t_xD[pi]], [tPS[4]])
        for h in range(8):
            g, hh = h // 4, h % 4
            mm(PS[4][:, h * 64:(h + 1) * 64], Mb2[pi][g][:, hh * 128:(hh + 1) * 128], xdt[0][pi][:, h * 64:(h + 1) * 64],
               False, False, [t_M2[pi][g], t_xdt[0][pi]], [tPS[4]])
            mm(PS[4][:, h * 64:(h + 1) * 64], Mb2[pi][2 + g][:, hh * 128:(hh + 1) * 128], xdt[1][pi][:, h * 64:(h + 1) * 64],
               False, h == 7, [t_M2[pi][2 + g], t_xdt[1][pi]], [tPS[4]])
        for d_ in range(2):
            for g in range(2):
                mm(PS[5 + d_][:, g * 256:(g + 1) * 256], xbc[:, 6 + g, cs], Hin[:, c, d_, g * 256:(g + 1) * 256],
                   True, True, [t_xbc[6 + g], t_Hin[c][d_]], [tPS[5 + d_]])
        tt(v8(ytmp[0]), v8(psf(5)), bc8(ecum[:, c, 0:8]), ALU.mult, [tPS[5], t_ecum], [t_yt[0]])
        tt(v8(ytmp[1]), v8(psf(6)), bc8(ecum[:, c, 8:16]), ALU.mult, [tPS[6], t_ecum], [t_yt[1]])
        tt(ytmp[0], ytmp[0], ytmp[1], ALU.add, [t_yt[0], t_yt[1]], [t_yt[0]])
        tt(ytmp[0], ytmp[0], psf(4), ALU.add, [t_yt[0], tPS[4]], [t_yt[0]])
        tt(ytmp[2 + pi], ytmp[0], zs[:, c, :], ALU.mult, [t_yt[0], t_zs[c]], [t_yt[2 + pi]])
        k_ = stat_ctr[0] % 32
        stat_ctr[0] += 1
        ss, rs, tst = stat[:, 2 * k_:2 * k_ + 1], stat[:, 2 * k_ + 1:2 * k_ + 2], t_stat[k_]
        act(junk[:, 0:512], ytmp[2 + pi], AF.Square, [t_yt[2 + pi]], [tst], accum_out=ss)
        act(rs, ss, AF.Ln, [tst], [tst], scale=1.0 / 512, bias=EPS)
        act(rs, rs, AF.Exp, [tst], [tst], scale=-0.5)
        yn_info[c] = (rs, tst)

    def ssd_s3(c):
        cs = slice(c * 128, (c + 1) * 128)
        pi = c % 2
        rs, tst = yn_info[c]
        stt(ynb2[pi], ytmp[2 + pi], rs, brow[:, BR_SNW:BR_SNW + 512], ALU.mult, ALU.mult, [t_yt[2 + pi], tst, t_brow], [t_yn2[pi]])
        pb = psb(1)
        for q in range(4):
            tr(pb[:, q * 128:(q + 1) * 128], ynb2[pi][:, q * 128:(q + 1) * 128], ident_bf, [t_yn2[pi], t_const], [tPS[1]])
        cp("act", mixT[:, 4:8, cs], pb[:, 0:512].rearrange("p (a b) -> p a b", a=4), [tPS[1]],
           [t_mix[4 + q][c] for q in range(4)])

    for it in range(16 + 2):
        if it < 16:
            ssd_s1(it)
        if 0 <= it - 1 < 16:
            ssd_s2(it - 1)
        if 0 <= it - 2 < 16:
            ssd_s3(it - 2)
    P.barrier()
    if stage == 2:
        return finish_dbg(mixT[:, 4:8, :].rearrange("p a b -> p (a b)"), 4 * S_LEN,
                          [t_mix[k][c] for k in range(4, 8) for c in range(16)])

    w_outb = M.view(146 * KB, [128, 8, D], BF16)
    t_wout = T("wout")
    dma("pool", w_outb, w_out_d.rearrange("(kc p) n -> p kc n", p=128), (), [t_wout])
    x1 = M.view(24 * KB, [128, NTT, D], F32)
    t_x1 = [T(f"x1_{i}") for i in range(NTT)]
    for tt_ in range(NTT):
        dma("sp", x1[:, tt_, :], x_v[tt_], (), [t_x1[tt_]])
    tmpw = [M.view(162 * KB + i * 2 * KB, [128, 512], F32) for i in range(2)]
    t_tmpw = [T(f"tmpw{i}") for i in range(2)]
    blk = 0
    for tt_ in range(NTT):
        for ch in range(2):
            bank = blk % 4
            i = blk % 2
            blk += 1
            for kc in range(8):
                mm(psf(bank), mixT[:, kc, tt_ * 128:(tt_ + 1) * 128], w_outb[:, kc, ch * 512:(ch + 1) * 512],
                   kc == 0, kc == 7, [t_mix[kc][tt_], t_wout], [tPS[bank]])
            tt(tmpw[i], psf(bank), g1b[:, ch * 512:(ch + 1) * 512], ALU.mult, [tPS[bank], t_g1b], [t_tmpw[i]])
            tt(x1[:, tt_, ch * 512:(ch + 1) * 512], x1[:, tt_, ch * 512:(ch + 1) * 512], tmpw[i], ALU.add,
               [t_x1[tt_], t_tmpw[i]], [t_x1[tt_]])
    adab1 = [M.view(128 * KB + i * 8 * KB, [128, 8, 512], BF16) for i in range(2)]
    t_adab1 = [T(f"adabx{i}") for i in range(2)]
    mod_pieces([6, 7, 8, 9, 10, 11], adab1, t_adab1, 166 * KB, 6, 7)
    t_a2 = T("a2")
    stt(a2, modT[:, 32:40, 0], 1.0, pvec[:, PV_NW2:PV_NW2 + 8], ALU.add, ALU.mult, [t_modT, t_pvec], [t_a, t_a2])
    if stage == 3:
        return finish_dbg(x1[:, 0:4, :].rearrange("p a b -> p (a b)"), 4096, [t_x1[i] for i in range(4)], direct=True)

    h2T = M.view(88 * KB, [128, 8, S_LEN], BF16)
    t_h2T = [[T(f"h2T{k}_{tb}") for tb in range(4)] for k in range(8)]
    xn2 = [M.view(120 * KB + i * 2 * KB, [128, 1024], BF16) for i in range(4)]
    t_xn2 = [T(f"xn2{i}") for i in range(4)]
    for tb in range(4):
        norm_group([x1[:, tb * 4 + i, :] for i in range(4)], [t_x1[tb * 4 + i] for i in range(4)], 4, xn2, t_xn2, a2, sh2,
                   (lambda tb: (lambda k: h2T[:, k, tb * 512:(tb + 1) * 512]))(tb),
                   [t_h2T[k][tb] for k in range(8)], t_a, 4)
    P.barrier()

    uT = M.view(120 * KB, [128, 8, S_LEN], BF16)
    t_uT = [T(f"uT{i}") for i in range(8)]
    wdn = M.view(152 * KB, [128, 8, D], BF16)
    t_wdn = T("wdn")
    wup = [M.view(168 * KB + i * 4 * KB, [128, 8, 256], BF16) for i in range(2)]
    t_wup = [T(f"wup{i}") for i in range(2)]
    gpad = [M.view(176 * KB + i * 8208, [128, 2052], F32) for i in range(2)]
    t_gpad = [T(f"gpad{i}") for i in range(2)]
    valb = M.view(197120, [128, S_LEN], BF16)
    t_val = T("val")
    facc = M.view(197120 + 4096, [128, S_LEN], F32)
    t_facc = T("facc")
    tmpd = [g1b[:, 0:512], g1b[:, 512:1024]]
    t_tmpd = [T("tmpd0"), T("tmpd1")]
    for i in range(2):
        memset(gpad[i][:, 0:1], 0.0, [t_gpad[i]])
        memset(gpad[i][:, 2049:2050], 0.0, [t_gpad[i]])
    w_down_v = w_down_d.rearrange("(f p) n -> p f n", p=128)
    out_v = out_d.rearrange("(t p) d -> t p d", p=128)
    out_ops = []
    fin_pending = []
    dblk = 0
    for gi, (f0, f1) in enumerate(FF_GROUPS):
        ng = f1 - f0
        for f in range(f0, f1):
            sl = f - f0
            wi = f % 2
            dma("pool", wup[wi].rearrange("p a b -> p (a b)"), w_up_d[f], (), [t_wup[wi]])
            if sl == 1:
                dma("pool", wdn[:, 0:ng, :], w_down_v[:, f0:f1, :], (), [t_wdn])
            if sl >= 2:
                for s2_ in ([0, 1, 2] if sl == 2 else [sl]):
                    tt(wdn[:, s2_, :], wdn[:, s2_, :], g2b, ALU.mult, [t_wdn, t_g2b], [t_wdn])
            for tb in range(4):
                bg, bv = tb % 2, 2 + tb % 2
                for kc in range(8):
                    mm(psf(bg), wup[wi][:, kc, 0:128], h2T[:, kc, tb * 512:(tb + 1) * 512], kc == 0, kc == 7,
                       [t_wup[wi], t_h2T[kc][tb]], [tPS[bg]])
                cp("act", gpad[wi][:, 1 + tb * 512:1 + (tb + 1) * 512], psf(bg), [tPS[bg]], [t_gpad[wi]])
                for kc in range(8):
                    mm(psf(bv), wup[wi][:, kc, 128:256], h2T[:, kc, tb * 512:(tb + 1) * 512], kc == 0, kc == 7,
                       [t_wup[wi], t_h2T[kc][tb]], [tPS[bv]])
                cp("act", valb[:, tb * 512:(tb + 1) * 512], psf(bv), [tPS[bv]], [t_val])
            w0 = pvec[:, PV_FCW + f * 3 + 0:PV_FCW + f * 3 + 1]
            w1 = pvec[:, PV_FCW + f * 3 + 1:PV_FCW + f * 3 + 2]
            w2 = pvec[:, PV_FCW + f * 3 + 2:PV_FCW + f * 3 + 3]
            bb = pvec[:, PV_FCB + f:PV_FCB + f + 1]
            ts(facc, gpad[wi][:, 1:1 + S_LEN], w1, ALU.mult, [t_gpad[wi], t_pvec], [t_facc], s2=bb, op1=ALU.add)
            stt(facc, gpad[wi][:, 0:S_LEN], w0, facc, ALU.mult, ALU.add, [t_gpad[wi], t_pvec, t_facc], [t_facc])
            stt(facc, gpad[wi][:, 2:2 + S_LEN], w2, facc, ALU.mult, ALU.add, [t_gpad[wi], t_pvec, t_facc], [t_facc])
            act(uT[:, sl, :], facc, AF.Silu, [t_facc], [t_uT[sl]])
            tt(uT[:, sl, :], uT[:, sl, :], valb, ALU.mult, [t_uT[sl], t_val], [t_uT[sl]])
        last = gi == len(FF_GROUPS) - 1
        for tt_ in range(NTT):
            for ch in range(2):
                bank = 4 + dblk % 4
                i = dblk % 2
                dblk += 1
                for sl in range(ng):
                    mm(psf(bank), uT[:, sl, tt_ * 128:(tt_ + 1) * 128], wdn[:, sl, ch * 512:(ch + 1) * 512],
                       sl == 0, sl == ng - 1, [t_uT[sl], t_wdn], [tPS[bank]])
                tt(x1[:, tt_, ch * 512:(ch + 1) * 512], x1[:, tt_, ch * 512:(ch + 1) * 512], psf(bank), ALU.add,
                   [t_x1[tt_], tPS[bank]], [t_x1[tt_]])
            if last:
                k_ = stat_ctr[0] % 32
                stat_ctr[0] += 1
                ss, rs, tst = stat[:, 2 * k_:2 * k_ + 1], stat[:, 2 * k_ + 1:2 * k_ + 2], t_stat[k_]
                act(junk, x1[:, tt_, :], AF.Square, [t_x1[tt_]], [tst], accum_out=ss)
                act(rs, ss, AF.Ln, [tst], [tst], scale=1.0 / D, bias=EPS)
                act(rs, rs, AF.Exp, [tst], [tst], scale=-0.5)
                fin_pending.append((tt_, rs, tst))
                while len(fin_pending) > (1 if tt_ < NTT - 1 else 0):
                    t2_, rs2, tst2 = fin_pending.pop(0)
                    stt(x1[:, t2_, :], x1[:, t2_, :], rs2, brow[:, BR_FNW:BR_FNW + D], ALU.mult, ALU.mult,
                        [t_x1[t2_], tst2, t_brow], [t_x1[t2_]])
                    out_ops.append(dma("sp", out_v[t2_], x1[:, t2_, :], [t_x1[t2_]], []))
    P.emit(final_wait_ops=out_ops)
    return nc


def _const_tables():
    a = np.arange(128)
    ident = (a[:, None] == a[None, :])
    swap = np.where((a % 64) < 32, a + 32, a - 32)
    rperm = (a[:, None] == swap[None, :])
    LE = a[:, None] <= a[None, :]
    GE = a[:, None] >= a[None, :]
    GT = a[:, None] > a[None, :]
    LT = a[:, None] < a[None, :]
    ones = np.ones((128, 128), bool)
    sel0 = np.broadcast_to((a == 0)[:, None], (128, 128))
    sel64 = np.broadcast_to((a == 64)[:, None], (128, 128))
    negf = np.where(a[None, :] < a[:, None], -30000.0, 0.0)
    negb = np.where(a[None, :] > a[:, None], -30000.0, 0.0)
    cmat = np.concatenate([m.astype(np.float32) for m in (ident, rperm, LE, GE, GT, LT, ones, sel0, sel64, negf, negb)], axis=1)
    t = np.arange(S_LEN)
    row_pos = (t // 64).astype(np.float32)
    col_pos = (t % 64).astype(np.float32)
    freqs = (np.float32(10000.0) ** (-np.arange(16, dtype=np.float32) / np.float32(16))).astype(np.float32)
    ang = np.concatenate([row_pos[:, None] * freqs, col_pos[:, None] * freqs], axis=-1).astype(np.float32)
    cos = np.cos(ang).astype(np.float32).T
    sin = np.sin(ang).astype(np.float32).T
    idx = a % 32
    sign = np.where((a % 64) < 32, -1.0, 1.0).astype(np.float32)
    rope = np.concatenate([cos[idx], sin[idx] * sign[:, None]], axis=1).astype(np.float32)
    p = np.arange(128)
    ip, cp_ = p // 64, p % 64
    e = np.arange(14)
    w = np.arange(64)
    dr = 13 - e[None, :, None] + ip[:, None, None] + 0 * w[None, None, :]
    dc = cp_[:, None, None] - w[None, None, :] + 15 + 0 * e[None, :, None]
    cs = np.clip(w - 8, 0, 48)
    colv = (cp_[:, None, None] >= cs[None, None, :]) & (cp_[:, None, None] < cs[None, None, :] + 16)
    colv = colv & (dr >= -99)
    drv = (dr >= 0) & (dr <= 14)
    m_int = colv & (dr >= 3) & (dr <= 10)
    m_edge = colv & drv
    mtab = np.stack([m_int, m_edge], axis=1).astype(np.float32).reshape(128, 2 * 14 * 64)
    dr_c = np.clip(dr, 0, 14)
    dc_c = np.clip(dc, 0, 30)
    return cmat, rope, mtab, dr_c, dc_c


def _prep_shared(inp):
    cmat, rope, mtab, dr_c, dc_c = _const_tables()
    rpb = inp["rpb"][0]
    gtab = rpb[:, dr_c, dc_c]
    gtab = np.ascontiguousarray(gtab.transpose(1, 0, 2, 3)).reshape(128, 8 * 14 * 64).astype(np.float32)

    def pl(v, n):
        return np.ascontiguousarray(v.reshape(n, 128).T)

    pvec = np.zeros((128, PV_N), np.float32)
    pvec[:, PV_NW1:PV_NW1 + 8] = pl(inp["norm1_w"][0], 8)
    pvec[:, PV_NW2:PV_NW2 + 8] = pl(inp["norm2_w"][0], 8)
    scw = inp["ssd_conv_w"][0]
    pvec[:, PV_SCW:PV_SCW + 24] = np.stack([pl(scw[k], 8) for k in range(3)], axis=-1).reshape(128, 24)
    pvec[:, PV_SCB:PV_SCB + 8] = pl(inp["ssd_conv_b"][0], 8)
    fcw = inp["ffn_conv_w"][0]
    pvec[:, PV_FCW:PV_FCW + 66] = np.stack([pl(fcw[k], 22) for k in range(3)], axis=-1).reshape(128, 66)
    pvec[:, PV_FCB:PV_FCB + 22] = pl(inp["ffn_conv_b"][0], 22)
    brow = np.zeros((BR_N,), np.float32)
    brow[BR_FNW:BR_FNW + 1024] = inp["final_norm_w"]
    brow[BR_SNW:BR_SNW + 512] = inp["ssd_norm_w"][0]
    brow[BR_DTB:BR_DTB + 16] = inp["dt_bias"][0].reshape(16)
    brow[BR_ALOG:BR_ALOG + 16] = inp["a_log"][0].reshape(16)
    brow[BR_D:BR_D + 8] = inp["ssd_d"][0]
    brow = np.ascontiguousarray(np.broadcast_to(brow[None, :], (128, BR_N)))
    w_up = inp["ffn_w_up"][0]
    gate = w_up[:, :D_FF].reshape(8, 128, NFF, 128)
    val = w_up[:, D_FF:].reshape(8, 128, NFF, 128)
    w_up_l = np.stack([gate, val], axis=3)
    w_up_l = np.ascontiguousarray(w_up_l.transpose(2, 1, 0, 3, 4)).reshape(NFF, 128, 2048)
    shared = {
        "ada_w": np.ascontiguousarray(inp["ada_w"][0]),
        "ada_b2": np.ascontiguousarray(np.broadcast_to(inp["ada_b"][0][None, :], (2, 6 * D))),
        "w_in": np.ascontiguousarray(inp["w_in"][0]),
        "w_out": np.ascontiguousarray(inp["w_out"][0]),
        "w_up": w_up_l,
        "w_down": np.ascontiguousarray(inp["ffn_w_down"][0]),
        "pvec": pvec, "brow": brow, "cmat": cmat, "rope": rope, "gtab": gtab, "mtab": mtab,
    }
    return shared


def _in_maps(inp, cores):
    inp = {k: np.asarray(v, dtype=np.float32) for k, v in inp.items()}
    shared = _prep_shared(inp)
    maps = []
    for b in cores:
        cc = np.stack([inp["c"][b], inp["c_ctx"]], axis=-1)
        ccT = np.ascontiguousarray(cc.reshape(8, 128, 2).transpose(1, 0, 2)).reshape(128, 16)
        m = dict(shared)
        m["x"] = np.ascontiguousarray(inp["x"][b])
        m["ctx"] = np.ascontiguousarray(inp["ctx"][b])
        m["ccT"] = ccT
        maps.append(m)
    return maps


_NC_CACHE = {}


def kernel(**inputs):
    if "nc" not in _NC_CACHE:
        _NC_CACHE["nc"] = build()
    nc = _NC_CACHE["nc"]
    maps = _in_maps(inputs, list(range(8)))
    res = run_bass_kernel_spmd(nc, maps, core_ids=list(range(8)))
    return np.stack([np.asarray(r["out"], dtype=np.float32) for r in res.results], axis=0)
```
